# Optimizing a Trainium2 kernel written in Bass

```python
import math
import jax, jax.numpy as jnp
from jax import lax
import numpy as np

D_MODEL = 4096
BATCH = 2
SEQ = 4096
DEPTH = 1
DEC_BATCH = 128
DEC_SEQ = 1
PAST_LEN = 8192
PAGE_SIZE = 128

D_RNN = D_MODEL
LRU_BLOCKS = 16
LRU_BLOCK = D_RNN // LRU_BLOCKS
CONV_W = 4
LRU_C = 8.0
N_HEADS = 32
N_KV = 8
HEAD_DIM = 128
GROUP = N_HEADS // N_KV
D_ATTN = N_HEADS * HEAD_DIM
D_KV = N_KV * HEAD_DIM
WINDOW = 128
ROPE_THETA = 10000.0
LN_EPS = 1e-5
DN_ALPHA = (2.0 * DEPTH) ** 0.25
DN_BETA = (8.0 * DEPTH) ** -0.25
NEG_BIG = -1e30
SPLIT_POINTS = (D_RNN, 2 * D_RNN, 2 * D_RNN + D_ATTN, 2 * D_RNN + D_ATTN + D_KV,
                2 * D_RNN + D_ATTN + 2 * D_KV, 2 * D_RNN + 2 * D_ATTN + 2 * D_KV,
                2 * D_RNN + 2 * D_ATTN + 2 * D_KV + D_MODEL)
N_COLS = 2 * D_RNN + 2 * D_ATTN + 2 * D_KV + 2 * D_MODEL

kernel_name = 'hawk_swa_sink_parallel_deepnorm_step'


def _layernorm(x, g, b):
    xf = x.astype(jnp.float32)
    mu = jnp.mean(xf, axis=-1, keepdims=True)
    var = jnp.mean(jnp.square(xf - mu), axis=-1, keepdims=True)
    return ((xf - mu) * lax.rsqrt(var + LN_EPS) * g.astype(jnp.float32) + b.astype(jnp.float32)).astype(x.dtype)


def _rope(x, pos):
    half = HEAD_DIM // 2
    inv = ROPE_THETA ** (-jnp.arange(half, dtype=jnp.float32) / half)
    ang = pos.astype(jnp.float32)[:, None] * inv[None, :]
    cos = jnp.cos(ang)[None, :, None, :]
    sin = jnp.sin(ang)[None, :, None, :]
    xf = x.astype(jnp.float32)
    x1, x2 = xf[..., :half], xf[..., half:]
    return jnp.concatenate([x1 * cos - x2 * sin, x2 * cos + x1 * sin], axis=-1).astype(x.dtype)


def _project(x, w_in):
    z = x @ w_in
    return jnp.split(z, SPLIT_POINTS, axis=-1)


def _lin_combine(c1, c2):
    a1, b1 = c1
    a2, b2 = c2
    return a1 * a2, a2 * b1 + b2


def _rglru_branch(u, g_rnn, conv_buf, h0, conv_w, conv_b, w_gate_a, b_gate_a, w_gate_x, b_gate_x, lru_lambda):
    B, T, _ = u.shape
    upad = jnp.concatenate([conv_buf.astype(u.dtype), u], axis=1)
    xc = conv_b + conv_w[0] * upad[:, 0:T]
    for j in range(1, CONV_W):
        xc = xc + conv_w[j] * upad[:, j:j + T]
    new_buf = upad[:, T:]
    xb = xc.reshape(B, T, LRU_BLOCKS, LRU_BLOCK)
    r = jax.nn.sigmoid((jnp.einsum('btnd,nde->btne', xb, w_gate_a).reshape(B, T, D_RNN) + b_gate_a).astype(jnp.float32))
    i = jax.nn.sigmoid((jnp.einsum('btnd,nde->btne', xb, w_gate_x).reshape(B, T, D_RNN) + b_gate_x).astype(jnp.float32))
    log_a = -LRU_C * r * jax.nn.softplus(-lru_lambda.astype(jnp.float32))
    a = jnp.exp(log_a)
    b = jnp.sqrt(-jnp.expm1(2.0 * log_a)) * i * xc.astype(jnp.float32)
    b = b.at[:, 0].add(a[:, 0] * h0.astype(jnp.float32))
    _, h = lax.associative_scan(_lin_combine, (a, b), axis=1)
    y = h.astype(u.dtype) * jax.nn.silu(g_rnn)
    return y, new_buf, h[:, -1]


def _window_attention(q, k, v, q_pos, k_pos, sinks):
    s = jnp.einsum('bnqkgd,bnskd->bnkgqs', q.astype(jnp.float32), k.astype(jnp.float32)) * (HEAD_DIM ** -0.5)
    dq = q_pos[:, :, None] - k_pos[:, None, :]
    vis = (dq >= 0) & (dq < WINDOW) & (k_pos[:, None, :] >= 0)
    s = jnp.where(vis[None, :, None, None], s, NEG_BIG)
    sink = sinks.astype(jnp.float32).reshape(N_KV, GROUP)[None, None, :, :, None, None]
    m = jnp.maximum(jnp.max(s, axis=-1, keepdims=True), sink)
    p = jnp.exp(s - m)
    denom = jnp.sum(p, axis=-1, keepdims=True) + jnp.exp(sink - m)
    o = jnp.einsum('bnkgqs,bnskd->bnqkgd', p / denom, v.astype(jnp.float32))
    return o


def _merge_out(x, y_rnn, o_attn, g_attn, m_rnn, m_attn, w_out_rnn, w_out_attn, w_o, ln_g, ln_b):
    br_rnn = y_rnn @ w_out_rnn
    br_attn = (o_attn * jax.nn.silu(g_attn)) @ w_out_attn
    merged = jax.nn.sigmoid(m_rnn) * br_rnn + jax.nn.sigmoid(m_attn) * br_attn
    return _layernorm(DN_ALPHA * x + merged @ w_o, ln_g, ln_b)


def _prev_block(t):
    return jnp.concatenate([jnp.zeros_like(t[:, :1]), t[:, :-1]], axis=1)


def _prompt_layer(x, w_in, conv_w, conv_b, w_gate_a, b_gate_a, w_gate_x, b_gate_x, lru_lambda, sinks,
                  w_out_rnn, w_out_attn, w_o, ln_g, ln_b):
    B, S, _ = x.shape
    u, g_rnn, q, k, v, g_attn, m_rnn, m_attn = _project(x, w_in)
    conv0 = jnp.zeros((B, CONV_W - 1, D_RNN), x.dtype)
    h0 = jnp.zeros((B, D_RNN), jnp.float32)
    y_rnn, conv_new, h_new = _rglru_branch(u, g_rnn, conv0, h0, conv_w, conv_b, w_gate_a, b_gate_a,
                                           w_gate_x, b_gate_x, lru_lambda)
    pos = jnp.arange(S, dtype=jnp.int32)
    q = _rope(q.reshape(B, S, N_HEADS, HEAD_DIM), pos)
    k = _rope(k.reshape(B, S, N_KV, HEAD_DIM), pos)
    v = v.reshape(B, S, N_KV, HEAD_DIM)
    nb = S // WINDOW
    qb = q.reshape(B, nb, WINDOW, N_KV, GROUP, HEAD_DIM)
    kb = k.reshape(B, nb, WINDOW, N_KV, HEAD_DIM)
    vb = v.reshape(B, nb, WINDOW, N_KV, HEAD_DIM)
    k_band = jnp.concatenate([_prev_block(kb), kb], axis=2)
    v_band = jnp.concatenate([_prev_block(vb), vb], axis=2)
    q_pos = pos.reshape(nb, WINDOW)
    k_pos = jnp.concatenate([q_pos - WINDOW, q_pos], axis=1)
    o = _window_attention(qb, k_band, v_band, q_pos, k_pos, sinks).reshape(B, S, D_ATTN).astype(x.dtype)
    y = _merge_out(x, y_rnn, o, g_attn, m_rnn, m_attn, w_out_rnn, w_out_attn, w_o, ln_g, ln_b)
    return y, conv_new, h_new, k[:, S - WINDOW:], v[:, S - WINDOW:]


def _sample_layer(x, conv_buf, h0, k_win, v_win, w_in, conv_w, conv_b, w_gate_a, b_gate_a, w_gate_x, b_gate_x,
                  lru_lambda, sinks, w_out_rnn, w_out_attn, w_o, ln_g, ln_b):
    Bd, T, _ = x.shape
    u, g_rnn, q, k, v, g_attn, m_rnn, m_attn = _project(x, w_in)
    y_rnn, conv_new, h_new = _rglru_branch(u, g_rnn, conv_buf, h0, conv_w, conv_b, w_gate_a, b_gate_a,
                                           w_gate_x, b_gate_x, lru_lambda)
    pos = PAST_LEN + jnp.arange(T, dtype=jnp.int32)
    q = _rope(q.reshape(Bd, T, N_HEADS, HEAD_DIM), pos)
    k = _rope(k.reshape(Bd, T, N_KV, HEAD_DIM), pos)
    v = v.reshape(Bd, T, N_KV, HEAD_DIM)
    k_ctx = jnp.concatenate([k_win.astype(k.dtype), k], axis=1)
    v_ctx = jnp.concatenate([v_win.astype(v.dtype), v], axis=1)
    q_pos = pos[None]
    k_pos = jnp.concatenate([PAST_LEN - WINDOW + jnp.arange(WINDOW, dtype=jnp.int32), pos])[None]
    o = _window_attention(q.reshape(Bd, 1, T, N_KV, GROUP, HEAD_DIM), k_ctx[:, None], v_ctx[:, None],
                          q_pos, k_pos, sinks).reshape(Bd, T, D_ATTN).astype(x.dtype)
    y = _merge_out(x, y_rnn, o, g_attn, m_rnn, m_attn, w_out_rnn, w_out_attn, w_o, ln_g, ln_b)
    return y, conv_new, h_new, k_ctx[:, T:], v_ctx[:, T:]


def setup_inputs(seed: int = 0) -> dict:
    key = jax.random.key(seed)
    ks = jax.random.split(key, 20)
    f32 = jnp.float32
    u_a = jax.random.uniform(ks[10], (DEPTH, D_RNN), f32, minval=0.9, maxval=0.999)
    a0 = u_a ** (1.0 / LRU_C)
    lru_lambda = jnp.log(a0) - jnp.log1p(-a0)
    return {
        'x_prompt': jax.random.normal(ks[0], (BATCH, SEQ, D_MODEL), f32),
        'x_sample': jax.random.normal(ks[1], (DEC_BATCH, DEC_SEQ, D_MODEL), f32),
        'state_conv': jax.random.normal(ks[2], (DEPTH, DEC_BATCH, CONV_W - 1, D_RNN), f32),
        'state_lru': 0.5 * jax.random.normal(ks[3], (DEPTH, DEC_BATCH, D_RNN), f32),
        'cache_k_win': jax.random.normal(ks[4], (DEPTH, DEC_BATCH, WINDOW, N_KV, HEAD_DIM), f32),
        'cache_v_win': jax.random.normal(ks[5], (DEPTH, DEC_BATCH, WINDOW, N_KV, HEAD_DIM), f32),
        'w_in': jax.random.normal(ks[6], (DEPTH, D_MODEL, N_COLS), f32) * D_MODEL ** -0.5,
        'conv_w': jax.random.normal(ks[7], (DEPTH, CONV_W, D_RNN), f32) * CONV_W ** -0.5,
        'conv_b': 0.01 * jax.random.normal(ks[8], (DEPTH, D_RNN), f32),
        'w_gate_a': jax.random.normal(ks[9], (DEPTH, LRU_BLOCKS, LRU_BLOCK, LRU_BLOCK), f32) * LRU_BLOCK ** -0.5,
        'b_gate_a': 0.01 * jax.random.normal(ks[11], (DEPTH, D_RNN), f32),
        'w_gate_x': jax.random.normal(ks[12], (DEPTH, LRU_BLOCKS, LRU_BLOCK, LRU_BLOCK), f32) * LRU_BLOCK ** -0.5,
        'b_gate_x': 0.01 * jax.random.normal(ks[13], (DEPTH, D_RNN), f32),
        'lru_lambda': lru_lambda,
        'sinks': jax.random.normal(ks[14], (DEPTH, N_HEADS), f32),
        'w_out_rnn': jax.random.normal(ks[15], (DEPTH, D_RNN, D_MODEL), f32) * (D_RNN ** -0.5) * DN_BETA,
        'w_out_attn': jax.random.normal(ks[16], (DEPTH, D_ATTN, D_MODEL), f32) * (D_ATTN ** -0.5) * DN_BETA,
        'w_o': jax.random.normal(ks[17], (DEPTH, D_MODEL, D_MODEL), f32) * (D_MODEL ** -0.5) * DN_BETA,
        'ln_g': 1.0 + 0.01 * jax.random.normal(ks[18], (DEPTH, D_MODEL), f32),
        'ln_b': 0.01 * jax.random.normal(ks[19], (DEPTH, D_MODEL), f32),
    }


def reference(x_prompt, x_sample, state_conv, state_lru, cache_k_win, cache_v_win, w_in, conv_w, conv_b,
              w_gate_a, b_gate_a, w_gate_x, b_gate_x, lru_lambda, sinks, w_out_rnn, w_out_attn, w_o, ln_g, ln_b):
    y_p, y_s = x_prompt, x_sample
    conv_p, lru_p, kw_p, vw_p = [], [], [], []
    conv_s, lru_s, kw_s, vw_s = [], [], [], []
    for l in range(DEPTH):
        y_p, c1, h1, k1, v1 = _prompt_layer(y_p, w_in[l], conv_w[l], conv_b[l], w_gate_a[l], b_gate_a[l],
                                            w_gate_x[l], b_gate_x[l], lru_lambda[l], sinks[l],
                                            w_out_rnn[l], w_out_attn[l], w_o[l], ln_g[l], ln_b[l])
        y_s, c2, h2, k2, v2 = _sample_layer(y_s, state_conv[l], state_lru[l], cache_k_win[l], cache_v_win[l],
                                            w_in[l], conv_w[l], conv_b[l], w_gate_a[l], b_gate_a[l],
                                            w_gate_x[l], b_gate_x[l], lru_lambda[l], sinks[l],
                                            w_out_rnn[l], w_out_attn[l], w_o[l], ln_g[l], ln_b[l])
        conv_p.append(c1); lru_p.append(h1); kw_p.append(k1); vw_p.append(v1)
        conv_s.append(c2); lru_s.append(h2); kw_s.append(k2); vw_s.append(v2)
    conv_prompt = jnp.stack(conv_p)
    lru_prompt = jnp.stack(lru_p)
    k_win_prompt = jnp.stack(kw_p)
    v_win_prompt = jnp.stack(vw_p)
    conv_sample = jnp.stack(conv_s)
    lru_sample = jnp.stack(lru_s)
    k_win_sample = jnp.stack(kw_s)
    v_win_sample = jnp.stack(vw_s)
    return (y_p, y_s, conv_prompt, lru_prompt, k_win_prompt, v_win_prompt,
            conv_sample, lru_sample, k_win_sample, v_win_sample)
```

```python
import math
import numpy as np
import concourse.bass as bass
import concourse.mybir as mybir
from concourse.bass_utils import run_bass_kernel_spmd

F32 = mybir.dt.float32
BF16 = mybir.dt.bfloat16
AF = mybir.ActivationFunctionType
ALU = mybir.AluOpType
AX = mybir.AxisListType

LRU_C = 8.0
LN_EPS = 1e-5
ROPE_THETA = 10000.0


class Sched:
    def __init__(self, nc):
        self.nc = nc
        self.eng = {'pe': nc.tensor, 'act': nc.scalar, 'dve': nc.vector,
                    'pool': nc.gpsimd, 'sp': nc.sync}
        self.prog = {e: [] for e in self.eng}
        self.sem = {}
        self.cnt = {}
        self.seen = {e: {} for e in self.eng}
        self.lastw = {}
        self.readers = {}

    SAFE = list(range(155, 251))
    NGEN = 8

    def _sem(self, name):
        if name not in self.sem:
            num = self.SAFE[len(self.sem)]
            self.sem[name] = self.nc.alloc_semaphore('s_' + name, num=num)
            self.cnt[name] = 0
        return self.sem[name]

    def op(self, engine, fn, reads=(), writes=(), dma=None, dma_inc=16):
        deps = {}

        def need(t):
            if t is None:
                return
            s, v = t
            if deps.get(s, 0) < v:
                deps[s] = v
        for r in reads:
            need(self.lastw.get(r))
        for w in writes:
            need(self.lastw.get(w))
            for t in self.readers.get(w, ()):
                need(t)
        if dma is not None:
            if not dma.startswith('d_ring'):
                self.gen_i = getattr(self, 'gen_i', 0) + 1
                dma = 'd_g%d' % (self.gen_i % self.NGEN)
            self._sem(dma)
            if self.cnt[dma] > 0:
                need((dma, self.cnt[dma]))
            sname, inc = dma, dma_inc
        else:
            sname, inc = engine, 1
            self._sem(sname)
        waits = []
        seen = self.seen[engine]
        for s, v in deps.items():
            if engine == 'pe' and s == 'pe' and dma is None:
                continue
            if seen.get(s, 0) >= v:
                continue
            seen[s] = v
            waits.append((self.sem[s], v))
        self.cnt[sname] += inc
        ticket = (sname, self.cnt[sname])
        semh = self.sem[sname]
        E = self.eng[engine]

        def thunk(E=E, waits=waits, fn=fn, semh=semh, inc=inc):
            for sh, v in waits:
                E.wait_ge(sh, v)
            ins = fn(E)
            ins.then_inc(semh, inc)
        self.prog[engine].append(thunk)
        for w in writes:
            self.lastw[w] = ticket
            self.readers[w] = []
        for r in reads:
            self.readers.setdefault(r, []).append(ticket)
        return ticket

    def barrier(self, engines=('pe', 'act', 'dve', 'pool', 'sp')):
        deps = {s: c for s, c in self.cnt.items() if c > 0}
        for e in engines:
            waits = []
            for s, v in deps.items():
                if self.seen[e].get(s, 0) >= v:
                    continue
                self.seen[e][s] = v
                waits.append((self.sem[s], v))
            E = self.eng[e]

            def thunk(E=E, waits=waits):
                for sh, v in waits:
                    E.wait_ge(sh, v)
            self.prog[e].append(thunk)

    def emit(self):
        nc = self.nc
        with nc.Block() as block:
            @block.tensor
            def _(e):
                for t in self.prog['pe']:
                    t()

            @block.scalar
            def _(e):
                for t in self.prog['act']:
                    t()

            @block.vector
            def _(e):
                for t in self.prog['dve']:
                    t()

            @block.gpsimd
            def _(e):
                for t in self.prog['pool']:
                    t()

            @block.sync
            def _(e):
                for t in self.prog['sp']:
                    t()


class Cfg:
    def __init__(self, D=4096, H=32, KV=8, T=512, NT=2, NS=16, B=2, S=4096, DEC=128, PAST=8192, NCORES=8):
        self.D, self.H, self.KV, self.T, self.NT, self.NS = D, H, KV, T, NT, NS
        self.B, self.S, self.DEC, self.PAST, self.NCORES = B, S, DEC, PAST, NCORES
        self.G = H // KV
        self.DA = H * 128
        self.DKV = KV * 128
        self.KC = D // 128
        self.KA = self.DA // 128
        self.NB = T // 128
        self.NCOLS = 2 * D + 2 * self.DA + 2 * self.DKV + 2 * D
        self.NBLK = D // 256
        self.CHUNK = T * NT
        self.CPS = S // self.CHUNK
        assert self.CPS * B == NCORES and NS * NCORES == DEC
        self.ALPHA = (2.0 * 1) ** 0.25


def build(cfg):
    D, H, KV, T, NT, NS = cfg.D, cfg.H, cfg.KV, cfg.T, cfg.NT, cfg.NS
    G, DA, DKV, KC, KA, NB, NCOLS, NBLK = cfg.G, cfg.DA, cfg.DKV, cfg.KC, cfg.KA, cfg.NB, cfg.NCOLS, cfg.NBLK
    OFF_U, OFF_G, OFF_Q = 0, D, 2 * D
    OFF_K = OFF_Q + DA
    OFF_V = OFF_K + DKV
    OFF_GA = OFF_V + DKV
    OFF_MR = OFF_GA + DA
    OFF_MA = OFF_MR + D
    KT = min(2, KC)
    TS = T + NS
    TH = T + 128
    NCR = cfg.NCORES
    KM = max(KC, KA)

    nc = bass.Bass("TRN2", target_bir_lowering=False)
    S = Sched(nc)
    USE_CC = getattr(cfg, 'use_cc', False)

    def din(name, shape):
        return nc.dram_tensor(name, list(shape), F32, kind="ExternalInput").ap()

    def dout(name, shape):
        return nc.dram_tensor(name, list(shape), F32, kind="ExternalOutput").ap()
    xp = din("xp", [128 + NT * T, D])
    NPRE = (cfg.CPS - 1) * NT
    xpre = din("xpre", [max(NPRE, 1) * T, D])
    vmask_d = din("vmask", [128, max(NPRE, 1) * T])
    xs = din("xs", [NS, D])
    sconv = din("sconv", [NS, 3, D])
    slru = din("slru", [NS, D])
    ck = din("ck", [NS, 128, DKV])
    cv = din("cv", [NS, 128, DKV])
    w_in = din("w_in", [D, NCOLS])
    w_or = din("w_or", [D, D])
    w_oa = din("w_oa", [DA, D])
    w_o = din("w_o", [D, D])
    wga = din("wga", [NBLK, 256, 256])
    wgx = din("wgx", [NBLK, 256, 256])
    par = din("par", [128, 8, KC])
    sinks = din("sinks", [H])
    lng = din("lng", [D])
    lnb = din("lnb", [D])
    ident_d = din("ident", [128, 128])
    rmat_d = din("rmat", [128, 128])
    cs_d = din("cs", [128, 2, NT * T + 128 + 1])
    masks_d = din("masks", [128, 3, 128])
    cmask_d = din("cmask", [128, NCR])
    yp = dout("yp", [NT * T, D])
    ys = dout("ys", [NS, D])
    convp = dout("convp", [3, D])
    lrup = dout("lrup", [1, D])
    kwp = dout("kwp", [128, DKV])
    vwp = dout("vwp", [128, DKV])
    convs = dout("convs", [NS, 3, D])
    lrus = dout("lrus", [NS, D])
    kws = dout("kws", [NS, 128, DKV])
    vws = dout("vws", [NS, 128, DKV])
    cc_in = nc.dram_tensor("cc_in", [128, 2 * KC], F32).ap()
    cc_out = nc.dram_tensor("cc_out", [128 * NCR, 2 * KC], F32).ap()

    base = [16512]
    LIMIT = 229344

    def sb(name, shape, dt, at=None):
        nb = int(np.prod(shape[1:])) * (2 if dt == BF16 else 4)
        nb = (nb + 31) // 32 * 32
        if at is None:
            off = base[0]
            base[0] += nb
        else:
            off = at
        assert off + nb <= LIMIT, (name, off, nb)
        return nc.alloc_sbuf_tensor_at(name, list(shape), dt, offset=off), off + nb

    ident, _ = sb("ident", [128, 128], F32)
    rmat, _ = sb("rmat", [128, 128], F32)
    ones, _ = sb("ones", [128, 128], BF16)
    zer, _ = sb("zer", [128, 512], BF16)
    parT, _ = sb("parT", [128, 8, KC], F32)
    nls, _ = sb("nls", [128, 2, KC], F32)
    nbias, _ = sb("nbias", [128, 2, KC], F32)
    hnls, _ = sb("hnls", [128, 2, KC], F32)
    esink, _ = sb("esink", [128, H], F32)
    cmask, _ = sb("cmask", [128, NCR], F32)
    maskt, _ = sb("maskt", [128, 2, 512], BF16)
    maskf, _ = sb("maskf", [128, 3, 128], F32)
    hcar, _ = sb("hcar", [128, KC], F32)
    hin, _ = sb("hin", [128, KC], F32)
    ucar, _ = sb("ucar", [128, KC, 3], F32)
    rsum, _ = sb("rsum", [128, KC, NT], F32)
    pend, _ = sb("pend", [128, 2, KC], F32)
    small, _ = sb("small", [128, 64], F32)
    tst, _ = sb("tst", [128, 512], F32)
    if USE_CC:
        gath, _ = sb("gath", [128, NCR, 2 * KC], F32)
        vm = None
    else:
        vm, _ = sb("vm", [128, T], F32)
    kcar, _ = sb("kcar", [128, KV, 128], BF16)
    vcar, _ = sb("vcar", [128, DKV], BF16)
    NSLOT = 12
    ring = [sb("ring%d" % i, [128, KT, 512], BF16)[0] for i in range(NSLOT)]
    mgT, _ = sb("mgT", [128, KC, TS], BF16)
    R0 = base[0]
    xT, _ = sb("xT", [128, KC, TH], BF16)
    yoT, R1 = sb("yoT", [128, KM, TS], BF16)
    base[0] = R1
    sm, _ = sb("sm", [128, 4, TS], F32)
    base[0] = R1
    uf, _ = sb("uf", [128, 4, 3 + TS], F32)
    xc, _ = sb("xc", [128, 4, TS], F32)
    xcb, _ = sb("xcb", [128, 4, TS], BF16)
    tmp = [sb("tmp%d" % i, [128, TS], F32)[0] for i in range(6)]
    hT_off = base[0]
    hT, _ = sb("hT", [128, 4, TS], F32)
    xin = None
    sgb, _ = sb("sgb", [128, 4, TS], BF16)
    gw = [sb("gw%d" % i, [128, 2, 2, 256], BF16)[0] for i in range(2)]
    sT_off = base[0]
    sT, _ = sb("sT", [128, KC, 3, NS], F32)
    h0T, _ = sb("h0T", [128, KC, NS], F32)
    if base[0] - sT_off >= 4 * 2048:
        NXIN = 4
        xin = [sb("xin%d" % i, [128, 512], F32, at=sT_off + i * 2048)[0] for i in range(NXIN)]
    else:
        NXIN = 1
        xin = [sb("xin0", [128, 512], F32)[0]]
    rnn_end = base[0]
    base[0] = R1
    kT, _ = sb("kT", [128, KV, TH], BF16)
    Vt, _ = sb("Vt", [128, NB + 1, DKV], BF16)
    qf_off = base[0]
    qf = [sb("qf%d" % i, [128, 512], F32)[0] for i in range(2)]
    rtmp, _ = sb("rtmp", [128, 512], F32)
    qT, _ = sb("qT", [128, G, T], BF16)
    qTs, _ = sb("qTs", [128, H, NS], BF16)
    sgT, _ = sb("sgT", [128, G, T], F32)
    sgs, _ = sb("sgs", [128, H, NS], F32)
    pT = [sb("pT%d" % i, [128, 512], BF16)[0] for i in range(2)]
    rden, _ = sb("rden", [128, 256], F32)
    otmp, _ = sb("otmp", [128, 256], F32)
    kf32, _ = sb("kf32", [128, KV, 128], F32)
    ksf, _ = sb("ksf", [128, KV, NS], F32)
    win32, _ = sb("win32", [128, 512], F32)
    assert DKV * 4 <= 2 * 2048
    kwin, _ = sb("kwin", [128, DKV], F32, at=qf_off)
    vwinb, _ = sb("vwinb", [128, DKV], BF16)
    kwT, _ = sb("kwT", [128, 128], BF16)
    pTs, _ = sb("pTs", [128, H], BF16)
    sa1, _ = sb("sa1", [128, H], F32)
    sa2, _ = sb("sa2", [128, H], F32)
    cst, _ = sb("cst", [128, 2, T + 128 + 1], F32)
    att_end = base[0]
    base[0] = R0
    res, _ = sb("res", [128, NB + 1, D], F32)
    lnG, _ = sb("lnG", [128, D], F32)
    lnB, _ = sb("lnB", [128, D], F32)
    NXR = 6
    xr = [sb("xr%d" % i, [128, 512], F32)[0] for i in range(NXR)]
    junk, _ = sb("junk", [128, 512], F32)
    s1, _ = sb("s1", [128, NB + 1, D // 512], F32)
    s2, _ = sb("s2", [128, NB + 1, D // 512], F32)
    st, _ = sb("st", [128, 8, NB + 1], F32)
    fin_end = base[0]

    A = [nc.alloc_psum_tensor("A%d" % i, [128, 512], F32) for i in range(4)]
    X = nc.alloc_psum_tensor("X", [128, 512], F32)
    S0 = nc.alloc_psum_tensor("S0", [128, 512], F32)
    S1 = nc.alloc_psum_tensor("S1", [128, 512], F32)
    OD = nc.alloc_psum_tensor("OD", [128, 512], F32)

    def dma(q, out, in_, sem, reads=(), writes=()):
        return S.op(q, lambda E: E.dma_start(out=out, in_=in_), reads=reads, writes=writes, dma=sem)

    def act(out, in_, func, reads, writes, **kw):
        return S.op('act', lambda E: E.activation(out=out, in_=in_, func=func, **kw), reads=reads, writes=writes)

    def dve(fn, reads, writes):
        return S.op('dve', fn, reads=reads, writes=writes)

    def pool(fn, reads, writes):
        return S.op('pool', fn, reads=reads, writes=writes)

    ring_i = [0]

    def next_slot():
        i = ring_i[0] % NSLOT
        ring_i[0] += 1
        return i

    def stream(wsrc, nk, col0, pe_body, reads, writes, hook=None):
        nkq = nk // KT
        for kq in range(nkq):
            si = next_slot()
            slot = ring[si]
            src = wsrc[kq * KT * 128:(kq + 1) * KT * 128, col0:col0 + 512].rearrange("(k p) c -> p k c", p=128)
            dma('pool', slot[:, :, :], src, 'd_ring%d' % si, writes=['ring%d' % si])
            rd = ['ring%d' % si, 'zer'] + list(reads)
            if kq < nkq - 1 or nkq == 1 and False:
                def body(E, slot=slot, kq=kq):
                    ins = None
                    for kk in range(KT):
                        ins = pe_body(E, slot, kq * KT + kk, kk, nk, None)
                    return ins
                S.op('pe', body, reads=rd, writes=writes)
            else:
                for part in range(pe_body.nparts):
                    def body(E, slot=slot, kq=kq, part=part):
                        ins = None
                        for kk in range(KT):
                            ins = pe_body(E, slot, kq * KT + kk, kk, nk, part)
                        return ins
                    S.op('pe', body, reads=rd, writes=pe_body.part_writes(part, writes))
            if hook is not None:
                hook(kq, nkq)

    def fm_body(segs):
        def body(E, slot, k, kk, nk, part):
            ins = None
            if k == 0 and len(segs) > 1 and part in (None, 0):
                E.matmul(X[:, :], lhsT=zer[:, 0:128], rhs=zer[:, :], start=True, stop=False)
            for c in (range(4) if part is None else [part]):
                for si_, (rhs_fn, out_fn) in enumerate(segs):
                    if si_ == 0:
                        ins = E.matmul(out_fn(c), lhsT=slot[:, kk, c * 128:(c + 1) * 128], rhs=rhs_fn(k),
                                       start=(k == 0), stop=(k == nk - 1))
                    else:
                        ins = E.matmul(out_fn(c), lhsT=slot[:, kk, c * 128:(c + 1) * 128], rhs=rhs_fn(k),
                                       start=False, stop=(k == nk - 1 and c == 3 and si_ == len(segs) - 1))
            return ins
        body.nparts = 4
        body.part_writes = lambda part, writes: ['A%d' % part] + (['X'] if len(segs) > 1 else [])
        return body

    def tm_body(segs, keys=None):
        def body(E, slot, k, kk, nk, part):
            ins = None
            for lhs_fn, out_ap in (segs if part is None else [segs[part]]):
                ins = E.matmul(out_ap, lhsT=lhs_fn(k), rhs=slot[:, kk, :], start=(k == 0), stop=(k == nk - 1))
            return ins
        body.nparts = 1 if keys is None else len(segs)
        body.part_writes = (lambda part, writes: writes) if keys is None else (lambda part, writes: keys[part])
        if keys is None:
            inner = body

            def body2(E, slot, k, kk, nk, part):
                return inner(E, slot, k, kk, nk, None)
            body2.nparts = 1
            body2.part_writes = lambda part, writes: writes
            return body2
        return body

    def tstore(srcs, n, dst, reads, key):
        m = len(srcs)

        def body(E):
            ins = None
            for i, s_ in enumerate(srcs):
                ins = E.transpose(OD[0:n, 256 + i * 0:256 + i * 0 + 0] if False else X2[0:n, i * 128:(i + 1) * 128], s_, ident[:, :])
            return ins
        S.op('pe', body, reads=list(reads) + ['ident'], writes=['S1'])
        act(tst[0:n, 0:128 * m], S1[0:n, 0:128 * m], AF.Copy, reads=['S1'], writes=['tst'])
        dma('sp', dst, tst[0:n, 0:128 * m], 'd_tst', reads=['tst'])
    X2 = S1

    dma('sp', ident[:, :], ident_d, 'd_c0', writes=['ident'])
    dma('sp', rmat[:, :], rmat_d, 'd_c1', writes=['rmat'])
    dma('sp', parT[:, :, :], par, 'd_c2', writes=['parT'])
    dma('sp', esink[:, :], sinks.partition_broadcast(128), 'd_c3', writes=['esink'])
    dma('sp', cmask[:, :], cmask_d, 'd_c4', writes=['cmask'])
    dma('sp', maskf[:, :, :], masks_d, 'd_c5', writes=['maskf'])
    dve(lambda E: E.memset(ones[:, :], 1.0), [], ['ones'])
    dve(lambda E: E.memset(zer[:, :], 0.0), [], ['zer'])
    act(esink[:, :], esink[:, :], AF.Exp, ['esink'], ['esink'])
    for v_, pi in ((0, 0), (1, 2)):
        for hh in range(2):
            dve(lambda E, v_=v_, pi=pi, hh=hh: E.tensor_copy(out=maskt[:, v_, hh * 128:(hh + 1) * 128], in_=maskf[:, pi, :]), ['maskf'], ['maskt'])
            dve(lambda E, v_=v_, hh=hh: E.tensor_copy(out=maskt[:, v_, 256 + hh * 128:256 + (hh + 1) * 128], in_=maskf[:, 1, :]), ['maskf'], ['maskt'])
    dve(lambda E: E.tensor_scalar(out=nbias[:, :, :], in0=parT[:, 5:7, :], scalar1=0.5, scalar2=None, op0=ALU.mult), ['parT'], ['nbias'])
    act(nls[:, 0, :], parT[:, 7, :], AF.Exp, ['parT'], ['nls'], scale=-1.0)
    act(nls[:, 0, :], nls[:, 0, :], AF.Ln, ['nls'], ['nls'], bias=1.0)
    dve(lambda E: E.tensor_scalar(out=nls[:, 1, :], in0=nls[:, 0, :], scalar1=-2.0 * LRU_C, scalar2=None, op0=ALU.mult), ['nls'], ['nls2'])
    dve(lambda E: E.tensor_scalar(out=nls[:, 0, :], in0=nls[:, 0, :], scalar1=-LRU_C, scalar2=None, op0=ALU.mult), ['nls', 'nls2'], ['nls'])
    dve(lambda E: E.tensor_scalar(out=hnls[:, 0, :], in0=nls[:, 0, :], scalar1=0.5, scalar2=None, op0=ALU.mult), ['nls'], ['nls'])
    dve(lambda E: E.tensor_copy(out=hnls[:, 1, :], in_=nls[:, 0, :]), ['nls'], ['nls'])

    def load_xT(row0, nblk, col0, xp=xp):
        li = [0]
        for b in range(nblk):
            for cgp in range(D // 512):
                xi = xin[li[0] % NXIN]
                key = 'xin%d' % (li[0] % NXIN)
                li[0] += 1
                dma('sp', xi[:, :], xp[row0 + b * 128:row0 + (b + 1) * 128, cgp * 512:(cgp + 1) * 512], 'd_' + key, writes=[key])

                def body(E, xi=xi):
                    ins = None
                    for i in range(4):
                        ins = E.transpose(OD[:, i * 128:(i + 1) * 128], xi[:, i * 128:(i + 1) * 128], ident[:, :])
                    return ins
                S.op('pe', body, reads=[key, 'ident'], writes=['OD0', 'OD1'])
                act(xT[:, cgp * 4:(cgp + 1) * 4, col0 + b * 128:col0 + (b + 1) * 128],
                    OD[:, :].rearrange("p (k t) -> p k t", k=4), AF.Copy, ['OD0', 'OD1'], ['xT'])

    def load_sample_states():
        for cgp in range(D // 512):
            for kind in range(3):
                xi = tst
                if kind == 0:
                    n = NS
                    dma('sp', xi[0:NS, :], xs[:, cgp * 512:(cgp + 1) * 512], 'd_tstx', writes=['tst'])
                elif kind == 1:
                    n = NS * 3
                    dma('sp', xi[0:n, :], sconv.rearrange("b j d -> (b j) d")[:, cgp * 512:(cgp + 1) * 512], 'd_tstx', writes=['tst'])
                else:
                    n = NS
                    dma('sp', xi[0:NS, :], slru[:, cgp * 512:(cgp + 1) * 512], 'd_tstx', writes=['tst'])

                def body(E, xi=xi, n=n):
                    ins = None
                    for i in range(4):
                        ins = E.transpose(S0[:, i * 128:i * 128 + n], xi[0:n, i * 128:(i + 1) * 128], ident[0:n, 0:n])
                    return ins
                S.op('pe', body, reads=['tst', 'ident'], writes=['S0'])
                src = S0[:, :].rearrange("p (k t) -> p k t", k=4)[:, :, 0:n]
                if kind == 0:
                    act(xT[:, cgp * 4:(cgp + 1) * 4, T:T + NS], src, AF.Copy, ['S0'], ['xT'])
                elif kind == 1:
                    act(sT[:, cgp * 4:(cgp + 1) * 4, :, :].rearrange("p k j b -> p k b j"),
                        src.rearrange("p k (b j) -> p k b j", j=3), AF.Copy, ['S0'], ['sT'])
                else:
                    act(h0T[:, cgp * 4:(cgp + 1) * 4, :], src, AF.Copy, ['S0'], ['h0T'])

    def rnn_stage(tile, prepass, pre_first=False, tail_fn=None):
        last = (tile == NT - 1)
        samp = last and not prepass
        ncol = TS if samp else T
        NG = D // 512
        use_x_halo = (tile == 0) and (not prepass or USE_CC)
        deferred = []

        def pop_hook(kq, nkq):
            if deferred:
                deferred.pop(0)()

        def proj_u(gi, hook=None):
            segs = [(lambda k: xT[:, k, 0:T], lambda c: A[c][:, 0:T])]
            if use_x_halo:
                segs.append((lambda k: xT[:, k, T + 125:T + 128], lambda c: X[:, c * 128:c * 128 + 3]))
            if samp:
                segs.append((lambda k: xT[:, k, T:T + NS], lambda c: X[:, c * 128:c * 128 + NS]))
            stream(w_in, KC, OFF_U + gi * 512, fm_body(segs), ['xT'], ['A0', 'A1', 'A2', 'A3', 'X'], hook=hook)

        def phase1(gi):
            phase1a(gi)
            phase1b(gi)

        def phase1a(gi):
            ch0 = gi * 4
            for c in range(4):
                act(uf[:, c, 3:3 + T], A[c][:, 0:T], AF.Copy, ['A%d' % c], ['uf%d' % c])
            for c in range(4):
                ch = ch0 + c
                if use_x_halo:
                    act(uf[:, c, 0:3], X[:, c * 128:c * 128 + 3], AF.Copy, ['X'], ['uf%d' % c])
                elif prepass and pre_first:
                    dve(lambda E, c=c: E.memset(uf[:, c, 0:3], 0.0), [], ['uf%d' % c])
                else:
                    act(uf[:, c, 0:3], ucar[:, ch, :], AF.Copy, ['ucar'], ['uf%d' % c])
                if samp:
                    act(uf[:, c, 3 + T:3 + TS], X[:, c * 128:c * 128 + NS], AF.Copy, ['X'], ['uf%d' % c])

        def phase1b(gi):
            ch0 = gi * 4
            for c in range(4):
                ch = ch0 + c
                dve(lambda E, c=c, ch=ch: E.tensor_scalar(out=xc[:, c, 0:T], in0=uf[:, c, 0:T], scalar1=parT[:, 0, ch:ch + 1],
                                                          scalar2=parT[:, 4, ch:ch + 1], op0=ALU.mult, op1=ALU.add),
                    ['uf%d' % c, 'parT'], ['xc%d' % c])
                for j in range(1, 4):
                    dve(lambda E, c=c, ch=ch, j=j: E.scalar_tensor_tensor(out=xc[:, c, 0:T], in0=uf[:, c, j:j + T], scalar=parT[:, j, ch:ch + 1],
                                                                          in1=xc[:, c, 0:T], op0=ALU.mult, op1=ALU.add),
                        ['uf%d' % c, 'parT', 'xc%d' % c], ['xc%d' % c])
                if samp:
                    dve(lambda E, c=c, ch=ch: E.tensor_scalar(out=xc[:, c, T:TS], in0=uf[:, c, 3 + T:3 + TS], scalar1=parT[:, 3, ch:ch + 1],
                                                              scalar2=parT[:, 4, ch:ch + 1], op0=ALU.mult, op1=ALU.add),
                        ['uf%d' % c, 'parT', 'xc%d' % c], ['xc%d' % c])
                    for j in range(3):
                        dve(lambda E, c=c, ch=ch, j=j: E.scalar_tensor_tensor(out=xc[:, c, T:TS], in0=sT[:, ch, j, :], scalar=parT[:, j, ch:ch + 1],
                                                                              in1=xc[:, c, T:TS], op0=ALU.mult, op1=ALU.add),
                            ['sT', 'parT', 'xc%d' % c], ['xc%d' % c])
            for c in range(4):
                ch = ch0 + c
                act(xcb[:, c, 0:ncol], xc[:, c, 0:ncol], AF.Copy, ['xc%d' % c], ['xcb%d' % c])
                act(ucar[:, ch, :], uf[:, c, T:T + 3], AF.Copy, ['uf%d' % c], ['ucar'])
            if samp:
                deferred.append(lambda gi=gi: tstore([uf[:, c, 3 + T:3 + TS] for c in range(4)], NS, convs[:, 2, gi * 512:(gi + 1) * 512], ['uf%d' % c for c in range(4)], 'x'))
            if last and not prepass:
                deferred.append(lambda gi=gi: tstore([uf[:, c, T:T + 3] for c in range(4)], 3, convp[:, gi * 512:(gi + 1) * 512], ['uf%d' % c for c in range(4)], 'x'))

        def proj_g(gi):
            segs = [(lambda k: xT[:, k, 0:T], lambda c: A[c][:, 0:T])]
            if samp:
                segs.append((lambda k: xT[:, k, T:T + NS], lambda c: X[:, c * 128:c * 128 + NS]))
            stream(w_in, KC, OFF_G + gi * 512, fm_body(segs), ['xT'], ['A0', 'A1', 'A2', 'A3', 'X'], hook=pop_hook)
            while deferred:
                deferred.pop(0)()
            for c in range(4):
                srcs = [(A[c][:, 0:T], 0, T, 'A%d' % c)]
                if samp:
                    srcs.append((X[:, c * 128:c * 128 + NS], T, TS, 'X'))
                for src, a0, a1, key in srcs:
                    act(tmp[3][:, a0:a1], src, AF.Tanh, [key], ['t4'], scale=0.5)
                    dve(lambda E, c=c, src=src, a0=a0, a1=a1: E.scalar_tensor_tensor(out=sgb[:, c, a0:a1], in0=tmp[3][:, a0:a1], scalar=1.0, in1=src,
                                                                                     op0=ALU.add, op1=ALU.mult), [key, 't4'], ['sgb%d' % c])

        def phase2_pair(gi, bi):
            ch0 = gi * 4
            blk = gi * 2 + bi
            g_ = gw[blk % 2]
            gk = 'gw%d' % (blk % 2)
            dma('pool', g_[:, 0, :, :], wga[blk].rearrange("(dc p) e -> p dc e", p=128), 'd_' + gk + 'a', writes=[gk + 'a'])
            dma('pool', g_[:, 1, :, :], wgx[blk].rearrange("(dc p) e -> p dc e", p=128), 'd_' + gk + 'x', writes=[gk + 'x'])
            t1 = tmp[0]
            bufs = {}
            for e in range(2):
                c = bi * 2 + e
                ch = ch0 + c

                def gbody(E, g_=g_, bi=bi, e=e):
                    ins = None
                    for ax, bank in ((0, S0), (1, S1)):
                        for dc in range(2):
                            ins = E.matmul(bank[:, 0:T], lhsT=g_[:, ax, dc, e * 128:(e + 1) * 128], rhs=xcb[:, bi * 2 + dc, 0:T],
                                           start=(dc == 0), stop=(dc == 1))
                    if samp:
                        for ax in range(2):
                            for dc in range(2):
                                ins = E.matmul(OD[:, ax * 128:ax * 128 + NS], lhsT=g_[:, ax, dc, e * 128:(e + 1) * 128],
                                               rhs=xcb[:, bi * 2 + dc, T:TS], start=(dc == 0), stop=(dc == 1))
                    return ins
                S.op('pe', gbody, reads=[gk + 'a', gk + 'x', 'xcb%d' % (bi * 2), 'xcb%d' % (bi * 2 + 1)], writes=['S0', 'S1', 'OD0'])
                t2, t3 = (tmp[1], tmp[2]) if e == 0 else (tmp[4], tmp[5])
                k2, k3 = ('t2', 't3') if e == 0 else ('t2b', 't3b')
                bufs[e] = (t2, t3, k2, k3)
                hk = 'hT%d' % c
                act(t1[:, 0:T], S0[:, 0:T], AF.Tanh, ['S0', 'nbias'], ['t1'], scale=0.5, bias=nbias[:, 0, ch:ch + 1])
                act(t2[:, 0:T], S1[:, 0:T], AF.Tanh, ['S1', 'nbias'], [k2], scale=0.5, bias=nbias[:, 1, ch:ch + 1])
                if samp:
                    act(t1[:, T:TS], OD[:, 0:NS], AF.Tanh, ['OD0', 'nbias'], ['t1'], scale=0.5, bias=nbias[:, 0, ch:ch + 1])
                    act(t2[:, T:TS], OD[:, 128:128 + NS], AF.Tanh, ['OD0', 'nbias'], [k2], scale=0.5, bias=nbias[:, 1, ch:ch + 1])
                act(t3[:, 0:ncol], t1[:, 0:ncol], AF.Exp, ['t1', 'nls'], [k3], scale=hnls[:, 0, ch:ch + 1], bias=hnls[:, 0, ch:ch + 1])
                act(hT[:, c, 0:ncol], t1[:, 0:ncol], AF.Exp, ['t1', 'nls'], [hk], scale=hnls[:, 1, ch:ch + 1], bias=hnls[:, 1, ch:ch + 1])
            for e in range(2):
                c = bi * 2 + e
                ch = ch0 + c
                t2, t3, k2, k3 = bufs[e]
                hk = 'hT%d' % c
                act(hT[:, c, 0:ncol], hT[:, c, 0:ncol], AF.Sqrt, [hk], [hk], scale=-1.0, bias=1.0)
                dve(lambda E, t2=t2, c=c: E.scalar_tensor_tensor(out=t2[:, 0:ncol], in0=t2[:, 0:ncol], scalar=1.0, in1=hT[:, c, 0:ncol], op0=ALU.add, op1=ALU.mult), [k2, hk], [k2])
                dve(lambda E, t2=t2, c=c: E.scalar_tensor_tensor(out=t2[:, 0:ncol], in0=t2[:, 0:ncol], scalar=0.5, in1=xc[:, c, 0:ncol], op0=ALU.mult, op1=ALU.mult),
                    [k2, 'xc%d' % c], [k2])
                hkey = 'hin' if (tile == 0 and not prepass) else 'hcar'
                if prepass and USE_CC:
                    init = 0.0 if tile == 0 else hcar[:, ch:ch + 1]
                elif prepass:
                    init = 0.0 if pre_first else hcar[:, ch:ch + 1]
                    dve(lambda E, t2=t2: E.tensor_tensor(out=t2[:, 0:T], in0=t2[:, 0:T], in1=vm[:, 0:T], op=ALU.mult), [k2, 'vm'], [k2])
                elif tile == 0:
                    init = hin[:, ch:ch + 1]
                else:
                    init = hcar[:, ch:ch + 1]
                dve(lambda E, t2=t2, t3=t3, c=c, init=init: E.tensor_tensor_scan(out=hT[:, c, 0:T], data0=t3[:, 0:T], data1=t2[:, 0:T], initial=init,
                                                                                 op0=ALU.mult, op1=ALU.add), [k3, k2, hkey], [hk])
                dve(lambda E, c=c, ch=ch: E.tensor_copy(out=hcar[:, ch:ch + 1], in_=hT[:, c, T - 1:T]), [hk], ['hcar'])
                if samp:
                    dve(lambda E, t3=t3, c=c, ch=ch: E.tensor_tensor(out=hT[:, c, T:TS], in0=t3[:, T:TS], in1=h0T[:, ch, :], op=ALU.mult), [k3, 'h0T'], [hk])
                    dve(lambda E, t2=t2, c=c: E.tensor_tensor(out=hT[:, c, T:TS], in0=hT[:, c, T:TS], in1=t2[:, T:TS], op=ALU.add), [k2, hk], [hk])

        def phase3(gi):
            ch0 = gi * 4
            if samp:
                deferred.append(lambda gi=gi: tstore([hT[:, c, T:TS] for c in range(4)], NS, lrus[:, gi * 512:(gi + 1) * 512], ['hT%d' % c for c in range(4)], 'x'))
            if last and not prepass:
                deferred.append(lambda gi=gi: tstore([hT[:, c, T - 1:T] for c in range(4)], 1, lrup[:, gi * 512:(gi + 1) * 512], ['hT%d' % c for c in range(4)], 'x'))
            if not prepass:
                for c in range(4):
                    ch = ch0 + c
                    dve(lambda E, c=c, ch=ch: E.scalar_tensor_tensor(out=yoT[:, ch, 0:ncol], in0=hT[:, c, 0:ncol], scalar=0.5, in1=sgb[:, c, 0:ncol], op0=ALU.mult, op1=ALU.mult),
                        ['hT%d' % c, 'sgb%d' % c], ['yoT'])

        proj_u(0)
        phase1(0)
        for gi in range(NG):
            if not prepass:
                proj_g(gi)
            if gi + 1 < NG:
                done = []

                def hook(kq, nkq, gi=gi, done=done):
                    if not done and 2 * (kq + 1) // nkq >= 1:
                        phase2_pair(gi, 0)
                        done.append(1)
                proj_u(gi + 1, hook=hook)
                phase1a(gi + 1)
                phase2_pair(gi, 1)
                phase3(gi)
                phase1b(gi + 1)
            else:
                phase2_pair(gi, 0)
                if tail_fn is not None:
                    tail_fn()
                phase2_pair(gi, 1)
                phase3(gi)
        while deferred:
            deferred.pop(0)()

    def merge(tile, part):
        samp = (tile == NT - 1)
        ncol = TS if samp else T
        for mg in range(D // 512):
            segs = [(lambda k: xT[:, k, 0:T], lambda c: A[c][:, 0:T])]
            if samp:
                segs.append((lambda k: xT[:, k, T:T + NS], lambda c: X[:, c * 128:c * 128 + NS]))
            stream(w_in, KC, (OFF_MR if part == 1 else OFF_MA) + mg * 512, fm_body(segs), ['xT'], ['A0', 'A1', 'A2', 'A3', 'X'])
            for c in range(4):
                act(sm[:, c, 0:T], A[c][:, 0:T], AF.Tanh, ['A%d' % c], ['sm%d' % c], scale=0.5)
                if samp:
                    act(sm[:, c, T:TS], X[:, c * 128:c * 128 + NS], AF.Tanh, ['X'], ['sm%d' % c], scale=0.5)
                dve(lambda E, c=c: E.tensor_scalar(out=sm[:, c, 0:ncol], in0=sm[:, c, 0:ncol], scalar1=0.5, scalar2=0.5, op0=ALU.mult, op1=ALU.add), ['sm%d' % c], ['sm%d' % c])
            segs = [(lambda k: yoT[:, k, 0:T], lambda c: A[c][:, 0:T])]
            if samp:
                segs.append((lambda k: yoT[:, k, T:T + NS], lambda c: X[:, c * 128:c * 128 + NS]))
            stream(w_or if part == 1 else w_oa, KC if part == 1 else KA, mg * 512, fm_body(segs), ['yoT'], ['A0', 'A1', 'A2', 'A3', 'X'])
            for c in range(4):
                ch = mg * 4 + c
                srcs = [(A[c][:, 0:T], 0, T, 'A%d' % c)]
                if samp:
                    srcs.append((X[:, c * 128:c * 128 + NS], T, TS, 'X'))
                for src, a0, a1, key in srcs:
                    if part == 1:
                        dve(lambda E, c=c, ch=ch, src=src, a0=a0, a1=a1: E.tensor_tensor(out=mgT[:, ch, a0:a1], in0=src, in1=sm[:, c, a0:a1], op=ALU.mult),
                            [key, 'sm%d' % c], ['mgT'])
                    else:
                        dve(lambda E, c=c, src=src, a0=a0, a1=a1: E.tensor_tensor(out=sm[:, c, a0:a1], in0=src, in1=sm[:, c, a0:a1], op=ALU.mult),
                            [key, 'sm%d' % c], ['sm%d' % c])
                        dve(lambda E, c=c, ch=ch, a0=a0, a1=a1: E.tensor_tensor(out=mgT[:, ch, a0:a1], in0=mgT[:, ch, a0:a1], in1=sm[:, c, a0:a1], op=ALU.add),
                            ['sm%d' % c, 'mgT'], ['mgT'])

    def rope(src, n, cosap, sinap, outb, out32, skey, okeys, qi):
        q_ = qf[qi]
        qk = 'qf%d' % qi
        act(q_[:, 0:n], src, AF.Copy, [skey], [qk])
        S.op('pe', lambda E: E.matmul(src, lhsT=rmat[:, :], rhs=q_[:, 0:n], start=True, stop=True), reads=['rmat', qk], writes=[skey])
        dve(lambda E: E.tensor_tensor(out=rtmp[:, 0:n], in0=src, in1=sinap, op=ALU.mult), [skey, 'cs'], ['rtmp'])
        dve(lambda E: E.tensor_tensor(out=q_[:, 0:n], in0=q_[:, 0:n], in1=cosap, op=ALU.mult), [qk, 'cs'], [qk])
        if out32 is not None:
            dve(lambda E: E.tensor_tensor(out=out32, in0=q_[:, 0:n], in1=rtmp[:, 0:n], op=ALU.add), [qk, 'rtmp'], okeys)
        if outb is not None:
            dve(lambda E: E.tensor_tensor(out=outb, in0=q_[:, 0:n], in1=rtmp[:, 0:n], op=ALU.add), [qk, 'rtmp'], okeys)

    def attn_stage(tile, cst):
        last = (tile == NT - 1)
        samp = last
        scale = 128.0 ** -0.5
        cosm, sinm = cst[:, 0, 0:T], cst[:, 1, 0:T]
        if tile > 0:
            dve(lambda E: E.tensor_copy(out=kT[:, :, 0:128], in_=kcar[:, :, :]), ['kcar'], ['kT'])
            dve(lambda E: E.tensor_copy(out=Vt[:, 0, :], in_=vcar[:, :]), ['vcar'], ['Vt'])
        for cg in range(DKV // 512):
            segs = [((lambda k, tb=tb: xT[:, k, tb * 128:(tb + 1) * 128]), A[tb][:, :]) for tb in range(NB)]
            wr = ['A%d' % tb for tb in range(NB)]
            if tile == 0:
                segs.append((lambda k: xT[:, k, T:T + 128], OD[:, :]))
                wr += ['OD0', 'OD1']
            if samp:
                segs.append((lambda k: xT[:, k, T:T + NS], X[0:NS, :]))
                wr.append('X')
            stream(w_in, KC, OFF_V + cg * 512, tm_body(segs), ['xT'], wr)
            for tb in range(NB):
                act(Vt[:, tb + 1, cg * 512:(cg + 1) * 512], A[tb][:, :], AF.Copy, ['A%d' % tb], ['Vt'])
            if tile == 0:
                act(Vt[:, 0, cg * 512:(cg + 1) * 512], OD[:, :], AF.Copy, ['OD0', 'OD1'], ['Vt'])
            if last:
                act(win32[:, :], A[NB - 1][:, :], AF.Copy, ['A%d' % (NB - 1)], ['win32'])
                dma('sp', vwp[:, cg * 512:(cg + 1) * 512], win32[:, :], 'd_win32', reads=['win32'])
            if samp:
                act(tst[0:NS, :], X[0:NS, :], AF.Copy, ['X'], ['tst'])
                dma('sp', vws[:, 127, cg * 512:(cg + 1) * 512], tst[0:NS, :], 'd_tst', reads=['tst'], writes=['vws_new'])
        for cg in range(DKV // 512):
            segs = [(lambda k: xT[:, k, 0:T], lambda c: A[c][:, 0:T])]
            if tile == 0:
                segs.append((lambda k: xT[:, k, T:T + 128], lambda c: X[:, c * 128:(c + 1) * 128]))
            if samp:
                segs.append((lambda k: xT[:, k, T:T + NS], lambda c: X[:, c * 128:c * 128 + NS]))
            stream(w_in, KC, OFF_K + cg * 512, fm_body(segs), ['xT'], ['A0', 'A1', 'A2', 'A3', 'X'])
            for c in range(4):
                j = cg * 4 + c
                rope(A[c][:, 0:T], T, cosm, sinm, kT[:, j, 128:128 + T], None, 'A%d' % c, ['kT'], c % 2)
                if last:
                    dve(lambda E, j=j, c=c: E.tensor_tensor(out=kf32[:, j, :], in0=qf[c % 2][:, T - 128:T], in1=rtmp[:, T - 128:T], op=ALU.add),
                        ['qf%d' % (c % 2), 'rtmp'], ['kf32'])
                if tile == 0:
                    rope(X[:, c * 128:(c + 1) * 128], 128, cst[:, 0, T:T + 128], cst[:, 1, T:T + 128], kT[:, j, 0:128], None, 'X', ['kT'], c % 2)
                if samp:
                    rope(X[:, c * 128:c * 128 + NS], NS, cst[:, 0, T + 128:T + 129].broadcast_to([128, NS]), cst[:, 1, T + 128:T + 129].broadcast_to([128, NS]),
                         None, ksf[:, j, :], 'X', ['ksf'], c % 2)
            if last:
                tstore([kf32[:, cg * 4 + c, :] for c in range(4)], 128, kwp[:, cg * 512:(cg + 1) * 512], ['kf32'], 'x')
            if samp:
                tstore([ksf[:, cg * 4 + c, :] for c in range(4)], NS, kws[:, 127, cg * 512:(cg + 1) * 512], ['ksf'], 'x')
        if samp:
            S.readers.setdefault('tstdone', [])
        ui = [0]

        def unit_S(j, tb, hp):
            u = ui[0]
            ui[0] += 1
            Sb = S0 if u % 2 == 0 else S1
            sk = 'S0' if u % 2 == 0 else 'S1'
            p_ = pT[u % 2]
            pk = 'pT%d' % (u % 2)

            def sbody(E, Sb=Sb, j=j, tb=tb, hp=hp):
                rhs = qT[:, 2 * hp:2 * hp + 2, tb * 128:(tb + 1) * 128]
                E.matmul(Sb[:, 0:256].rearrange("p (h q) -> p h q", h=2), lhsT=kT[:, j, tb * 128:(tb + 1) * 128], rhs=rhs, start=True, stop=True)
                return E.matmul(Sb[:, 256:512].rearrange("p (h q) -> p h q", h=2), lhsT=kT[:, j, (tb + 1) * 128:(tb + 2) * 128], rhs=rhs, start=True, stop=True)
            S.op('pe', sbody, reads=['kT', 'qT'], writes=[sk])
            act(p_[:, :], Sb[:, :], AF.Exp, [sk], [pk], scale=scale)
            mv = 1 if (tile == 0 and tb == 0) else 0
            dve(lambda E, p_=p_, mv=mv: E.tensor_tensor(out=p_[:, :], in0=p_[:, :], in1=maskt[:, mv, :], op=ALU.mult), [pk, 'maskt'], [pk])
            return (j, tb, hp, p_, pk)

        def unit_PV(st_):
            j, tb, hp, p_, pk = st_

            def pvbody(E, p_=p_, j=j, tb=tb):
                E.matmul(OD[:, 0:256], lhsT=Vt[:, tb, j * 128:(j + 1) * 128], rhs=p_[:, 0:256], start=True, stop=False)
                E.matmul(OD[:, 0:256], lhsT=Vt[:, tb + 1, j * 128:(j + 1) * 128], rhs=p_[:, 256:512], start=False, stop=True)
                E.matmul(OD[:, 256:512], lhsT=ones[:, :], rhs=p_[:, 0:256], start=True, stop=False)
                return E.matmul(OD[:, 256:512], lhsT=ones[:, :], rhs=p_[:, 256:512], start=False, stop=True)
            S.op('pe', pvbody, reads=['Vt', pk, 'ones'], writes=['OD0', 'OD1'])
            for hh in range(2):
                hd = j * 4 + 2 * hp + hh
                dve(lambda E, hh=hh, hd=hd: E.tensor_scalar(out=rden[:, hh * 128:(hh + 1) * 128], in0=OD[:, 256 + hh * 128:256 + (hh + 1) * 128],
                                                            scalar1=esink[:, hd:hd + 1], scalar2=None, op0=ALU.add), ['OD1', 'esink'], ['rden'])
            dve(lambda E: E.reciprocal(out=rden[:, :], in_=rden[:, :]), ['rden'], ['rden'])
            dve(lambda E: E.tensor_tensor(out=otmp[:, :], in0=OD[:, 0:256], in1=rden[:, :], op=ALU.mult), ['OD0', 'rden'], ['otmp'])
            dve(lambda E, j=j, tb=tb, hp=hp: E.scalar_tensor_tensor(out=yoT[:, j * 4 + 2 * hp:j * 4 + 2 * hp + 2, tb * 128:(tb + 1) * 128],
                                                              in0=otmp[:, :].rearrange("p (h q) -> p h q", h=2), scalar=0.5,
                                                              in1=sgT[:, 2 * hp:2 * hp + 2, tb * 128:(tb + 1) * 128], op0=ALU.mult, op1=ALU.mult),
                ['otmp', 'sgT'], ['yoT'])

        for j in range(KV + 1):
            todo = [(j - 1, tb, hp) for tb in range(NB) for hp in range(2)] if j > 0 else []
            pend = []

            def ahook(kq, nkq, todo=todo, pend=pend):
                while pend:
                    unit_PV(pend.pop(0))
                n_ = (len(todo) + (nkq - kq) - 1) // (nkq - kq)
                for _ in range(min(n_, 2, len(todo))):
                    pend.append(unit_S(*todo.pop(0)))
            if j < KV:
                segs = [(lambda k: xT[:, k, 0:T], lambda c: A[c][:, 0:T])]
                if samp:
                    segs.append((lambda k: xT[:, k, T:T + NS], lambda c: X[:, c * 128:c * 128 + NS]))
                stream(w_in, KC, OFF_Q + j * 512, fm_body(segs), ['xT'], ['A0', 'A1', 'A2', 'A3', 'X'], hook=ahook)
            while pend or todo:
                while pend:
                    unit_PV(pend.pop(0))
                for _ in range(min(2, len(todo))):
                    pend.append(unit_S(*todo.pop(0)))
            if j == KV:
                break
            for c in range(4):
                rope(A[c][:, 0:T], T, cosm, sinm, qT[:, c, :], None, 'A%d' % c, ['qT'], c % 2)
                if samp:
                    rope(X[:, c * 128:c * 128 + NS], NS, cst[:, 0, T + 128:T + 129].broadcast_to([128, NS]), cst[:, 1, T + 128:T + 129].broadcast_to([128, NS]),
                         qTs[:, j * 4 + c, :], None, 'X', ['qTs'], c % 2)
            stream(w_in, KC, OFF_GA + j * 512, fm_body(segs), ['xT'], ['A0', 'A1', 'A2', 'A3', 'X'])
            for c in range(4):
                act(sgT[:, c, :], A[c][:, 0:T], AF.Tanh, ['A%d' % c], ['sgT'], scale=0.5)
                dve(lambda E, c=c: E.scalar_tensor_tensor(out=sgT[:, c, :], in0=sgT[:, c, :], scalar=1.0, in1=A[c][:, 0:T], op0=ALU.add, op1=ALU.mult), ['sgT', 'A%d' % c], ['sgT'])
                if samp:
                    hh_ = j * 4 + c
                    act(sgs[:, hh_, :], X[:, c * 128:c * 128 + NS], AF.Tanh, ['X'], ['sgs'], scale=0.5)
                    dve(lambda E, hh_=hh_, c=c: E.scalar_tensor_tensor(out=sgs[:, hh_, :], in0=sgs[:, hh_, :], scalar=1.0, in1=X[:, c * 128:c * 128 + NS], op0=ALU.add, op1=ALU.mult), ['sgs', 'X'], ['sgs'])
        if not last:
            dve(lambda E: E.tensor_copy(out=kcar[:, :, :], in_=kT[:, :, T:T + 128]), ['kT'], ['kcar'])
            dve(lambda E: E.tensor_copy(out=vcar[:, :], in_=Vt[:, NB, :]), ['Vt'], ['vcar'])
        if samp:
            dma('sp', kws[:, 0:127, :], ck[:, 1:128, :], 'd_shk', writes=['kws_old'])
            dma('sp', vws[:, 0:127, :], cv[:, 1:128, :], 'd_shv', writes=['vws_old'])
            S.barrier()
            for b in range(NS):
                dma('sp', kwin[:, :], kws[b], 'd_kwin', reads=['kws_old'], writes=['kwin'])
                dma('pool', vwinb[:, :], vws[b], 'd_vwin', reads=['vws_old', 'vws_new'], writes=['vwinb'])
                for j in range(KV):
                    S.op('pe', lambda E, j=j: E.transpose(S0[:, 0:128], kwin[:, j * 128:(j + 1) * 128], ident[:, :]), reads=['kwin', 'ident'], writes=['S0'])
                    act(kwT[:, :], S0[:, 0:128], AF.Copy, ['S0'], ['kwT'])
                    S.op('pe', lambda E, j=j, b=b: E.matmul(S1[:, j * 4:(j + 1) * 4], lhsT=kwT[:, :], rhs=qTs[:, j * 4:(j + 1) * 4, b], start=True, stop=True),
                         reads=['kwT', 'qTs'], writes=['S1'])
                act(pTs[:, :], S1[:, 0:H], AF.Exp, ['S1'], ['pTs'], scale=scale)

                def pvs(E):
                    for j in range(KV):
                        E.matmul(OD[:, j * 4:(j + 1) * 4], lhsT=vwinb[:, j * 128:(j + 1) * 128], rhs=pTs[:, j * 4:(j + 1) * 4], start=True, stop=True)
                    return E.matmul(OD[:, 256:256 + H], lhsT=ones[:, :], rhs=pTs[:, :], start=True, stop=True)
                S.op('pe', pvs, reads=['vwinb', 'pTs', 'ones'], writes=['OD0', 'OD1'])
                dve(lambda E: E.tensor_tensor(out=sa1[:, :], in0=OD[:, 256:256 + H], in1=esink[:, :], op=ALU.add), ['OD1', 'esink'], ['sa1'])
                dve(lambda E: E.reciprocal(out=sa1[:, :], in_=sa1[:, :]), ['sa1'], ['sa1'])
                dve(lambda E: E.tensor_tensor(out=sa2[:, :], in0=OD[:, 0:H], in1=sa1[:, :], op=ALU.mult), ['OD0', 'sa1'], ['sa2'])
                dve(lambda E, b=b: E.scalar_tensor_tensor(out=yoT[:, 0:H, T + b], in0=sa2[:, :], scalar=0.5, in1=sgs[:, :, b], op0=ALU.mult, op1=ALU.mult), ['sa2', 'sgs'], ['yoT'])

    def final_stage(tile):
        samp = (tile == NT - 1)
        NCG = D // 512
        nbx = NB + (1 if samp else 0)
        dma('sp', lnG[:, :], lng.partition_broadcast(128), 'd_lnG', writes=['lnG'])
        dma('sp', lnB[:, :], lnb.partition_broadcast(128), 'd_lnB', writes=['lnB'])
        li = 0
        dve(lambda E: E.memset(s1[:, :, :], 0.0), [], ['s1'])
        dve(lambda E: E.memset(s2[:, :, :], 0.0), [], ['s2'])
        for cg in range(NCG):
            segs = [((lambda k, tb=tb: mgT[:, k, tb * 128:(tb + 1) * 128]), A[tb][:, :]) for tb in range(NB)]
            wr = ['A%d' % tb for tb in range(NB)]
            if samp:
                segs.append((lambda k: mgT[:, k, T:T + NS], X[0:NS, :]))
                wr.append('X')
            stream(w_o, KC, cg * 512, tm_body(segs, keys=[[w_] for w_ in wr]), ['mgT'], wr)
            for tb in range(nbx):
                xi = xr[li % NXR]
                xk = 'xr%d' % (li % NXR)
                li += 1
                if tb < NB:
                    n = 128
                    src = A[tb][:, :]
                    skey = 'A%d' % tb
                    dma('sp', xi[:, :], xp[128 + tile * T + tb * 128:128 + tile * T + (tb + 1) * 128, cg * 512:(cg + 1) * 512], 'd_' + xk, writes=[xk])
                else:
                    n = NS
                    src = X[0:NS, :]
                    skey = 'X'
                    dma('sp', xi[0:NS, :], xs[:, cg * 512:(cg + 1) * 512], 'd_' + xk, writes=[xk])
                dve(lambda E, xi=xi, n=n, src=src, tb=tb, cg=cg: E.scalar_tensor_tensor(out=res[0:n, tb, cg * 512:(cg + 1) * 512], in0=xi[0:n, :], scalar=cfg.ALPHA,
                                                                                  in1=src, op0=ALU.mult, op1=ALU.add, accum_out=s1[0:n, tb, cg:cg + 1]),
                    [xk, skey], ['res%d' % tb, 's1'])
                act(junk[0:n, :], res[0:n, tb, cg * 512:(cg + 1) * 512], AF.Square, ['res%d' % tb], ['junk', 's2'], accum_out=s2[0:n, tb, cg:cg + 1])
        dve(lambda E: E.tensor_reduce(out=st[:, 0, 0:nbx], in_=s1[:, 0:nbx, :], axis=AX.X, op=ALU.add), ['s1'], ['st'])
        dve(lambda E: E.tensor_reduce(out=st[:, 1, 0:nbx], in_=s2[:, 0:nbx, :], axis=AX.X, op=ALU.add), ['s2', 'st'], ['st'])
        dve(lambda E: E.tensor_scalar(out=st[:, 0, 0:nbx], in0=st[:, 0, 0:nbx], scalar1=1.0 / D, scalar2=None, op0=ALU.mult), ['st'], ['st'])
        dve(lambda E: E.tensor_scalar(out=st[:, 1, 0:nbx], in0=st[:, 1, 0:nbx], scalar1=1.0 / D, scalar2=None, op0=ALU.mult), ['st'], ['st'])
        dve(lambda E: E.tensor_tensor(out=st[:, 2, 0:nbx], in0=st[:, 0, 0:nbx], in1=st[:, 0, 0:nbx], op=ALU.mult), ['st'], ['st'])
        dve(lambda E: E.tensor_tensor(out=st[:, 1, 0:nbx], in0=st[:, 1, 0:nbx], in1=st[:, 2, 0:nbx], op=ALU.subtract), ['st'], ['st'])
        dve(lambda E: E.tensor_scalar(out=st[:, 1, 0:nbx], in0=st[:, 1, 0:nbx], scalar1=LN_EPS, scalar2=None, op0=ALU.add), ['st'], ['st'])
        act(st[:, 1, 0:nbx], st[:, 1, 0:nbx], AF.Ln, ['st'], ['st'])
        act(st[:, 1, 0:nbx], st[:, 1, 0:nbx], AF.Exp, ['st'], ['st'], scale=-0.5)
        dve(lambda E: E.tensor_tensor(out=st[:, 3, 0:nbx], in0=st[:, 0, 0:nbx], in1=st[:, 1, 0:nbx], op=ALU.mult), ['st'], ['st'])
        dve(lambda E: E.tensor_scalar(out=st[:, 3, 0:nbx], in0=st[:, 3, 0:nbx], scalar1=-1.0, scalar2=None, op0=ALU.mult), ['st'], ['st'])
        for tb in range(nbx):
            n = 128 if tb < NB else NS
            rk = 'res%d' % tb
            act(res[0:n, tb, :], res[0:n, tb, :], AF.Identity, [rk, 'st'], [rk], scale=st[0:n, 1, tb:tb + 1], bias=st[0:n, 3, tb:tb + 1])
            dve(lambda E, n=n, tb=tb: E.tensor_tensor(out=res[0:n, tb, :], in0=res[0:n, tb, :], in1=lnG[0:n, :], op=ALU.mult), [rk, 'lnG'], [rk])
            dve(lambda E, n=n, tb=tb: E.tensor_tensor(out=res[0:n, tb, :], in0=res[0:n, tb, :], in1=lnB[0:n, :], op=ALU.add), [rk, 'lnB'], [rk])
            if tb < NB:
                dma('sp', yp[tile * T + tb * 128:tile * T + (tb + 1) * 128, :], res[:, tb, :], 'd_' + rk, reads=[rk])
            else:
                dma('sp', ys[:, :], res[0:NS, tb, :], 'd_' + rk, reads=[rk])

    print("SBUF R0", R0, "R1", R1, "rnn_end", rnn_end, "att_end", att_end, "fin_end", fin_end)
    assert max(rnn_end, att_end, fin_end) <= LIMIT

    def load_tile(tile, with_cs):
        load_xT(128 + tile * T, NB, 0)
        if tile == 0:
            load_xT(0, 1, T)

    def load_cs(tile):
        if True:
            dma('sp', cst[:, :, 0:T], cs_d[:, :, 128 + tile * T:128 + (tile + 1) * T], 'd_cs', writes=['cs'])
            dma('sp', cst[:, :, T:T + 128], cs_d[:, :, 0:128], 'd_cs', writes=['cs'])
            S.op('sp', lambda E: E.dma_start(out=cst[:, :, T + 128:T + 129], in_=cs_d[:, :, 128 + NT * T:129 + NT * T], allow_slow_non_contiguous=True),
                 writes=['cs'], dma='d_cs')

    if USE_CC:
        dve(lambda E: E.memset(rsum[:, :, :], 0.0), [], ['rsum'])
        for tile in range(NT):
            load_tile(tile, False)
            S.barrier()
            rnn_stage(tile, True)
            S.barrier()
        dve(lambda E: E.tensor_reduce(out=pend[:, 0, :], in_=rsum[:, :, :], axis=AX.X, op=ALU.add), ['rsum'], ['pend'])
        dve(lambda E: E.tensor_tensor(out=pend[:, 0, :], in0=pend[:, 0, :], in1=nls[:, 0, :], op=ALU.mult), ['pend', 'nls'], ['pend'])
        act(pend[:, 0, :], pend[:, 0, :], AF.Exp, ['pend'], ['pend'])
        act(pend[:, 1, :], hcar[:, :], AF.Copy, ['hcar', 'pend'], ['pend'])
        dma('sp', cc_in, pend[:, :, :].rearrange("p a k -> p (a k)"), 'd_cc', reads=['pend'], writes=['cc_in'])
        S.barrier()
        S.op('pool', lambda E: E.collective_compute("AllGather", ALU.bypass, replica_groups=[list(range(NCR))], ins=[cc_in], outs=[cc_out]),
             reads=['cc_in'], writes=['cc_out'], dma='d_ccg', dma_inc=1)
        dma('sp', gath[:, :, :], cc_out.rearrange("(r p) f -> p r f", p=128), 'd_cc2', reads=['cc_out'], writes=['gath'])
        dve(lambda E: E.memset(hin[:, :], 0.0), [], ['hin'])
        for r in range(NCR):
            dve(lambda E, r=r: E.tensor_tensor(out=small[:, 0:KC], in0=gath[:, r, 0:KC], in1=hin[:, :], op=ALU.mult), ['gath', 'hin'], ['small'])
            dve(lambda E, r=r: E.tensor_tensor(out=small[:, 0:KC], in0=small[:, 0:KC], in1=gath[:, r, KC:2 * KC], op=ALU.add), ['gath', 'small'], ['small'])
            dve(lambda E: E.tensor_tensor(out=small[:, 0:KC], in0=small[:, 0:KC], in1=hin[:, :], op=ALU.subtract), ['hin', 'small'], ['small'])
            dve(lambda E, r=r: E.scalar_tensor_tensor(out=hin[:, :], in0=small[:, 0:KC], scalar=cmask[:, r:r + 1], in1=hin[:, :], op0=ALU.mult, op1=ALU.add),
                ['small', 'cmask', 'hin'], ['hin'])
    else:
        dve(lambda E: E.memset(hcar[:, :], 0.0), [], ['hcar'])
        load_xT(0, NB, 0, xp=xpre)
        for pt in range(NPRE):
            dma('sp', vm[:, :], vmask_d[:, pt * T:(pt + 1) * T], 'd_vm', writes=['vm'])
            if pt + 1 < NPRE:
                nxt = (lambda pt=pt: load_xT((pt + 1) * T, NB, 0, xp=xpre))
            else:
                nxt = (lambda: load_tile(0, True))
            rnn_stage(0, True, pre_first=(pt == 0), tail_fn=nxt)
        act(hin[:, :], hcar[:, :], AF.Copy, ['hcar'], ['hin'])
    dma('sp', convs[:, 0:2, :], sconv[:, 1:3, :], 'd_cvs')
    S.barrier()
    for tile in range(NT):
        if tile > 0 or USE_CC or NPRE == 0:
            load_tile(tile, True)
        if tile == NT - 1:
            S.barrier()
            load_sample_states()
        rnn_stage(tile, False)
        S.barrier()
        merge(tile, 1)
        S.barrier()
        load_cs(tile)
        attn_stage(tile, cst)
        S.barrier()
        merge(tile, 2)
        S.barrier()
        final_stage(tile)
        S.barrier()
    S.barrier()
    S.emit()
    return nc


_CACHE = {}


def _consts(cfg, core):
    T, NT = cfg.T, cfg.NT
    seqi, q = divmod(core, cfg.CPS)
    ident = np.eye(128, dtype=np.float32)
    rm = np.zeros((128, 128), np.float32)
    for d in range(64):
        rm[d + 64, d] = -1.0
        rm[d, d + 64] = 1.0
    half = 64
    inv = (ROPE_THETA ** (-np.arange(half, dtype=np.float32) / half)).astype(np.float32)
    start = q * cfg.CHUNK
    pos = np.concatenate([np.arange(start - 128, start + cfg.CHUNK), [cfg.PAST]]).astype(np.float32)
    ang = (pos[None, :] * np.concatenate([inv, inv])[:, None]).astype(np.float32)
    cs = np.stack([np.cos(ang), np.sin(ang)], axis=1).astype(np.float32)
    s_ = np.arange(128)[:, None]
    q_ = np.arange(128)[None, :]
    mprev = (s_ > q_).astype(np.float32)
    mcur = (s_ <= q_).astype(np.float32)
    mh = mprev if q > 0 else np.zeros_like(mprev)
    masks = np.stack([mprev, mcur, mh], axis=1).astype(np.float32)
    cm = np.zeros((128, cfg.NCORES), np.float32)
    for r in range(cfg.NCORES):
        if r // cfg.CPS == seqi and r < core:
            cm[:, r] = 1.0
    return dict(ident=ident, rmat=rm, cs=np.ascontiguousarray(cs), masks=np.ascontiguousarray(masks), cmask=cm)


def run(cfg, x_prompt, x_sample, state_conv, state_lru, cache_k_win, cache_v_win, w_in, conv_w, conv_b,
        w_gate_a, b_gate_a, w_gate_x, b_gate_x, lru_lambda, sinks, w_out_rnn, w_out_attn, w_o, ln_g, ln_b):
    key = (cfg.D, cfg.H, cfg.KV, cfg.T, cfg.NT, cfg.NS)
    if key not in _CACHE:
        _CACHE[key] = build(cfg)
    nc = _CACHE[key]
    f = lambda a: np.ascontiguousarray(np.asarray(a, dtype=np.float32))
    D, KC, NS = cfg.D, cfg.KC, cfg.NS
    fm = lambda v: f(v).reshape(KC, 128).T
    par = np.ascontiguousarray(np.stack([fm(conv_w[0][0]), fm(conv_w[0][1]), fm(conv_w[0][2]), fm(conv_w[0][3]), fm(conv_b[0]),
                                         fm(b_gate_a[0]), fm(b_gate_x[0]), fm(lru_lambda[0])], axis=1))
    shared = dict(w_in=f(w_in[0]), w_or=f(w_out_rnn[0]), w_oa=f(w_out_attn[0]), w_o=f(w_o[0]), wga=f(w_gate_a[0]), wgx=f(w_gate_x[0]),
                  par=par, sinks=f(sinks[0]), lng=f(ln_g[0]), lnb=f(ln_b[0]))
    xpr = f(x_prompt)
    in_maps = []
    for core in range(cfg.NCORES):
        seqi, q = divmod(core, cfg.CPS)
        start = q * cfg.CHUNK
        xpc = np.zeros((128 + cfg.CHUNK, D), np.float32)
        if q > 0:
            xpc[:] = xpr[seqi, start - 128:start + cfg.CHUNK]
        else:
            xpc[128:] = xpr[seqi, 0:cfg.CHUNK]
        sl = slice(core * NS, (core + 1) * NS)
        PRE = max((cfg.CPS - 1) * cfg.NT, 1) * cfg.T
        xpre = np.zeros((PRE, D), np.float32)
        vmask = np.zeros((128, PRE), np.float32)
        if start > 0:
            xpre[PRE - start:] = xpr[seqi, 0:start]
            vmask[:, PRE - start:] = 1.0
        m = dict(shared)
        m.update(xpre=xpre, vmask=vmask)
        m.update(_consts(cfg, core))
        m.update(xp=xpc, xs=f(x_sample[sl, 0]), sconv=f(state_conv[0, sl]), slru=f(state_lru[0, sl]),
                 ck=f(cache_k_win[0, sl]).reshape(NS, 128, cfg.DKV), cv=f(cache_v_win[0, sl]).reshape(NS, 128, cfg.DKV))
        in_maps.append(m)
    resu = run_bass_kernel_spmd(nc, in_maps, core_ids=list(range(cfg.NCORES)))
    R = resu.results
    B, S_, CPS = cfg.B, cfg.S, cfg.CPS
    y_p = np.zeros((B, S_, D), np.float32)
    for core in range(cfg.NCORES):
        seqi, q = divmod(core, CPS)
        y_p[seqi, q * cfg.CHUNK:(q + 1) * cfg.CHUNK] = R[core]["yp"]
    y_s = np.concatenate([R[c]["ys"] for c in range(cfg.NCORES)], axis=0)[:, None, :]
    lastc = [seqi * CPS + CPS - 1 for seqi in range(B)]
    conv_p = np.stack([R[c]["convp"] for c in lastc])[None]
    lru_p = np.stack([R[c]["lrup"][0] for c in lastc])[None]
    kw_p = np.stack([R[c]["kwp"].reshape(128, cfg.KV, 128) for c in lastc])[None]
    vw_p = np.stack([R[c]["vwp"].reshape(128, cfg.KV, 128) for c in lastc])[None]
    conv_s = np.concatenate([R[c]["convs"] for c in range(cfg.NCORES)], axis=0)[None]
    lru_s = np.concatenate([R[c]["lrus"] for c in range(cfg.NCORES)], axis=0)[None]
    kw_s = np.concatenate([R[c]["kws"] for c in range(cfg.NCORES)], axis=0).reshape(cfg.DEC, 128, cfg.KV, 128)[None]
    vw_s = np.concatenate([R[c]["vws"] for c in range(cfg.NCORES)], axis=0).reshape(cfg.DEC, 128, cfg.KV, 128)[None]
    return (y_p, y_s, conv_p, lru_p, kw_p, vw_p, conv_s, lru_s, kw_s, vw_s)


def kernel(**inputs):
    return run(Cfg(), **inputs)
```

```python
import math
import numpy as np
import concourse.bass as bass
import concourse.mybir as mybir
from concourse.bass_utils import run_bass_kernel_spmd

F32 = mybir.dt.float32
BF16 = mybir.dt.bfloat16
AF = mybir.ActivationFunctionType
ALU = mybir.AluOpType
AX = mybir.AxisListType

LRU_C = 8.0
LN_EPS = 1e-5
ROPE_THETA = 10000.0


class Sched:
    def __init__(self, nc):
        self.nc = nc
        self.eng = {'pe': nc.tensor, 'act': nc.scalar, 'dve': nc.vector,
                    'pool': nc.gpsimd, 'sp': nc.sync}
        self.prog = {e: [] for e in self.eng}
        self.sem = {}
        self.cnt = {}
        self.seen = {e: {} for e in self.eng}
        self.lastw = {}
        self.readers = {}

    SAFE = list(range(155, 251))
    NGEN = 8

    def _sem(self, name):
        if name not in self.sem:
            num = self.SAFE[len(self.sem)]
            self.sem[name] = self.nc.alloc_semaphore('s_' + name, num=num)
            self.cnt[name] = 0
        return self.sem[name]

    def op(self, engine, fn, reads=(), writes=(), dma=None, dma_inc=16):
        deps = {}

        def need(t):
            if t is None:
                return
            s, v = t
            if deps.get(s, 0) < v:
                deps[s] = v
        for r in reads:
            need(self.lastw.get(r))
        for w in writes:
            need(self.lastw.get(w))
            for t in self.readers.get(w, ()):
                need(t)
        if dma is not None:
            if not dma.startswith('d_ring'):
                self.gen_i = getattr(self, 'gen_i', 0) + 1
                dma = 'd_g%d' % (self.gen_i % self.NGEN)
            self._sem(dma)
            if self.cnt[dma] > 0:
                need((dma, self.cnt[dma]))
            sname, inc = dma, dma_inc
        else:
            sname, inc = engine, 1
            self._sem(sname)
        waits = []
        seen = self.seen[engine]
        for s, v in deps.items():
            if engine == 'pe' and s == 'pe' and dma is None:
                continue
            if seen.get(s, 0) >= v:
                continue
            seen[s] = v
            waits.append((self.sem[s], v))
        self.cnt[sname] += inc
        ticket = (sname, self.cnt[sname])
        semh = self.sem[sname]
        E = self.eng[engine]

        def thunk(E=E, waits=waits, fn=fn, semh=semh, inc=inc):
            for sh, v in waits:
                E.wait_ge(sh, v)
            ins = fn(E)
            ins.then_inc(semh, inc)
        self.prog[engine].append(thunk)
        for w in writes:
            self.lastw[w] = ticket
            self.readers[w] = []
        for r in reads:
            self.readers.setdefault(r, []).append(ticket)
        return ticket

    def barrier(self, engines=('pe', 'act', 'dve', 'pool', 'sp')):
        deps = {s: c for s, c in self.cnt.items() if c > 0}
        for e in engines:
            waits = []
            for s, v in deps.items():
                if self.seen[e].get(s, 0) >= v:
                    continue
                self.seen[e][s] = v
                waits.append((self.sem[s], v))
            E = self.eng[e]

            def thunk(E=E, waits=waits):
                for sh, v in waits:
                    E.wait_ge(sh, v)
            self.prog[e].append(thunk)

    def emit(self):
        nc = self.nc
        with nc.Block() as block:
            @block.tensor
            def _(e):
                for t in self.prog['pe']:
                    t()

            @block.scalar
            def _(e):
                for t in self.prog['act']:
                    t()

            @block.vector
            def _(e):
                for t in self.prog['dve']:
                    t()

            @block.gpsimd
            def _(e):
                for t in self.prog['pool']:
                    t()

            @block.sync
            def _(e):
                for t in self.prog['sp']:
                    t()


class Cfg:
    def __init__(self, D=4096, H=32, KV=8, T=512, NT=2, NS=16, B=2, S=4096, DEC=128, PAST=8192, NCORES=8):
        self.D, self.H, self.KV, self.T, self.NT, self.NS = D, H, KV, T, NT, NS
        self.B, self.S, self.DEC, self.PAST, self.NCORES = B, S, DEC, PAST, NCORES
        self.G = H // KV
        self.DA = H * 128
        self.DKV = KV * 128
        self.KC = D // 128
        self.KA = self.DA // 128
        self.NB = T // 128
        self.NCOLS = 2 * D + 2 * self.DA + 2 * self.DKV + 2 * D
        self.NBLK = D // 256
        self.CHUNK = T * NT
        self.CPS = S // self.CHUNK
        assert self.CPS * B == NCORES and NS * NCORES == DEC
        self.ALPHA = (2.0 * 1) ** 0.25


def build(cfg):
    D, H, KV, T, NT, NS = cfg.D, cfg.H, cfg.KV, cfg.T, cfg.NT, cfg.NS
    G, DA, DKV, KC, KA, NB, NCOLS, NBLK = cfg.G, cfg.DA, cfg.DKV, cfg.KC, cfg.KA, cfg.NB, cfg.NCOLS, cfg.NBLK
    OFF_U, OFF_G, OFF_Q = 0, D, 2 * D
    OFF_K = OFF_Q + DA
    OFF_V = OFF_K + DKV
    OFF_GA = OFF_V + DKV
    OFF_MR = OFF_GA + DA
    OFF_MA = OFF_MR + D
    KT = min(4, KC)
    TS = T + NS
    TH = T + 128
    NCR = cfg.NCORES
    KM = max(KC, KA)

    nc = bass.Bass("TRN2", target_bir_lowering=False)
    S = Sched(nc)
    USE_CC = getattr(cfg, 'use_cc', False)

    def din(name, shape):
        return nc.dram_tensor(name, list(shape), F32, kind="ExternalInput").ap()

    def dout(name, shape):
        return nc.dram_tensor(name, list(shape), F32, kind="ExternalOutput").ap()
    xp = din("xp", [128 + NT * T, D])
    NPRE = (cfg.CPS - 1) * NT
    xpre = din("xpre", [max(NPRE, 1) * T, D])
    vmask_d = din("vmask", [128, max(NPRE, 1) * T])
    xs = din("xs", [NS, D])
    sconv = din("sconv", [NS, 3, D])
    slru = din("slru", [NS, D])
    ck = din("ck", [NS, 128, DKV])
    cv = din("cv", [NS, 128, DKV])
    w_in = din("w_in", [D, NCOLS])
    w_or = din("w_or", [D, D])
    w_oa = din("w_oa", [DA, D])
    w_o = din("w_o", [D, D])
    wga = din("wga", [NBLK, 256, 256])
    wgx = din("wgx", [NBLK, 256, 256])
    par = din("par", [128, 8, KC])
    sinks = din("sinks", [H])
    lng = din("lng", [D])
    lnb = din("lnb", [D])
    ident_d = din("ident", [128, 128])
    rmat_d = din("rmat", [128, 128])
    cs_d = din("cs", [128, 2, NT * T + 128 + 1])
    masks_d = din("masks", [128, 3, 128])
    cmask_d = din("cmask", [128, NCR])
    yp = dout("yp", [NT * T, D])
    ys = dout("ys", [NS, D])
    convp = dout("convp", [3, D])
    lrup = dout("lrup", [1, D])
    kwp = dout("kwp", [128, DKV])
    vwp = dout("vwp", [128, DKV])
    convs = dout("convs", [NS, 3, D])
    lrus = dout("lrus", [NS, D])
    kws = dout("kws", [NS, 128, DKV])
    vws = dout("vws", [NS, 128, DKV])
    cc_in = nc.dram_tensor("cc_in", [128, 2 * KC], F32).ap()
    cc_out = nc.dram_tensor("cc_out", [128 * NCR, 2 * KC], F32).ap()

    base = [16512]
    LIMIT = 229344

    def sb(name, shape, dt, at=None):
        nb = int(np.prod(shape[1:])) * (2 if dt == BF16 else 4)
        nb = (nb + 31) // 32 * 32
        if at is None:
            off = base[0]
            base[0] += nb
        else:
            off = at
        assert off + nb <= LIMIT, (name, off, nb)
        return nc.alloc_sbuf_tensor_at(name, list(shape), dt, offset=off), off + nb

    ident, _ = sb("ident", [128, 128], F32)
    rmat, _ = sb("rmat", [128, 128], F32)
    ones, _ = sb("ones", [128, 128], BF16)
    zer, _ = sb("zer", [128, 512], BF16)
    parT, _ = sb("parT", [128, 8, KC], F32)
    nls, _ = sb("nls", [128, 2, KC], F32)
    nbias, _ = sb("nbias", [128, 2, KC], F32)
    hnls, _ = sb("hnls", [128, 2, KC], F32)
    esink, _ = sb("esink", [128, H], F32)
    cmask, _ = sb("cmask", [128, NCR], F32)
    maskt, _ = sb("maskt", [128, 2, 512], BF16)
    maskf, _ = sb("maskf", [128, 3, 128], F32)
    hcar, _ = sb("hcar", [128, KC], F32)
    hin, _ = sb("hin", [128, KC], F32)
    ucar, _ = sb("ucar", [128, KC, 3], F32)
    rsum, _ = sb("rsum", [128, KC, NT], F32)
    pend, _ = sb("pend", [128, 2, KC], F32)
    small, _ = sb("small", [128, 64], F32)
    tst, _ = sb("tst", [128, 512], F32)
    if USE_CC:
        gath, _ = sb("gath", [128, NCR, 2 * KC], F32)
        vm = None
    else:
        vm, _ = sb("vm", [128, T], F32)
    kcar, _ = sb("kcar", [128, KV, 128], BF16)
    vcar, _ = sb("vcar", [128, DKV], BF16)
    NSLOT = 6
    ring = [sb("ring%d" % i, [128, KT, 512], BF16)[0] for i in range(NSLOT)]
    mgT, _ = sb("mgT", [128, KC, TS], BF16)
    R0 = base[0]
    xT, _ = sb("xT", [128, KC, TH], BF16)
    yoT, R1 = sb("yoT", [128, KM, TS], BF16)
    base[0] = R1
    sm, _ = sb("sm", [128, 4, TS], F32)
    base[0] = R1
    uf, _ = sb("uf", [128, 4, 3 + TS], F32)
    xc, _ = sb("xc", [128, 4, TS], F32)
    xcb, _ = sb("xcb", [128, 4, TS], BF16)
    tmp = [sb("tmp%d" % i, [128, TS], F32)[0] for i in range(6)]
    hT_off = base[0]
    hT, _ = sb("hT", [128, 4, TS], F32)
    xin = None
    sgb, _ = sb("sgb", [128, 4, TS], BF16)
    gw = [sb("gw%d" % i, [128, 2, 2, 256], BF16)[0] for i in range(2)]
    sT_off = base[0]
    sT, _ = sb("sT", [128, KC, 3, NS], F32)
    h0T, _ = sb("h0T", [128, KC, NS], F32)
    if base[0] - sT_off >= 4 * 2048:
        NXIN = 4
        xin = [sb("xin%d" % i, [128, 512], F32, at=sT_off + i * 2048)[0] for i in range(NXIN)]
    else:
        NXIN = 1
        xin = [sb("xin0", [128, 512], F32)[0]]
    rnn_end = base[0]
    base[0] = R1
    kT, _ = sb("kT", [128, KV, TH], BF16)
    Vt, _ = sb("Vt", [128, NB + 1, DKV], BF16)
    qf_off = base[0]
    qf = [sb("qf%d" % i, [128, 512], F32)[0] for i in range(2)]
    rtmp, _ = sb("rtmp", [128, 512], F32)
    qT, _ = sb("qT", [128, G, T], BF16)
    qTs, _ = sb("qTs", [128, H, NS], BF16)
    sgT, _ = sb("sgT", [128, G, T], F32)
    sgs, _ = sb("sgs", [128, H, NS], F32)
    pT = [sb("pT%d" % i, [128, 512], BF16)[0] for i in range(2)]
    rden, _ = sb("rden", [128, 256], F32)
    otmp, _ = sb("otmp", [128, 256], F32)
    kf32, _ = sb("kf32", [128, KV, 128], F32)
    ksf, _ = sb("ksf", [128, KV, NS], F32)
    win32, _ = sb("win32", [128, 512], F32)
    assert DKV * 4 <= 2 * 2048
    kwin, _ = sb("kwin", [128, DKV], F32, at=qf_off)
    vwinb, _ = sb("vwinb", [128, DKV], BF16)
    kwT, _ = sb("kwT", [128, 128], BF16)
    pTs, _ = sb("pTs", [128, H], BF16)
    sa1, _ = sb("sa1", [128, H], F32)
    sa2, _ = sb("sa2", [128, H], F32)
    cst, _ = sb("cst", [128, 2, T + 128 + 1], F32)
    att_end = base[0]
    base[0] = R0
    res, _ = sb("res", [128, NB + 1, D], F32)
    lnG, _ = sb("lnG", [128, D], F32)
    lnB, _ = sb("lnB", [128, D], F32)
    NXR = 6
    xr = [sb("xr%d" % i, [128, 512], F32)[0] for i in range(NXR)]
    junk, _ = sb("junk", [128, 512], F32)
    s1, _ = sb("s1", [128, NB + 1, D // 512], F32)
    s2, _ = sb("s2", [128, NB + 1, D // 512], F32)
    st, _ = sb("st", [128, 8, NB + 1], F32)
    fin_end = base[0]

    A = [nc.alloc_psum_tensor("A%d" % i, [128, 512], F32) for i in range(4)]
    X = nc.alloc_psum_tensor("X", [128, 512], F32)
    S0 = nc.alloc_psum_tensor("S0", [128, 512], F32)
    S1 = nc.alloc_psum_tensor("S1", [128, 512], F32)
    OD = nc.alloc_psum_tensor("OD", [128, 512], F32)

    def dma(q, out, in_, sem, reads=(), writes=()):
        return S.op(q, lambda E: E.dma_start(out=out, in_=in_), reads=reads, writes=writes, dma=sem)

    def act(out, in_, func, reads, writes, **kw):
        return S.op('act', lambda E: E.activation(out=out, in_=in_, func=func, **kw), reads=reads, writes=writes)

    def dve(fn, reads, writes):
        return S.op('dve', fn, reads=reads, writes=writes)

    def pool(fn, reads, writes):
        return S.op('pool', fn, reads=reads, writes=writes)

    ring_i = [0]

    def next_slot():
        i = ring_i[0] % NSLOT
        ring_i[0] += 1
        return i

    def stream(wsrc, nk, col0, pe_body, reads, writes, hook=None):
        nkq = nk // KT
        for kq in range(nkq):
            si = next_slot()
            slot = ring[si]
            src = wsrc[kq * KT * 128:(kq + 1) * KT * 128, col0:col0 + 512].rearrange("(k p) c -> p k c", p=128)
            dma('pool', slot[:, :, :], src, 'd_ring%d' % si, writes=['ring%d' % si])
            rd = ['ring%d' % si, 'zer'] + list(reads)
            if kq < nkq - 1 or nkq == 1 and False:
                def body(E, slot=slot, kq=kq):
                    ins = None
                    for kk in range(KT):
                        ins = pe_body(E, slot, kq * KT + kk, kk, nk, None)
                    return ins
                S.op('pe', body, reads=rd, writes=writes)
            else:
                for part in range(pe_body.nparts):
                    def body(E, slot=slot, kq=kq, part=part):
                        ins = None
                        for kk in range(KT):
                            ins = pe_body(E, slot, kq * KT + kk, kk, nk, part)
                        return ins
                    S.op('pe', body, reads=rd, writes=pe_body.part_writes(part, writes))
            if hook is not None:
                hook(kq, nkq)

    def fm_body(segs):
        def body(E, slot, k, kk, nk, part):
            ins = None
            if k == 0 and len(segs) > 1 and part in (None, 0):
                E.matmul(X[:, :], lhsT=zer[:, 0:128], rhs=zer[:, :], start=True, stop=False)
            for c in (range(4) if part is None else [part]):
                for si_, (rhs_fn, out_fn) in enumerate(segs):
                    if si_ == 0:
                        ins = E.matmul(out_fn(c), lhsT=slot[:, kk, c * 128:(c + 1) * 128], rhs=rhs_fn(k),
                                       start=(k == 0), stop=(k == nk - 1))
                    else:
                        ins = E.matmul(out_fn(c), lhsT=slot[:, kk, c * 128:(c + 1) * 128], rhs=rhs_fn(k),
                                       start=False, stop=(k == nk - 1 and c == 3 and si_ == len(segs) - 1))
            return ins
        body.nparts = 4
        body.part_writes = lambda part, writes: ['A%d' % part] + (['X'] if len(segs) > 1 else [])
        return body

    def tm_body(segs, keys=None):
        def body(E, slot, k, kk, nk, part):
            ins = None
            for lhs_fn, out_ap in (segs if part is None else [segs[part]]):
                ins = E.matmul(out_ap, lhsT=lhs_fn(k), rhs=slot[:, kk, :], start=(k == 0), stop=(k == nk - 1))
            return ins
        body.nparts = 1 if keys is None else len(segs)
        body.part_writes = (lambda part, writes: writes) if keys is None else (lambda part, writes: keys[part])
        if keys is None:
            inner = body

            def body2(E, slot, k, kk, nk, part):
                return inner(E, slot, k, kk, nk, None)
            body2.nparts = 1
            body2.part_writes = lambda part, writes: writes
            return body2
        return body

    def tstore(srcs, n, dst, reads, key):
        m = len(srcs)

        def body(E):
            ins = None
            for i, s_ in enumerate(srcs):
                ins = E.transpose(OD[0:n, 256 + i * 0:256 + i * 0 + 0] if False else X2[0:n, i * 128:(i + 1) * 128], s_, ident[:, :])
            return ins
        S.op('pe', body, reads=list(reads) + ['ident'], writes=['S1'])
        act(tst[0:n, 0:128 * m], S1[0:n, 0:128 * m], AF.Copy, reads=['S1'], writes=['tst'])
        dma('sp', dst, tst[0:n, 0:128 * m], 'd_tst', reads=['tst'])
    X2 = S1

    dma('sp', ident[:, :], ident_d, 'd_c0', writes=['ident'])
    dma('sp', rmat[:, :], rmat_d, 'd_c1', writes=['rmat'])
    dma('sp', parT[:, :, :], par, 'd_c2', writes=['parT'])
    dma('sp', esink[:, :], sinks.partition_broadcast(128), 'd_c3', writes=['esink'])
    dma('sp', cmask[:, :], cmask_d, 'd_c4', writes=['cmask'])
    dma('sp', maskf[:, :, :], masks_d, 'd_c5', writes=['maskf'])
    dve(lambda E: E.memset(ones[:, :], 1.0), [], ['ones'])
    dve(lambda E: E.memset(zer[:, :], 0.0), [], ['zer'])
    act(esink[:, :], esink[:, :], AF.Exp, ['esink'], ['esink'])
    for v_, pi in ((0, 0), (1, 2)):
        for hh in range(2):
            dve(lambda E, v_=v_, pi=pi, hh=hh: E.tensor_copy(out=maskt[:, v_, hh * 128:(hh + 1) * 128], in_=maskf[:, pi, :]), ['maskf'], ['maskt'])
            dve(lambda E, v_=v_, hh=hh: E.tensor_copy(out=maskt[:, v_, 256 + hh * 128:256 + (hh + 1) * 128], in_=maskf[:, 1, :]), ['maskf'], ['maskt'])
    dve(lambda E: E.tensor_scalar(out=nbias[:, :, :], in0=parT[:, 5:7, :], scalar1=0.5, scalar2=None, op0=ALU.mult), ['parT'], ['nbias'])
    act(nls[:, 0, :], parT[:, 7, :], AF.Exp, ['parT'], ['nls'], scale=-1.0)
    act(nls[:, 0, :], nls[:, 0, :], AF.Ln, ['nls'], ['nls'], bias=1.0)
    dve(lambda E: E.tensor_scalar(out=nls[:, 1, :], in0=nls[:, 0, :], scalar1=-2.0 * LRU_C, scalar2=None, op0=ALU.mult), ['nls'], ['nls2'])
    dve(lambda E: E.tensor_scalar(out=nls[:, 0, :], in0=nls[:, 0, :], scalar1=-LRU_C, scalar2=None, op0=ALU.mult), ['nls', 'nls2'], ['nls'])
    dve(lambda E: E.tensor_scalar(out=hnls[:, 0, :], in0=nls[:, 0, :], scalar1=0.5, scalar2=None, op0=ALU.mult), ['nls'], ['nls'])
    dve(lambda E: E.tensor_copy(out=hnls[:, 1, :], in_=nls[:, 0, :]), ['nls'], ['nls'])

    def load_xT(row0, nblk, col0, xp=xp):
        li = [0]
        for b in range(nblk):
            for cgp in range(D // 512):
                xi = xin[li[0] % NXIN]
                key = 'xin%d' % (li[0] % NXIN)
                li[0] += 1
                dma('sp', xi[:, :], xp[row0 + b * 128:row0 + (b + 1) * 128, cgp * 512:(cgp + 1) * 512], 'd_' + key, writes=[key])

                def body(E, xi=xi):
                    ins = None
                    for i in range(4):
                        ins = E.transpose(OD[:, i * 128:(i + 1) * 128], xi[:, i * 128:(i + 1) * 128], ident[:, :])
                    return ins
                S.op('pe', body, reads=[key, 'ident'], writes=['OD0', 'OD1'])
                act(xT[:, cgp * 4:(cgp + 1) * 4, col0 + b * 128:col0 + (b + 1) * 128],
                    OD[:, :].rearrange("p (k t) -> p k t", k=4), AF.Copy, ['OD0', 'OD1'], ['xT'])

    def load_sample_states():
        for cgp in range(D // 512):
            for kind in range(3):
                xi = tst
                if kind == 0:
                    n = NS
                    dma('sp', xi[0:NS, :], xs[:, cgp * 512:(cgp + 1) * 512], 'd_tstx', writes=['tst'])
                elif kind == 1:
                    n = NS * 3
                    dma('sp', xi[0:n, :], sconv.rearrange("b j d -> (b j) d")[:, cgp * 512:(cgp + 1) * 512], 'd_tstx', writes=['tst'])
                else:
                    n = NS
                    dma('sp', xi[0:NS, :], slru[:, cgp * 512:(cgp + 1) * 512], 'd_tstx', writes=['tst'])

                def body(E, xi=xi, n=n):
                    ins = None
                    for i in range(4):
                        ins = E.transpose(S0[:, i * 128:i * 128 + n], xi[0:n, i * 128:(i + 1) * 128], ident[0:n, 0:n])
                    return ins
                S.op('pe', body, reads=['tst', 'ident'], writes=['S0'])
                src = S0[:, :].rearrange("p (k t) -> p k t", k=4)[:, :, 0:n]
                if kind == 0:
                    act(xT[:, cgp * 4:(cgp + 1) * 4, T:T + NS], src, AF.Copy, ['S0'], ['xT'])
                elif kind == 1:
                    act(sT[:, cgp * 4:(cgp + 1) * 4, :, :].rearrange("p k j b -> p k b j"),
                        src.rearrange("p k (b j) -> p k b j", j=3), AF.Copy, ['S0'], ['sT'])
                else:
                    act(h0T[:, cgp * 4:(cgp + 1) * 4, :], src, AF.Copy, ['S0'], ['h0T'])

    def rnn_stage(tile, prepass, pre_first=False, tail_fn=None):
        last = (tile == NT - 1)
        samp = last and not prepass
        ncol = TS if samp else T
        NG = D // 512
        use_x_halo = (tile == 0) and (not prepass or USE_CC)
        deferred = []

        def pop_hook(kq, nkq):
            if deferred:
                deferred.pop(0)()

        def proj_u(gi, hook=None):
            segs = [(lambda k: xT[:, k, 0:T], lambda c: A[c][:, 0:T])]
            if use_x_halo:
                segs.append((lambda k: xT[:, k, T + 125:T + 128], lambda c: X[:, c * 128:c * 128 + 3]))
            if samp:
                segs.append((lambda k: xT[:, k, T:T + NS], lambda c: X[:, c * 128:c * 128 + NS]))
            stream(w_in, KC, OFF_U + gi * 512, fm_body(segs), ['xT'], ['A0', 'A1', 'A2', 'A3', 'X'], hook=hook)

        def phase1(gi):
            phase1a(gi)
            phase1b(gi)

        def phase1a(gi):
            ch0 = gi * 4
            for c in range(4):
                act(uf[:, c, 3:3 + T], A[c][:, 0:T], AF.Copy, ['A%d' % c], ['uf%d' % c])
            for c in range(4):
                ch = ch0 + c
                if use_x_halo:
                    act(uf[:, c, 0:3], X[:, c * 128:c * 128 + 3], AF.Copy, ['X'], ['uf%d' % c])
                elif prepass and pre_first:
                    dve(lambda E, c=c: E.memset(uf[:, c, 0:3], 0.0), [], ['uf%d' % c])
                else:
                    act(uf[:, c, 0:3], ucar[:, ch, :], AF.Copy, ['ucar'], ['uf%d' % c])
                if samp:
                    act(uf[:, c, 3 + T:3 + TS], X[:, c * 128:c * 128 + NS], AF.Copy, ['X'], ['uf%d' % c])

        def phase1b(gi):
            ch0 = gi * 4
            for c in range(4):
                ch = ch0 + c
                dve(lambda E, c=c, ch=ch: E.tensor_scalar(out=xc[:, c, 0:T], in0=uf[:, c, 0:T], scalar1=parT[:, 0, ch:ch + 1],
                                                          scalar2=parT[:, 4, ch:ch + 1], op0=ALU.mult, op1=ALU.add),
                    ['uf%d' % c, 'parT'], ['xc%d' % c])
                for j in range(1, 4):
                    dve(lambda E, c=c, ch=ch, j=j: E.scalar_tensor_tensor(out=xc[:, c, 0:T], in0=uf[:, c, j:j + T], scalar=parT[:, j, ch:ch + 1],
                                                                          in1=xc[:, c, 0:T], op0=ALU.mult, op1=ALU.add),
                        ['uf%d' % c, 'parT', 'xc%d' % c], ['xc%d' % c])
                if samp:
                    dve(lambda E, c=c, ch=ch: E.tensor_scalar(out=xc[:, c, T:TS], in0=uf[:, c, 3 + T:3 + TS], scalar1=parT[:, 3, ch:ch + 1],
                                                              scalar2=parT[:, 4, ch:ch + 1], op0=ALU.mult, op1=ALU.add),
                        ['uf%d' % c, 'parT', 'xc%d' % c], ['xc%d' % c])
                    for j in range(3):
                        dve(lambda E, c=c, ch=ch, j=j: E.scalar_tensor_tensor(out=xc[:, c, T:TS], in0=sT[:, ch, j, :], scalar=parT[:, j, ch:ch + 1],
                                                                              in1=xc[:, c, T:TS], op0=ALU.mult, op1=ALU.add),
                            ['sT', 'parT', 'xc%d' % c], ['xc%d' % c])
            for c in range(4):
                ch = ch0 + c
                act(xcb[:, c, 0:ncol], xc[:, c, 0:ncol], AF.Copy, ['xc%d' % c], ['xcb%d' % c])
                act(ucar[:, ch, :], uf[:, c, T:T + 3], AF.Copy, ['uf%d' % c], ['ucar'])
            if samp:
                deferred.append(lambda gi=gi: tstore([uf[:, c, 3 + T:3 + TS] for c in range(4)], NS, convs[:, 2, gi * 512:(gi + 1) * 512], ['uf%d' % c for c in range(4)], 'x'))
            if last and not prepass:
                deferred.append(lambda gi=gi: tstore([uf[:, c, T:T + 3] for c in range(4)], 3, convp[:, gi * 512:(gi + 1) * 512], ['uf%d' % c for c in range(4)], 'x'))

        def proj_g(gi):
            segs = [(lambda k: xT[:, k, 0:T], lambda c: A[c][:, 0:T])]
            if samp:
                segs.append((lambda k: xT[:, k, T:T + NS], lambda c: X[:, c * 128:c * 128 + NS]))
            stream(w_in, KC, OFF_G + gi * 512, fm_body(segs), ['xT'], ['A0', 'A1', 'A2', 'A3', 'X'], hook=pop_hook)
            while deferred:
                deferred.pop(0)()
            for c in range(4):
                srcs = [(A[c][:, 0:T], 0, T, 'A%d' % c)]
                if samp:
                    srcs.append((X[:, c * 128:c * 128 + NS], T, TS, 'X'))
                for src, a0, a1, key in srcs:
                    act(tmp[3][:, a0:a1], src, AF.Tanh, [key], ['t4'], scale=0.5)
                    dve(lambda E, c=c, src=src, a0=a0, a1=a1: E.scalar_tensor_tensor(out=sgb[:, c, a0:a1], in0=tmp[3][:, a0:a1], scalar=1.0, in1=src,
                                                                                     op0=ALU.add, op1=ALU.mult), [key, 't4'], ['sgb%d' % c])

        def phase2_pair(gi, bi):
            ch0 = gi * 4
            blk = gi * 2 + bi
            g_ = gw[blk % 2]
            gk = 'gw%d' % (blk % 2)
            dma('pool', g_[:, 0, :, :], wga[blk].rearrange("(dc p) e -> p dc e", p=128), 'd_' + gk + 'a', writes=[gk + 'a'])
            dma('pool', g_[:, 1, :, :], wgx[blk].rearrange("(dc p) e -> p dc e", p=128), 'd_' + gk + 'x', writes=[gk + 'x'])
            t1 = tmp[0]
            bufs = {}
            for e in range(2):
                c = bi * 2 + e
                ch = ch0 + c

                def gbody(E, g_=g_, bi=bi, e=e):
                    ins = None
                    for ax, bank in ((0, S0), (1, S1)):
                        for dc in range(2):
                            ins = E.matmul(bank[:, 0:T], lhsT=g_[:, ax, dc, e * 128:(e + 1) * 128], rhs=xcb[:, bi * 2 + dc, 0:T],
                                           start=(dc == 0), stop=(dc == 1))
                    if samp:
                        for ax in range(2):
                            for dc in range(2):
                                ins = E.matmul(OD[:, ax * 128:ax * 128 + NS], lhsT=g_[:, ax, dc, e * 128:(e + 1) * 128],
                                               rhs=xcb[:, bi * 2 + dc, T:TS], start=(dc == 0), stop=(dc == 1))
                    return ins
                S.op('pe', gbody, reads=[gk + 'a', gk + 'x', 'xcb%d' % (bi * 2), 'xcb%d' % (bi * 2 + 1)], writes=['S0', 'S1', 'OD0'])
                t2, t3 = (tmp[1], tmp[2]) if e == 0 else (tmp[4], tmp[5])
                k2, k3 = ('t2', 't3') if e == 0 else ('t2b', 't3b')
                bufs[e] = (t2, t3, k2, k3)
                hk = 'hT%d' % c
                act(t1[:, 0:T], S0[:, 0:T], AF.Tanh, ['S0', 'nbias'], ['t1'], scale=0.5, bias=nbias[:, 0, ch:ch + 1])
                act(t2[:, 0:T], S1[:, 0:T], AF.Tanh, ['S1', 'nbias'], [k2], scale=0.5, bias=nbias[:, 1, ch:ch + 1])
                if samp:
                    act(t1[:, T:TS], OD[:, 0:NS], AF.Tanh, ['OD0', 'nbias'], ['t1'], scale=0.5, bias=nbias[:, 0, ch:ch + 1])
                    act(t2[:, T:TS], OD[:, 128:128 + NS], AF.Tanh, ['OD0', 'nbias'], [k2], scale=0.5, bias=nbias[:, 1, ch:ch + 1])
                act(t3[:, 0:ncol], t1[:, 0:ncol], AF.Exp, ['t1', 'nls'], [k3], scale=hnls[:, 0, ch:ch + 1], bias=hnls[:, 0, ch:ch + 1])
                act(hT[:, c, 0:ncol], t1[:, 0:ncol], AF.Exp, ['t1', 'nls'], [hk], scale=hnls[:, 1, ch:ch + 1], bias=hnls[:, 1, ch:ch + 1])
            for e in range(2):
                c = bi * 2 + e
                ch = ch0 + c
                t2, t3, k2, k3 = bufs[e]
                hk = 'hT%d' % c
                act(hT[:, c, 0:ncol], hT[:, c, 0:ncol], AF.Sqrt, [hk], [hk], scale=-1.0, bias=1.0)
                dve(lambda E, t2=t2, c=c: E.scalar_tensor_tensor(out=t2[:, 0:ncol], in0=t2[:, 0:ncol], scalar=1.0, in1=hT[:, c, 0:ncol], op0=ALU.add, op1=ALU.mult), [k2, hk], [k2])
                dve(lambda E, t2=t2, c=c: E.scalar_tensor_tensor(out=t2[:, 0:ncol], in0=t2[:, 0:ncol], scalar=0.5, in1=xc[:, c, 0:ncol], op0=ALU.mult, op1=ALU.mult),
                    [k2, 'xc%d' % c], [k2])
                hkey = 'hin' if (tile == 0 and not prepass) else 'hcar'
                if prepass and USE_CC:
                    init = 0.0 if tile == 0 else hcar[:, ch:ch + 1]
                elif prepass:
                    init = 0.0 if pre_first else hcar[:, ch:ch + 1]
                    dve(lambda E, t2=t2: E.tensor_tensor(out=t2[:, 0:T], in0=t2[:, 0:T], in1=vm[:, 0:T], op=ALU.mult), [k2, 'vm'], [k2])
                elif tile == 0:
                    init = hin[:, ch:ch + 1]
                else:
                    init = hcar[:, ch:ch + 1]
                dve(lambda E, t2=t2, t3=t3, c=c, init=init: E.tensor_tensor_scan(out=hT[:, c, 0:T], data0=t3[:, 0:T], data1=t2[:, 0:T], initial=init,
                                                                                 op0=ALU.mult, op1=ALU.add), [k3, k2, hkey], [hk])
                dve(lambda E, c=c, ch=ch: E.tensor_copy(out=hcar[:, ch:ch + 1], in_=hT[:, c, T - 1:T]), [hk], ['hcar'])
                if samp:
                    dve(lambda E, t3=t3, c=c, ch=ch: E.tensor_tensor(out=hT[:, c, T:TS], in0=t3[:, T:TS], in1=h0T[:, ch, :], op=ALU.mult), [k3, 'h0T'], [hk])
                    dve(lambda E, t2=t2, c=c: E.tensor_tensor(out=hT[:, c, T:TS], in0=hT[:, c, T:TS], in1=t2[:, T:TS], op=ALU.add), [k2, hk], [hk])

        def phase3(gi):
            ch0 = gi * 4
            if samp:
                deferred.append(lambda gi=gi: tstore([hT[:, c, T:TS] for c in range(4)], NS, lrus[:, gi * 512:(gi + 1) * 512], ['hT%d' % c for c in range(4)], 'x'))
            if last and not prepass:
                deferred.append(lambda gi=gi: tstore([hT[:, c, T - 1:T] for c in range(4)], 1, lrup[:, gi * 512:(gi + 1) * 512], ['hT%d' % c for c in range(4)], 'x'))
            if not prepass:
                for c in range(4):
                    ch = ch0 + c
                    dve(lambda E, c=c, ch=ch: E.scalar_tensor_tensor(out=yoT[:, ch, 0:ncol], in0=hT[:, c, 0:ncol], scalar=0.5, in1=sgb[:, c, 0:ncol], op0=ALU.mult, op1=ALU.mult),
                        ['hT%d' % c, 'sgb%d' % c], ['yoT'])

        proj_u(0)
        phase1(0)
        for gi in range(NG):
            if not prepass:
                proj_g(gi)
            if gi + 1 < NG:
                done = []

                def hook(kq, nkq, gi=gi, done=done):
                    if not done and 2 * (kq + 1) // nkq >= 1:
                        phase2_pair(gi, 0)
                        done.append(1)
                proj_u(gi + 1, hook=hook)
                phase1a(gi + 1)
                phase2_pair(gi, 1)
                phase3(gi)
                phase1b(gi + 1)
            else:
                phase2_pair(gi, 0)
                if tail_fn is not None:
                    tail_fn()
                phase2_pair(gi, 1)
                phase3(gi)
        while deferred:
            deferred.pop(0)()

    def merge(tile, part):
        samp = (tile == NT - 1)
        ncol = TS if samp else T
        for mg in range(D // 512):
            segs = [(lambda k: xT[:, k, 0:T], lambda c: A[c][:, 0:T])]
            if samp:
                segs.append((lambda k: xT[:, k, T:T + NS], lambda c: X[:, c * 128:c * 128 + NS]))
            stream(w_in, KC, (OFF_MR if part == 1 else OFF_MA) + mg * 512, fm_body(segs), ['xT'], ['A0', 'A1', 'A2', 'A3', 'X'])
            for c in range(4):
                act(sm[:, c, 0:T], A[c][:, 0:T], AF.Tanh, ['A%d' % c], ['sm%d' % c], scale=0.5)
                if samp:
                    act(sm[:, c, T:TS], X[:, c * 128:c * 128 + NS], AF.Tanh, ['X'], ['sm%d' % c], scale=0.5)
                dve(lambda E, c=c: E.tensor_scalar(out=sm[:, c, 0:ncol], in0=sm[:, c, 0:ncol], scalar1=0.5, scalar2=0.5, op0=ALU.mult, op1=ALU.add), ['sm%d' % c], ['sm%d' % c])
            segs = [(lambda k: yoT[:, k, 0:T], lambda c: A[c][:, 0:T])]
            if samp:
                segs.append((lambda k: yoT[:, k, T:T + NS], lambda c: X[:, c * 128:c * 128 + NS]))
            stream(w_or if part == 1 else w_oa, KC if part == 1 else KA, mg * 512, fm_body(segs), ['yoT'], ['A0', 'A1', 'A2', 'A3', 'X'])
            for c in range(4):
                ch = mg * 4 + c
                srcs = [(A[c][:, 0:T], 0, T, 'A%d' % c)]
                if samp:
                    srcs.append((X[:, c * 128:c * 128 + NS], T, TS, 'X'))
                for src, a0, a1, key in srcs:
                    if part == 1:
                        dve(lambda E, c=c, ch=ch, src=src, a0=a0, a1=a1: E.tensor_tensor(out=mgT[:, ch, a0:a1], in0=src, in1=sm[:, c, a0:a1], op=ALU.mult),
                            [key, 'sm%d' % c], ['mgT'])
                    else:
                        dve(lambda E, c=c, src=src, a0=a0, a1=a1: E.tensor_tensor(out=sm[:, c, a0:a1], in0=src, in1=sm[:, c, a0:a1], op=ALU.mult),
                            [key, 'sm%d' % c], ['sm%d' % c])
                        dve(lambda E, c=c, ch=ch, a0=a0, a1=a1: E.tensor_tensor(out=mgT[:, ch, a0:a1], in0=mgT[:, ch, a0:a1], in1=sm[:, c, a0:a1], op=ALU.add),
                            ['sm%d' % c, 'mgT'], ['mgT'])

    def rope(src, n, cosap, sinap, outb, out32, skey, okeys, qi):
        q_ = qf[qi]
        qk = 'qf%d' % qi
        act(q_[:, 0:n], src, AF.Copy, [skey], [qk])
        S.op('pe', lambda E: E.matmul(src, lhsT=rmat[:, :], rhs=q_[:, 0:n], start=True, stop=True), reads=['rmat', qk], writes=[skey])
        dve(lambda E: E.tensor_tensor(out=rtmp[:, 0:n], in0=src, in1=sinap, op=ALU.mult), [skey, 'cs'], ['rtmp'])
        dve(lambda E: E.tensor_tensor(out=q_[:, 0:n], in0=q_[:, 0:n], in1=cosap, op=ALU.mult), [qk, 'cs'], [qk])
        if out32 is not None:
            dve(lambda E: E.tensor_tensor(out=out32, in0=q_[:, 0:n], in1=rtmp[:, 0:n], op=ALU.add), [qk, 'rtmp'], okeys)
        if outb is not None:
            dve(lambda E: E.tensor_tensor(out=outb, in0=q_[:, 0:n], in1=rtmp[:, 0:n], op=ALU.add), [qk, 'rtmp'], okeys)

    def attn_stage(tile, cst):
        last = (tile == NT - 1)
        samp = last
        scale = 128.0 ** -0.5
        cosm, sinm = cst[:, 0, 0:T], cst[:, 1, 0:T]
        if tile > 0:
            dve(lambda E: E.tensor_copy(out=kT[:, :, 0:128], in_=kcar[:, :, :]), ['kcar'], ['kT'])
            dve(lambda E: E.tensor_copy(out=Vt[:, 0, :], in_=vcar[:, :]), ['vcar'], ['Vt'])
        for cg in range(DKV // 512):
            segs = [((lambda k, tb=tb: xT[:, k, tb * 128:(tb + 1) * 128]), A[tb][:, :]) for tb in range(NB)]
            wr = ['A%d' % tb for tb in range(NB)]
            if tile == 0:
                segs.append((lambda k: xT[:, k, T:T + 128], OD[:, :]))
                wr += ['OD0', 'OD1']
            if samp:
                segs.append((lambda k: xT[:, k, T:T + NS], X[0:NS, :]))
                wr.append('X')
            stream(w_in, KC, OFF_V + cg * 512, tm_body(segs), ['xT'], wr)
            for tb in range(NB):
                act(Vt[:, tb + 1, cg * 512:(cg + 1) * 512], A[tb][:, :], AF.Copy, ['A%d' % tb], ['Vt'])
            if tile == 0:
                act(Vt[:, 0, cg * 512:(cg + 1) * 512], OD[:, :], AF.Copy, ['OD0', 'OD1'], ['Vt'])
            if last:
                act(win32[:, :], A[NB - 1][:, :], AF.Copy, ['A%d' % (NB - 1)], ['win32'])
                dma('sp', vwp[:, cg * 512:(cg + 1) * 512], win32[:, :], 'd_win32', reads=['win32'])
            if samp:
                act(tst[0:NS, :], X[0:NS, :], AF.Copy, ['X'], ['tst'])
                dma('sp', vws[:, 127, cg * 512:(cg + 1) * 512], tst[0:NS, :], 'd_tst', reads=['tst'], writes=['vws_new'])
        for cg in range(DKV // 512):
            segs = [(lambda k: xT[:, k, 0:T], lambda c: A[c][:, 0:T])]
            if tile == 0:
                segs.append((lambda k: xT[:, k, T:T + 128], lambda c: X[:, c * 128:(c + 1) * 128]))
            if samp:
                segs.append((lambda k: xT[:, k, T:T + NS], lambda c: X[:, c * 128:c * 128 + NS]))
            stream(w_in, KC, OFF_K + cg * 512, fm_body(segs), ['xT'], ['A0', 'A1', 'A2', 'A3', 'X'])
            for c in range(4):
                j = cg * 4 + c
                rope(A[c][:, 0:T], T, cosm, sinm, kT[:, j, 128:128 + T], None, 'A%d' % c, ['kT'], c % 2)
                if last:
                    dve(lambda E, j=j, c=c: E.tensor_tensor(out=kf32[:, j, :], in0=qf[c % 2][:, T - 128:T], in1=rtmp[:, T - 128:T], op=ALU.add),
                        ['qf%d' % (c % 2), 'rtmp'], ['kf32'])
                if tile == 0:
                    rope(X[:, c * 128:(c + 1) * 128], 128, cst[:, 0, T:T + 128], cst[:, 1, T:T + 128], kT[:, j, 0:128], None, 'X', ['kT'], c % 2)
                if samp:
                    rope(X[:, c * 128:c * 128 + NS], NS, cst[:, 0, T + 128:T + 129].broadcast_to([128, NS]), cst[:, 1, T + 128:T + 129].broadcast_to([128, NS]),
                         None, ksf[:, j, :], 'X', ['ksf'], c % 2)
            if last:
                tstore([kf32[:, cg * 4 + c, :] for c in range(4)], 128, kwp[:, cg * 512:(cg + 1) * 512], ['kf32'], 'x')
            if samp:
                tstore([ksf[:, cg * 4 + c, :] for c in range(4)], NS, kws[:, 127, cg * 512:(cg + 1) * 512], ['ksf'], 'x')
        if samp:
            S.readers.setdefault('tstdone', [])
        ui = [0]

        def unit_S(j, tb, hp):
            u = ui[0]
            ui[0] += 1
            Sb = S0 if u % 2 == 0 else S1
            sk = 'S0' if u % 2 == 0 else 'S1'
            p_ = pT[u % 2]
            pk = 'pT%d' % (u % 2)

            def sbody(E, Sb=Sb, j=j, tb=tb, hp=hp):
                rhs = qT[:, 2 * hp:2 * hp + 2, tb * 128:(tb + 1) * 128]
                E.matmul(Sb[:, 0:256].rearrange("p (h q) -> p h q", h=2), lhsT=kT[:, j, tb * 128:(tb + 1) * 128], rhs=rhs, start=True, stop=True)
                return E.matmul(Sb[:, 256:512].rearrange("p (h q) -> p h q", h=2), lhsT=kT[:, j, (tb + 1) * 128:(tb + 2) * 128], rhs=rhs, start=True, stop=True)
            S.op('pe', sbody, reads=['kT', 'qT'], writes=[sk])
            act(p_[:, :], Sb[:, :], AF.Exp, [sk], [pk], scale=scale)
            mv = 1 if (tile == 0 and tb == 0) else 0
            dve(lambda E, p_=p_, mv=mv: E.tensor_tensor(out=p_[:, :], in0=p_[:, :], in1=maskt[:, mv, :], op=ALU.mult), [pk, 'maskt'], [pk])
            return (j, tb, hp, p_, pk)

        def unit_PV(st_):
            j, tb, hp, p_, pk = st_

            def pvbody(E, p_=p_, j=j, tb=tb):
                E.matmul(OD[:, 0:256], lhsT=Vt[:, tb, j * 128:(j + 1) * 128], rhs=p_[:, 0:256], start=True, stop=False)
                E.matmul(OD[:, 0:256], lhsT=Vt[:, tb + 1, j * 128:(j + 1) * 128], rhs=p_[:, 256:512], start=False, stop=True)
                E.matmul(OD[:, 256:512], lhsT=ones[:, :], rhs=p_[:, 0:256], start=True, stop=False)
                return E.matmul(OD[:, 256:512], lhsT=ones[:, :], rhs=p_[:, 256:512], start=False, stop=True)
            S.op('pe', pvbody, reads=['Vt', pk, 'ones'], writes=['OD0', 'OD1'])
            for hh in range(2):
                hd = j * 4 + 2 * hp + hh
                dve(lambda E, hh=hh, hd=hd: E.tensor_scalar(out=rden[:, hh * 128:(hh + 1) * 128], in0=OD[:, 256 + hh * 128:256 + (hh + 1) * 128],
                                                            scalar1=esink[:, hd:hd + 1], scalar2=None, op0=ALU.add), ['OD1', 'esink'], ['rden'])
            dve(lambda E: E.reciprocal(out=rden[:, :], in_=rden[:, :]), ['rden'], ['rden'])
            dve(lambda E: E.tensor_tensor(out=otmp[:, :], in0=OD[:, 0:256], in1=rden[:, :], op=ALU.mult), ['OD0', 'rden'], ['otmp'])
            dve(lambda E, j=j, tb=tb, hp=hp: E.scalar_tensor_tensor(out=yoT[:, j * 4 + 2 * hp:j * 4 + 2 * hp + 2, tb * 128:(tb + 1) * 128],
                                                              in0=otmp[:, :].rearrange("p (h q) -> p h q", h=2), scalar=0.5,
                                                              in1=sgT[:, 2 * hp:2 * hp + 2, tb * 128:(tb + 1) * 128], op0=ALU.mult, op1=ALU.mult),
                ['otmp', 'sgT'], ['yoT'])

        for j in range(KV + 1):
            todo = [(j - 1, tb, hp) for tb in range(NB) for hp in range(2)] if j > 0 else []
            pend = []

            def ahook(kq, nkq, todo=todo, pend=pend):
                while pend:
                    unit_PV(pend.pop(0))
                n_ = (len(todo) + (nkq - kq) - 1) // (nkq - kq)
                for _ in range(min(n_, 2, len(todo))):
                    pend.append(unit_S(*todo.pop(0)))
            if j < KV:
                segs = [(lambda k: xT[:, k, 0:T], lambda c: A[c][:, 0:T])]
                if samp:
                    segs.append((lambda k: xT[:, k, T:T + NS], lambda c: X[:, c * 128:c * 128 + NS]))
                stream(w_in, KC, OFF_Q + j * 512, fm_body(segs), ['xT'], ['A0', 'A1', 'A2', 'A3', 'X'], hook=ahook)
            while pend or todo:
                while pend:
                    unit_PV(pend.pop(0))
                for _ in range(min(2, len(todo))):
                    pend.append(unit_S(*todo.pop(0)))
            if j == KV:
                break
            for c in range(4):
                rope(A[c][:, 0:T], T, cosm, sinm, qT[:, c, :], None, 'A%d' % c, ['qT'], c % 2)
            if samp:
                for c in range(4):
                    rope(X[:, c * 128:c * 128 + NS], NS, cst[:, 0, T + 128:T + 129].broadcast_to([128, NS]), cst[:, 1, T + 128:T + 129].broadcast_to([128, NS]),
                         qTs[:, j * 4 + c, :], None, 'X', ['qTs'], c % 2)
            stream(w_in, KC, OFF_GA + j * 512, fm_body(segs), ['xT'], ['A0', 'A1', 'A2', 'A3', 'X'])
            for c in range(4):
                act(sgT[:, c, :], A[c][:, 0:T], AF.Tanh, ['A%d' % c], ['sgT'], scale=0.5)
                dve(lambda E, c=c: E.scalar_tensor_tensor(out=sgT[:, c, :], in0=sgT[:, c, :], scalar=1.0, in1=A[c][:, 0:T], op0=ALU.add, op1=ALU.mult), ['sgT', 'A%d' % c], ['sgT'])
                if samp:
                    hh_ = j * 4 + c
                    act(sgs[:, hh_, :], X[:, c * 128:c * 128 + NS], AF.Tanh, ['X'], ['sgs'], scale=0.5)
                    dve(lambda E, hh_=hh_, c=c: E.scalar_tensor_tensor(out=sgs[:, hh_, :], in0=sgs[:, hh_, :], scalar=1.0, in1=X[:, c * 128:c * 128 + NS], op0=ALU.add, op1=ALU.mult), ['sgs', 'X'], ['sgs'])
        if not last:
            dve(lambda E: E.tensor_copy(out=kcar[:, :, :], in_=kT[:, :, T:T + 128]), ['kT'], ['kcar'])
            dve(lambda E: E.tensor_copy(out=vcar[:, :], in_=Vt[:, NB, :]), ['Vt'], ['vcar'])
        if samp:
            dma('sp', kws[:, 0:127, :], ck[:, 1:128, :], 'd_shk', writes=['kws_old'])
            dma('sp', vws[:, 0:127, :], cv[:, 1:128, :], 'd_shv', writes=['vws_old'])
            S.barrier()
            for b in range(NS):
                dma('sp', kwin[:, :], kws[b], 'd_kwin', reads=['kws_old'], writes=['kwin'])
                dma('pool', vwinb[:, :], vws[b], 'd_vwin', reads=['vws_old', 'vws_new'], writes=['vwinb'])
                for j in range(KV):
                    S.op('pe', lambda E, j=j: E.transpose(S0[:, 0:128], kwin[:, j * 128:(j + 1) * 128], ident[:, :]), reads=['kwin', 'ident'], writes=['S0'])
                    act(kwT[:, :], S0[:, 0:128], AF.Copy, ['S0'], ['kwT'])
                    S.op('pe', lambda E, j=j, b=b: E.matmul(S1[:, j * 4:(j + 1) * 4], lhsT=kwT[:, :], rhs=qTs[:, j * 4:(j + 1) * 4, b], start=True, stop=True),
                         reads=['kwT', 'qTs'], writes=['S1'])
                act(pTs[:, :], S1[:, 0:H], AF.Exp, ['S1'], ['pTs'], scale=scale)

                def pvs(E):
                    for j in range(KV):
                        E.matmul(OD[:, j * 4:(j + 1) * 4], lhsT=vwinb[:, j * 128:(j + 1) * 128], rhs=pTs[:, j * 4:(j + 1) * 4], start=True, stop=True)
                    return E.matmul(OD[:, 256:256 + H], lhsT=ones[:, :], rhs=pTs[:, :], start=True, stop=True)
                S.op('pe', pvs, reads=['vwinb', 'pTs', 'ones'], writes=['OD0', 'OD1'])
                dve(lambda E: E.tensor_tensor(out=sa1[:, :], in0=OD[:, 256:256 + H], in1=esink[:, :], op=ALU.add), ['OD1', 'esink'], ['sa1'])
                dve(lambda E: E.reciprocal(out=sa1[:, :], in_=sa1[:, :]), ['sa1'], ['sa1'])
                dve(lambda E: E.tensor_tensor(out=sa2[:, :], in0=OD[:, 0:H], in1=sa1[:, :], op=ALU.mult), ['OD0', 'sa1'], ['sa2'])
                dve(lambda E, b=b: E.scalar_tensor_tensor(out=yoT[:, 0:H, T + b], in0=sa2[:, :], scalar=0.5, in1=sgs[:, :, b], op0=ALU.mult, op1=ALU.mult), ['sa2', 'sgs'], ['yoT'])

    def final_stage(tile):
        samp = (tile == NT - 1)
        NCG = D // 512
        nbx = NB + (1 if samp else 0)
        dma('sp', lnG[:, :], lng.partition_broadcast(128), 'd_lnG', writes=['lnG'])
        dma('sp', lnB[:, :], lnb.partition_broadcast(128), 'd_lnB', writes=['lnB'])
        li = 0
        dve(lambda E: E.memset(s1[:, :, :], 0.0), [], ['s1'])
        dve(lambda E: E.memset(s2[:, :, :], 0.0), [], ['s2'])
        for cg in range(NCG):
            segs = [((lambda k, tb=tb: mgT[:, k, tb * 128:(tb + 1) * 128]), A[tb][:, :]) for tb in range(NB)]
            wr = ['A%d' % tb for tb in range(NB)]
            if samp:
                segs.append((lambda k: mgT[:, k, T:T + NS], X[0:NS, :]))
                wr.append('X')
            stream(w_o, KC, cg * 512, tm_body(segs, keys=[[w_] for w_ in wr]), ['mgT'], wr)
            for tb in range(nbx):
                xi = xr[li % NXR]
                xk = 'xr%d' % (li % NXR)
                li += 1
                if tb < NB:
                    n = 128
                    src = A[tb][:, :]
                    skey = 'A%d' % tb
                    dma('sp', xi[:, :], xp[128 + tile * T + tb * 128:128 + tile * T + (tb + 1) * 128, cg * 512:(cg + 1) * 512], 'd_' + xk, writes=[xk])
                else:
                    n = NS
                    src = X[0:NS, :]
                    skey = 'X'
                    dma('sp', xi[0:NS, :], xs[:, cg * 512:(cg + 1) * 512], 'd_' + xk, writes=[xk])
                dve(lambda E, xi=xi, n=n, src=src, tb=tb, cg=cg: E.scalar_tensor_tensor(out=res[0:n, tb, cg * 512:(cg + 1) * 512], in0=xi[0:n, :], scalar=cfg.ALPHA,
                                                                                  in1=src, op0=ALU.mult, op1=ALU.add, accum_out=s1[0:n, tb, cg:cg + 1]),
                    [xk, skey], ['res%d' % tb, 's1'])
                act(junk[0:n, :], res[0:n, tb, cg * 512:(cg + 1) * 512], AF.Square, ['res%d' % tb], ['junk', 's2'], accum_out=s2[0:n, tb, cg:cg + 1])
        dve(lambda E: E.tensor_reduce(out=st[:, 0, 0:nbx], in_=s1[:, 0:nbx, :], axis=AX.X, op=ALU.add), ['s1'], ['st'])
        dve(lambda E: E.tensor_reduce(out=st[:, 1, 0:nbx], in_=s2[:, 0:nbx, :], axis=AX.X, op=ALU.add), ['s2', 'st'], ['st'])
        dve(lambda E: E.tensor_scalar(out=st[:, 0, 0:nbx], in0=st[:, 0, 0:nbx], scalar1=1.0 / D, scalar2=None, op0=ALU.mult), ['st'], ['st'])
        dve(lambda E: E.tensor_scalar(out=st[:, 1, 0:nbx], in0=st[:, 1, 0:nbx], scalar1=1.0 / D, scalar2=None, op0=ALU.mult), ['st'], ['st'])
        dve(lambda E: E.tensor_tensor(out=st[:, 2, 0:nbx], in0=st[:, 0, 0:nbx], in1=st[:, 0, 0:nbx], op=ALU.mult), ['st'], ['st'])
        dve(lambda E: E.tensor_tensor(out=st[:, 1, 0:nbx], in0=st[:, 1, 0:nbx], in1=st[:, 2, 0:nbx], op=ALU.subtract), ['st'], ['st'])
        dve(lambda E: E.tensor_scalar(out=st[:, 1, 0:nbx], in0=st[:, 1, 0:nbx], scalar1=LN_EPS, scalar2=None, op0=ALU.add), ['st'], ['st'])
        act(st[:, 1, 0:nbx], st[:, 1, 0:nbx], AF.Ln, ['st'], ['st'])
        act(st[:, 1, 0:nbx], st[:, 1, 0:nbx], AF.Exp, ['st'], ['st'], scale=-0.5)
        dve(lambda E: E.tensor_tensor(out=st[:, 3, 0:nbx], in0=st[:, 0, 0:nbx], in1=st[:, 1, 0:nbx], op=ALU.mult), ['st'], ['st'])
        dve(lambda E: E.tensor_scalar(out=st[:, 3, 0:nbx], in0=st[:, 3, 0:nbx], scalar1=-1.0, scalar2=None, op0=ALU.mult), ['st'], ['st'])
        for tb in range(nbx):
            n = 128 if tb < NB else NS
            rk = 'res%d' % tb
            act(res[0:n, tb, :], res[0:n, tb, :], AF.Identity, [rk, 'st'], [rk], scale=st[0:n, 1, tb:tb + 1], bias=st[0:n, 3, tb:tb + 1])
            dve(lambda E, n=n, tb=tb: E.tensor_tensor(out=res[0:n, tb, :], in0=res[0:n, tb, :], in1=lnG[0:n, :], op=ALU.mult), [rk, 'lnG'], [rk])
            dve(lambda E, n=n, tb=tb: E.tensor_tensor(out=res[0:n, tb, :], in0=res[0:n, tb, :], in1=lnB[0:n, :], op=ALU.add), [rk, 'lnB'], [rk])
            if tb < NB:
                dma('sp', yp[tile * T + tb * 128:tile * T + (tb + 1) * 128, :], res[:, tb, :], 'd_' + rk, reads=[rk])
            else:
                dma('sp', ys[:, :], res[0:NS, tb, :], 'd_' + rk, reads=[rk])

    print("SBUF R0", R0, "R1", R1, "rnn_end", rnn_end, "att_end", att_end, "fin_end", fin_end)
    assert max(rnn_end, att_end, fin_end) <= LIMIT

    def load_tile(tile, with_cs):
        load_xT(128 + tile * T, NB, 0)
        if tile == 0:
            load_xT(0, 1, T)

    def load_cs(tile):
        if True:
            dma('sp', cst[:, :, 0:T], cs_d[:, :, 128 + tile * T:128 + (tile + 1) * T], 'd_cs', writes=['cs'])
            dma('sp', cst[:, :, T:T + 128], cs_d[:, :, 0:128], 'd_cs', writes=['cs'])
            S.op('sp', lambda E: E.dma_start(out=cst[:, :, T + 128:T + 129], in_=cs_d[:, :, 128 + NT * T:129 + NT * T], allow_slow_non_contiguous=True),
                 writes=['cs'], dma='d_cs')

    if USE_CC:
        dve(lambda E: E.memset(rsum[:, :, :], 0.0), [], ['rsum'])
        for tile in range(NT):
            load_tile(tile, False)
            S.barrier()
            rnn_stage(tile, True)
            S.barrier()
        dve(lambda E: E.tensor_reduce(out=pend[:, 0, :], in_=rsum[:, :, :], axis=AX.X, op=ALU.add), ['rsum'], ['pend'])
        dve(lambda E: E.tensor_tensor(out=pend[:, 0, :], in0=pend[:, 0, :], in1=nls[:, 0, :], op=ALU.mult), ['pend', 'nls'], ['pend'])
        act(pend[:, 0, :], pend[:, 0, :], AF.Exp, ['pend'], ['pend'])
        act(pend[:, 1, :], hcar[:, :], AF.Copy, ['hcar', 'pend'], ['pend'])
        dma('sp', cc_in, pend[:, :, :].rearrange("p a k -> p (a k)"), 'd_cc', reads=['pend'], writes=['cc_in'])
        S.barrier()
        S.op('pool', lambda E: E.collective_compute("AllGather", ALU.bypass, replica_groups=[list(range(NCR))], ins=[cc_in], outs=[cc_out]),
             reads=['cc_in'], writes=['cc_out'], dma='d_ccg', dma_inc=1)
        dma('sp', gath[:, :, :], cc_out.rearrange("(r p) f -> p r f", p=128), 'd_cc2', reads=['cc_out'], writes=['gath'])
        dve(lambda E: E.memset(hin[:, :], 0.0), [], ['hin'])
        for r in range(NCR):
            dve(lambda E, r=r: E.tensor_tensor(out=small[:, 0:KC], in0=gath[:, r, 0:KC], in1=hin[:, :], op=ALU.mult), ['gath', 'hin'], ['small'])
            dve(lambda E, r=r: E.tensor_tensor(out=small[:, 0:KC], in0=small[:, 0:KC], in1=gath[:, r, KC:2 * KC], op=ALU.add), ['gath', 'small'], ['small'])
            dve(lambda E: E.tensor_tensor(out=small[:, 0:KC], in0=small[:, 0:KC], in1=hin[:, :], op=ALU.subtract), ['hin', 'small'], ['small'])
            dve(lambda E, r=r: E.scalar_tensor_tensor(out=hin[:, :], in0=small[:, 0:KC], scalar=cmask[:, r:r + 1], in1=hin[:, :], op0=ALU.mult, op1=ALU.add),
                ['small', 'cmask', 'hin'], ['hin'])
    else:
        dve(lambda E: E.memset(hcar[:, :], 0.0), [], ['hcar'])
        load_xT(0, NB, 0, xp=xpre)
        for pt in range(NPRE):
            dma('sp', vm[:, :], vmask_d[:, pt * T:(pt + 1) * T], 'd_vm', writes=['vm'])
            if pt + 1 < NPRE:
                nxt = (lambda pt=pt: load_xT((pt + 1) * T, NB, 0, xp=xpre))
            else:
                nxt = (lambda: load_tile(0, True))
            rnn_stage(0, True, pre_first=(pt == 0), tail_fn=nxt)
        act(hin[:, :], hcar[:, :], AF.Copy, ['hcar'], ['hin'])
    dma('sp', convs[:, 0:2, :], sconv[:, 1:3, :], 'd_cvs')
    S.barrier()
    for tile in range(NT):
        if tile > 0 or USE_CC or NPRE == 0:
            load_tile(tile, True)
        if tile == NT - 1:
            S.barrier()
            load_sample_states()
        rnn_stage(tile, False)
        S.barrier()
        merge(tile, 1)
        S.barrier()
        load_cs(tile)
        attn_stage(tile, cst)
        S.barrier()
        merge(tile, 2)
        S.barrier()
        final_stage(tile)
        S.barrier()
    S.barrier()
    S.emit()
    return nc


_CACHE = {}


def _consts(cfg, core):
    T, NT = cfg.T, cfg.NT
    seqi, q = divmod(core, cfg.CPS)
    ident = np.eye(128, dtype=np.float32)
    rm = np.zeros((128, 128), np.float32)
    for d in range(64):
        rm[d + 64, d] = -1.0
        rm[d, d + 64] = 1.0
    half = 64
    inv = (ROPE_THETA ** (-np.arange(half, dtype=np.float32) / half)).astype(np.float32)
    start = q * cfg.CHUNK
    pos = np.concatenate([np.arange(start - 128, start + cfg.CHUNK), [cfg.PAST]]).astype(np.float32)
    ang = (pos[None, :] * np.concatenate([inv, inv])[:, None]).astype(np.float32)
    cs = np.stack([np.cos(ang), np.sin(ang)], axis=1).astype(np.float32)
    s_ = np.arange(128)[:, None]
    q_ = np.arange(128)[None, :]
    mprev = (s_ > q_).astype(np.float32)
    mcur = (s_ <= q_).astype(np.float32)
    mh = mprev if q > 0 else np.zeros_like(mprev)
    masks = np.stack([mprev, mcur, mh], axis=1).astype(np.float32)
    cm = np.zeros((128, cfg.NCORES), np.float32)
    for r in range(cfg.NCORES):
        if r // cfg.CPS == seqi and r < core:
            cm[:, r] = 1.0
    return dict(ident=ident, rmat=rm, cs=np.ascontiguousarray(cs), masks=np.ascontiguousarray(masks), cmask=cm)


def run(cfg, x_prompt, x_sample, state_conv, state_lru, cache_k_win, cache_v_win, w_in, conv_w, conv_b,
        w_gate_a, b_gate_a, w_gate_x, b_gate_x, lru_lambda, sinks, w_out_rnn, w_out_attn, w_o, ln_g, ln_b):
    key = (cfg.D, cfg.H, cfg.KV, cfg.T, cfg.NT, cfg.NS)
    if key not in _CACHE:
        _CACHE[key] = build(cfg)
    nc = _CACHE[key]
    f = lambda a: np.ascontiguousarray(np.asarray(a, dtype=np.float32))
    D, KC, NS = cfg.D, cfg.KC, cfg.NS
    fm = lambda v: f(v).reshape(KC, 128).T
    par = np.ascontiguousarray(np.stack([fm(conv_w[0][0]), fm(conv_w[0][1]), fm(conv_w[0][2]), fm(conv_w[0][3]), fm(conv_b[0]),
                                         fm(b_gate_a[0]), fm(b_gate_x[0]), fm(lru_lambda[0])], axis=1))
    shared = dict(w_in=f(w_in[0]), w_or=f(w_out_rnn[0]), w_oa=f(w_out_attn[0]), w_o=f(w_o[0]), wga=f(w_gate_a[0]), wgx=f(w_gate_x[0]),
                  par=par, sinks=f(sinks[0]), lng=f(ln_g[0]), lnb=f(ln_b[0]))
    xpr = f(x_prompt)
    in_maps = []
    for core in range(cfg.NCORES):
        seqi, q = divmod(core, cfg.CPS)
        start = q * cfg.CHUNK
        xpc = np.zeros((128 + cfg.CHUNK, D), np.float32)
        if q > 0:
            xpc[:] = xpr[seqi, start - 128:start + cfg.CHUNK]
        else:
            xpc[128:] = xpr[seqi, 0:cfg.CHUNK]
        sl = slice(core * NS, (core + 1) * NS)
        PRE = max((cfg.CPS - 1) * cfg.NT, 1) * cfg.T
        xpre = np.zeros((PRE, D), np.float32)
        vmask = np.zeros((128, PRE), np.float32)
        if start > 0:
            xpre[PRE - start:] = xpr[seqi, 0:start]
            vmask[:, PRE - start:] = 1.0
        m = dict(shared)
        m.update(xpre=xpre, vmask=vmask)
        m.update(_consts(cfg, core))
        m.update(xp=xpc, xs=f(x_sample[sl, 0]), sconv=f(state_conv[0, sl]), slru=f(state_lru[0, sl]),
                 ck=f(cache_k_win[0, sl]).reshape(NS, 128, cfg.DKV), cv=f(cache_v_win[0, sl]).reshape(NS, 128, cfg.DKV))
        in_maps.append(m)
    resu = run_bass_kernel_spmd(nc, in_maps, core_ids=list(range(cfg.NCORES)))
    R = resu.results
    B, S_, CPS = cfg.B, cfg.S, cfg.CPS
    y_p = np.zeros((B, S_, D), np.float32)
    for core in range(cfg.NCORES):
        seqi, q = divmod(core, CPS)
        y_p[seqi, q * cfg.CHUNK:(q + 1) * cfg.CHUNK] = R[core]["yp"]
    y_s = np.concatenate([R[c]["ys"] for c in range(cfg.NCORES)], axis=0)[:, None, :]
    lastc = [seqi * CPS + CPS - 1 for seqi in range(B)]
    conv_p = np.stack([R[c]["convp"] for c in lastc])[None]
    lru_p = np.stack([R[c]["lrup"][0] for c in lastc])[None]
    kw_p = np.stack([R[c]["kwp"].reshape(128, cfg.KV, 128) for c in lastc])[None]
    vw_p = np.stack([R[c]["vwp"].reshape(128, cfg.KV, 128) for c in lastc])[None]
    conv_s = np.concatenate([R[c]["convs"] for c in range(cfg.NCORES)], axis=0)[None]
    lru_s = np.concatenate([R[c]["lrus"] for c in range(cfg.NCORES)], axis=0)[None]
    kw_s = np.concatenate([R[c]["kws"] for c in range(cfg.NCORES)], axis=0).reshape(cfg.DEC, 128, cfg.KV, 128)[None]
    vw_s = np.concatenate([R[c]["vws"] for c in range(cfg.NCORES)], axis=0).reshape(cfg.DEC, 128, cfg.KV, 128)[None]
    return (y_p, y_s, conv_p, lru_p, kw_p, vw_p, conv_s, lru_s, kw_s, vw_s)


def kernel(**inputs):
    return run(Cfg(), **inputs)
```

```python
import math
import numpy as np
import concourse.bass as bass
import concourse.mybir as mybir
from concourse.bass_utils import run_bass_kernel_spmd

F32 = mybir.dt.float32
BF16 = mybir.dt.bfloat16
AF = mybir.ActivationFunctionType
ALU = mybir.AluOpType
AX = mybir.AxisListType

LRU_C = 8.0
LN_EPS = 1e-5
ROPE_THETA = 10000.0


class Sched:
    def __init__(self, nc):
        self.nc = nc
        self.eng = {'pe': nc.tensor, 'act': nc.scalar, 'dve': nc.vector,
                    'pool': nc.gpsimd, 'sp': nc.sync}
        self.prog = {e: [] for e in self.eng}
        self.sem = {}
        self.cnt = {}
        self.seen = {e: {} for e in self.eng}
        self.lastw = {}
        self.readers = {}

    SAFE = list(range(155, 251))
    NGEN = 8

    def _sem(self, name):
        if name not in self.sem:
            num = self.SAFE[len(self.sem)]
            self.sem[name] = self.nc.alloc_semaphore('s_' + name, num=num)
            self.cnt[name] = 0
        return self.sem[name]

    def op(self, engine, fn, reads=(), writes=(), dma=None, dma_inc=16):
        deps = {}

        def need(t):
            if t is None:
                return
            s, v = t
            if deps.get(s, 0) < v:
                deps[s] = v
        for r in reads:
            need(self.lastw.get(r))
        for w in writes:
            need(self.lastw.get(w))
            for t in self.readers.get(w, ()):
                need(t)
        if dma is not None:
            if not dma.startswith('d_ring'):
                self.gen_i = getattr(self, 'gen_i', 0) + 1
                dma = 'd_g%d' % (self.gen_i % self.NGEN)
            self._sem(dma)
            if self.cnt[dma] > 0:
                need((dma, self.cnt[dma]))
            sname, inc = dma, dma_inc
        else:
            sname, inc = engine, 1
            self._sem(sname)
        waits = []
        seen = self.seen[engine]
        for s, v in deps.items():
            if engine == 'pe' and s == 'pe' and dma is None:
                continue
            if seen.get(s, 0) >= v:
                continue
            seen[s] = v
            waits.append((self.sem[s], v))
        self.cnt[sname] += inc
        ticket = (sname, self.cnt[sname])
        semh = self.sem[sname]
        E = self.eng[engine]

        def thunk(E=E, waits=waits, fn=fn, semh=semh, inc=inc):
            for sh, v in waits:
                E.wait_ge(sh, v)
            ins = fn(E)
            ins.then_inc(semh, inc)
        self.prog[engine].append(thunk)
        for w in writes:
            self.lastw[w] = ticket
            self.readers[w] = []
        for r in reads:
            self.readers.setdefault(r, []).append(ticket)
        return ticket

    def barrier(self, engines=('pe', 'act', 'dve', 'pool', 'sp')):
        deps = {s: c for s, c in self.cnt.items() if c > 0}
        for e in engines:
            waits = []
            for s, v in deps.items():
                if self.seen[e].get(s, 0) >= v:
                    continue
                self.seen[e][s] = v
                waits.append((self.sem[s], v))
            E = self.eng[e]

            def thunk(E=E, waits=waits):
                for sh, v in waits:
                    E.wait_ge(sh, v)
            self.prog[e].append(thunk)

    def emit(self):
        nc = self.nc
        with nc.Block() as block:
            @block.tensor
            def _(e):
                for t in self.prog['pe']:
                    t()

            @block.scalar
            def _(e):
                for t in self.prog['act']:
                    t()

            @block.vector
            def _(e):
                for t in self.prog['dve']:
                    t()

            @block.gpsimd
            def _(e):
                for t in self.prog['pool']:
                    t()

            @block.sync
            def _(e):
                for t in self.prog['sp']:
                    t()


class Cfg:
    def __init__(self, D=4096, H=32, KV=8, T=512, NT=2, NS=16, B=2, S=4096, DEC=128, PAST=8192, NCORES=8):
        self.D, self.H, self.KV, self.T, self.NT, self.NS = D, H, KV, T, NT, NS
        self.B, self.S, self.DEC, self.PAST, self.NCORES = B, S, DEC, PAST, NCORES
        self.G = H // KV
        self.DA = H * 128
        self.DKV = KV * 128
        self.KC = D // 128
        self.KA = self.DA // 128
        self.NB = T // 128
        self.NCOLS = 2 * D + 2 * self.DA + 2 * self.DKV + 2 * D
        self.NBLK = D // 256
        self.CHUNK = T * NT
        self.CPS = S // self.CHUNK
        assert self.CPS * B == NCORES and NS * NCORES == DEC
        self.ALPHA = (2.0 * 1) ** 0.25


def build(cfg):
    D, H, KV, T, NT, NS = cfg.D, cfg.H, cfg.KV, cfg.T, cfg.NT, cfg.NS
    G, DA, DKV, KC, KA, NB, NCOLS, NBLK = cfg.G, cfg.DA, cfg.DKV, cfg.KC, cfg.KA, cfg.NB, cfg.NCOLS, cfg.NBLK
    OFF_U, OFF_G, OFF_Q = 0, D, 2 * D
    OFF_K = OFF_Q + DA
    OFF_V = OFF_K + DKV
    OFF_GA = OFF_V + DKV
    OFF_MR = OFF_GA + DA
    OFF_MA = OFF_MR + D
    KT = min(4, KC)
    TS = T + NS
    TH = T + 128
    NCR = cfg.NCORES
    KM = max(KC, KA)

    nc = bass.Bass("TRN2", target_bir_lowering=False)
    S = Sched(nc)
    USE_CC = getattr(cfg, 'use_cc', False)

    def din(name, shape):
        return nc.dram_tensor(name, list(shape), F32, kind="ExternalInput").ap()

    def dout(name, shape):
        return nc.dram_tensor(name, list(shape), F32, kind="ExternalOutput").ap()
    xp = din("xp", [128 + NT * T, D])
    NPRE = (cfg.CPS - 1) * NT
    xpre = din("xpre", [max(NPRE, 1) * T, D])
    vmask_d = din("vmask", [128, max(NPRE, 1) * T])
    xs = din("xs", [NS, D])
    sconv = din("sconv", [NS, 3, D])
    slru = din("slru", [NS, D])
    ck = din("ck", [NS, 128, DKV])
    cv = din("cv", [NS, 128, DKV])
    w_in = din("w_in", [D, NCOLS])
    w_or = din("w_or", [D, D])
    w_oa = din("w_oa", [DA, D])
    w_o = din("w_o", [D, D])
    wga = din("wga", [NBLK, 256, 256])
    wgx = din("wgx", [NBLK, 256, 256])
    par = din("par", [128, 8, KC])
    sinks = din("sinks", [H])
    lng = din("lng", [D])
    lnb = din("lnb", [D])
    ident_d = din("ident", [128, 128])
    rmat_d = din("rmat", [128, 128])
    cs_d = din("cs", [128, 2, NT * T + 128 + 1])
    masks_d = din("masks", [128, 3, 128])
    cmask_d = din("cmask", [128, NCR])
    yp = dout("yp", [NT * T, D])
    ys = dout("ys", [NS, D])
    convp = dout("convp", [3, D])
    lrup = dout("lrup", [1, D])
    kwp = dout("kwp", [128, DKV])
    vwp = dout("vwp", [128, DKV])
    convs = dout("convs", [NS, 3, D])
    lrus = dout("lrus", [NS, D])
    kws = dout("kws", [NS, 128, DKV])
    vws = dout("vws", [NS, 128, DKV])
    cc_in = nc.dram_tensor("cc_in", [128, 2 * KC], F32).ap()
    cc_out = nc.dram_tensor("cc_out", [128 * NCR, 2 * KC], F32).ap()

    base = [16512]
    LIMIT = 229344

    def sb(name, shape, dt, at=None):
        nb = int(np.prod(shape[1:])) * (2 if dt == BF16 else 4)
        nb = (nb + 31) // 32 * 32
        if at is None:
            off = base[0]
            base[0] += nb
        else:
            off = at
        assert off + nb <= LIMIT, (name, off, nb)
        return nc.alloc_sbuf_tensor_at(name, list(shape), dt, offset=off), off + nb

    ident, _ = sb("ident", [128, 128], F32)
    rmat, _ = sb("rmat", [128, 128], F32)
    ones, _ = sb("ones", [128, 128], BF16)
    zer, _ = sb("zer", [128, 512], BF16)
    parT, _ = sb("parT", [128, 8, KC], F32)
    nls, _ = sb("nls", [128, 2, KC], F32)
    nbias, _ = sb("nbias", [128, 2, KC], F32)
    hnls, _ = sb("hnls", [128, 2, KC], F32)
    esink, _ = sb("esink", [128, H], F32)
    cmask, _ = sb("cmask", [128, NCR], F32)
    maskt, _ = sb("maskt", [128, 2, 512], BF16)
    maskf, _ = sb("maskf", [128, 3, 128], F32)
    hcar, _ = sb("hcar", [128, KC], F32)
    hin, _ = sb("hin", [128, KC], F32)
    ucar, _ = sb("ucar", [128, KC, 3], F32)
    rsum, _ = sb("rsum", [128, KC, NT], F32)
    pend, _ = sb("pend", [128, 2, KC], F32)
    small, _ = sb("small", [128, 64], F32)
    tst, _ = sb("tst", [128, 512], F32)
    if USE_CC:
        gath, _ = sb("gath", [128, NCR, 2 * KC], F32)
        vm = None
    else:
        vm, _ = sb("vm", [128, T], F32)
    kcar, _ = sb("kcar", [128, KV, 128], BF16)
    vcar, _ = sb("vcar", [128, DKV], BF16)
    NSLOT = 6
    ring = [sb("ring%d" % i, [128, KT, 512], BF16)[0] for i in range(NSLOT)]
    mgT, _ = sb("mgT", [128, KC, TS], BF16)
    R0 = base[0]
    xT, _ = sb("xT", [128, KC, TH], BF16)
    yoT, R1 = sb("yoT", [128, KM, TS], BF16)
    base[0] = R1
    sm, _ = sb("sm", [128, 4, TS], F32)
    base[0] = R1
    uf, _ = sb("uf", [128, 4, 3 + TS], F32)
    xc, _ = sb("xc", [128, 4, TS], F32)
    xcb, _ = sb("xcb", [128, 4, TS], BF16)
    tmp = [sb("tmp%d" % i, [128, TS], F32)[0] for i in range(6)]
    hT_off = base[0]
    hT, _ = sb("hT", [128, 4, TS], F32)
    xin = None
    sgb, _ = sb("sgb", [128, 4, TS], BF16)
    gw = [sb("gw%d" % i, [128, 2, 2, 256], BF16)[0] for i in range(2)]
    sT_off = base[0]
    sT, _ = sb("sT", [128, KC, 3, NS], F32)
    h0T, _ = sb("h0T", [128, KC, NS], F32)
    if base[0] - sT_off >= 4 * 2048:
        NXIN = 4
        xin = [sb("xin%d" % i, [128, 512], F32, at=sT_off + i * 2048)[0] for i in range(NXIN)]
    else:
        NXIN = 1
        xin = [sb("xin0", [128, 512], F32)[0]]
    rnn_end = base[0]
    base[0] = R1
    kT, _ = sb("kT", [128, KV, TH], BF16)
    Vt, _ = sb("Vt", [128, NB + 1, DKV], BF16)
    qf_off = base[0]
    qf = [sb("qf%d" % i, [128, 512], F32)[0] for i in range(2)]
    rtmp, _ = sb("rtmp", [128, 512], F32)
    qT, _ = sb("qT", [128, G, T], BF16)
    qTs, _ = sb("qTs", [128, H, NS], BF16)
    sgT, _ = sb("sgT", [128, G, T], F32)
    sgs, _ = sb("sgs", [128, H, NS], F32)
    pT = [sb("pT%d" % i, [128, 512], BF16)[0] for i in range(2)]
    rden, _ = sb("rden", [128, 256], F32)
    otmp, _ = sb("otmp", [128, 256], F32)
    kf32, _ = sb("kf32", [128, KV, 128], F32)
    ksf, _ = sb("ksf", [128, KV, NS], F32)
    win32, _ = sb("win32", [128, 512], F32)
    assert DKV * 4 <= 2 * 2048
    kwin, _ = sb("kwin", [128, DKV], F32, at=qf_off)
    vwinb, _ = sb("vwinb", [128, DKV], BF16)
    kwT4, _ = sb("kwT4", [128, 4, 128], BF16)
    pTs, _ = sb("pTs", [128, H], BF16)
    sa1, _ = sb("sa1", [128, H], F32)
    sa2, _ = sb("sa2", [128, H], F32)
    cst, _ = sb("cst", [128, 2, T + 128 + 1], F32)
    att_end = base[0]
    base[0] = R0
    res, _ = sb("res", [128, NB + 1, D], F32)
    lnG, _ = sb("lnG", [128, D], F32)
    lnB, _ = sb("lnB", [128, D], F32)
    NXR = 6
    xr = [sb("xr%d" % i, [128, 512], F32)[0] for i in range(NXR)]
    junk, _ = sb("junk", [128, 512], F32)
    s1, _ = sb("s1", [128, NB + 1, D // 512], F32)
    s2, _ = sb("s2", [128, NB + 1, D // 512], F32)
    st, _ = sb("st", [128, 8, NB + 1], F32)
    fin_end = base[0]

    A = [nc.alloc_psum_tensor("A%d" % i, [128, 512], F32) for i in range(4)]
    X = nc.alloc_psum_tensor("X", [128, 512], F32)
    S0 = nc.alloc_psum_tensor("S0", [128, 512], F32)
    S1 = nc.alloc_psum_tensor("S1", [128, 512], F32)
    OD = nc.alloc_psum_tensor("OD", [128, 512], F32)

    def dma(q, out, in_, sem, reads=(), writes=()):
        return S.op(q, lambda E: E.dma_start(out=out, in_=in_), reads=reads, writes=writes, dma=sem)

    def act(out, in_, func, reads, writes, **kw):
        return S.op('act', lambda E: E.activation(out=out, in_=in_, func=func, **kw), reads=reads, writes=writes)

    def dve(fn, reads, writes):
        return S.op('dve', fn, reads=reads, writes=writes)

    def pool(fn, reads, writes):
        return S.op('pool', fn, reads=reads, writes=writes)

    ring_i = [0]

    def next_slot():
        i = ring_i[0] % NSLOT
        ring_i[0] += 1
        return i

    def stream(wsrc, nk, col0, pe_body, reads, writes, hook=None):
        nkq = nk // KT
        for kq in range(nkq):
            si = next_slot()
            slot = ring[si]
            src = wsrc[kq * KT * 128:(kq + 1) * KT * 128, col0:col0 + 512].rearrange("(k p) c -> p k c", p=128)
            dma('pool', slot[:, :, :], src, 'd_ring%d' % si, writes=['ring%d' % si])
            rd = ['ring%d' % si, 'zer'] + list(reads)
            if kq < nkq - 1 or nkq == 1 and False:
                def body(E, slot=slot, kq=kq):
                    ins = None
                    for kk in range(KT):
                        ins = pe_body(E, slot, kq * KT + kk, kk, nk, None)
                    return ins
                S.op('pe', body, reads=rd, writes=writes)
            else:
                for part in range(pe_body.nparts):
                    def body(E, slot=slot, kq=kq, part=part):
                        ins = None
                        for kk in range(KT):
                            ins = pe_body(E, slot, kq * KT + kk, kk, nk, part)
                        return ins
                    S.op('pe', body, reads=rd, writes=pe_body.part_writes(part, writes))
            if hook is not None:
                hook(kq, nkq)

    def fm_body(segs):
        def body(E, slot, k, kk, nk, part):
            ins = None
            if k == 0 and len(segs) > 1 and part in (None, 0):
                E.matmul(X[:, :], lhsT=zer[:, 0:128], rhs=zer[:, :], start=True, stop=False)
            for c in (range(4) if part is None else [part]):
                for si_, (rhs_fn, out_fn) in enumerate(segs):
                    if si_ == 0:
                        ins = E.matmul(out_fn(c), lhsT=slot[:, kk, c * 128:(c + 1) * 128], rhs=rhs_fn(k),
                                       start=(k == 0), stop=(k == nk - 1))
                    else:
                        ins = E.matmul(out_fn(c), lhsT=slot[:, kk, c * 128:(c + 1) * 128], rhs=rhs_fn(k),
                                       start=False, stop=(k == nk - 1 and c == 3 and si_ == len(segs) - 1))
            return ins
        body.nparts = 4
        body.part_writes = lambda part, writes: ['A%d' % part] + (['X'] if len(segs) > 1 else [])
        return body

    def tm_body(segs, keys=None):
        def body(E, slot, k, kk, nk, part):
            ins = None
            for lhs_fn, out_ap in (segs if part is None else [segs[part]]):
                ins = E.matmul(out_ap, lhsT=lhs_fn(k), rhs=slot[:, kk, :], start=(k == 0), stop=(k == nk - 1))
            return ins
        body.nparts = 1 if keys is None else len(segs)
        body.part_writes = (lambda part, writes: writes) if keys is None else (lambda part, writes: keys[part])
        if keys is None:
            inner = body

            def body2(E, slot, k, kk, nk, part):
                return inner(E, slot, k, kk, nk, None)
            body2.nparts = 1
            body2.part_writes = lambda part, writes: writes
            return body2
        return body

    def tstore(srcs, n, dst, reads, key):
        m = len(srcs)

        def body(E):
            ins = None
            for i, s_ in enumerate(srcs):
                ins = E.transpose(OD[0:n, 256 + i * 0:256 + i * 0 + 0] if False else X2[0:n, i * 128:(i + 1) * 128], s_, ident[:, :])
            return ins
        S.op('pe', body, reads=list(reads) + ['ident'], writes=['S1'])
        act(tst[0:n, 0:128 * m], S1[0:n, 0:128 * m], AF.Copy, reads=['S1'], writes=['tst'])
        dma('sp', dst, tst[0:n, 0:128 * m], 'd_tst', reads=['tst'])
    X2 = S1

    dma('sp', ident[:, :], ident_d, 'd_c0', writes=['ident'])
    dma('sp', rmat[:, :], rmat_d, 'd_c1', writes=['rmat'])
    dma('sp', parT[:, :, :], par, 'd_c2', writes=['parT'])
    dma('sp', esink[:, :], sinks.partition_broadcast(128), 'd_c3', writes=['esink'])
    dma('sp', cmask[:, :], cmask_d, 'd_c4', writes=['cmask'])
    dma('sp', maskf[:, :, :], masks_d, 'd_c5', writes=['maskf'])
    dve(lambda E: E.memset(ones[:, :], 1.0), [], ['ones'])
    dve(lambda E: E.memset(zer[:, :], 0.0), [], ['zer'])
    act(esink[:, :], esink[:, :], AF.Exp, ['esink'], ['esink'])
    for v_, pi in ((0, 0), (1, 2)):
        for hh in range(2):
            dve(lambda E, v_=v_, pi=pi, hh=hh: E.tensor_copy(out=maskt[:, v_, hh * 128:(hh + 1) * 128], in_=maskf[:, pi, :]), ['maskf'], ['maskt'])
            dve(lambda E, v_=v_, hh=hh: E.tensor_copy(out=maskt[:, v_, 256 + hh * 128:256 + (hh + 1) * 128], in_=maskf[:, 1, :]), ['maskf'], ['maskt'])
    dve(lambda E: E.tensor_scalar(out=nbias[:, :, :], in0=parT[:, 5:7, :], scalar1=0.5, scalar2=None, op0=ALU.mult), ['parT'], ['nbias'])
    act(nls[:, 0, :], parT[:, 7, :], AF.Exp, ['parT'], ['nls'], scale=-1.0)
    act(nls[:, 0, :], nls[:, 0, :], AF.Ln, ['nls'], ['nls'], bias=1.0)
    dve(lambda E: E.tensor_scalar(out=nls[:, 1, :], in0=nls[:, 0, :], scalar1=-2.0 * LRU_C, scalar2=None, op0=ALU.mult), ['nls'], ['nls2'])
    dve(lambda E: E.tensor_scalar(out=nls[:, 0, :], in0=nls[:, 0, :], scalar1=-LRU_C, scalar2=None, op0=ALU.mult), ['nls', 'nls2'], ['nls'])
    dve(lambda E: E.tensor_scalar(out=hnls[:, 0, :], in0=nls[:, 0, :], scalar1=0.5, scalar2=None, op0=ALU.mult), ['nls'], ['nls'])
    dve(lambda E: E.tensor_copy(out=hnls[:, 1, :], in_=nls[:, 0, :]), ['nls'], ['nls'])

    def load_xT(row0, nblk, col0, xp=xp):
        li = [0]
        for b in range(nblk):
            for cgp in range(D // 512):
                xi = xin[li[0] % NXIN]
                key = 'xin%d' % (li[0] % NXIN)
                li[0] += 1
                dma('sp', xi[:, :], xp[row0 + b * 128:row0 + (b + 1) * 128, cgp * 512:(cgp + 1) * 512], 'd_' + key, writes=[key])

                def body(E, xi=xi):
                    ins = None
                    for i in range(4):
                        ins = E.transpose(OD[:, i * 128:(i + 1) * 128], xi[:, i * 128:(i + 1) * 128], ident[:, :])
                    return ins
                S.op('pe', body, reads=[key, 'ident'], writes=['OD0', 'OD1'])
                act(xT[:, cgp * 4:(cgp + 1) * 4, col0 + b * 128:col0 + (b + 1) * 128],
                    OD[:, :].rearrange("p (k t) -> p k t", k=4), AF.Copy, ['OD0', 'OD1'], ['xT'])

    def load_sample_states():
        for cgp in range(D // 512):
            for kind in range(3):
                xi = tst
                if kind == 0:
                    n = NS
                    dma('sp', xi[0:NS, :], xs[:, cgp * 512:(cgp + 1) * 512], 'd_tstx', writes=['tst'])
                elif kind == 1:
                    n = NS * 3
                    dma('sp', xi[0:n, :], sconv.rearrange("b j d -> (b j) d")[:, cgp * 512:(cgp + 1) * 512], 'd_tstx', writes=['tst'])
                else:
                    n = NS
                    dma('sp', xi[0:NS, :], slru[:, cgp * 512:(cgp + 1) * 512], 'd_tstx', writes=['tst'])

                def body(E, xi=xi, n=n):
                    ins = None
                    for i in range(4):
                        ins = E.transpose(S0[:, i * 128:i * 128 + n], xi[0:n, i * 128:(i + 1) * 128], ident[0:n, 0:n])
                    return ins
                S.op('pe', body, reads=['tst', 'ident'], writes=['S0'])
                src = S0[:, :].rearrange("p (k t) -> p k t", k=4)[:, :, 0:n]
                if kind == 0:
                    act(xT[:, cgp * 4:(cgp + 1) * 4, T:T + NS], src, AF.Copy, ['S0'], ['xT'])
                elif kind == 1:
                    act(sT[:, cgp * 4:(cgp + 1) * 4, :, :].rearrange("p k j b -> p k b j"),
                        src.rearrange("p k (b j) -> p k b j", j=3), AF.Copy, ['S0'], ['sT'])
                else:
                    act(h0T[:, cgp * 4:(cgp + 1) * 4, :], src, AF.Copy, ['S0'], ['h0T'])

    def rnn_stage(tile, prepass, pre_first=False, tail_fn=None):
        last = (tile == NT - 1)
        samp = last and not prepass
        ncol = TS if samp else T
        NG = D // 512
        use_x_halo = (tile == 0) and (not prepass or USE_CC)
        deferred = []

        def pop_hook(kq, nkq):
            if deferred:
                deferred.pop(0)()

        def proj_u(gi, hook=None):
            segs = [(lambda k: xT[:, k, 0:T], lambda c: A[c][:, 0:T])]
            if use_x_halo:
                segs.append((lambda k: xT[:, k, T + 125:T + 128], lambda c: X[:, c * 128:c * 128 + 3]))
            if samp:
                segs.append((lambda k: xT[:, k, T:T + NS], lambda c: X[:, c * 128:c * 128 + NS]))
            stream(w_in, KC, OFF_U + gi * 512, fm_body(segs), ['xT'], ['A0', 'A1', 'A2', 'A3', 'X'], hook=hook)

        def phase1(gi):
            phase1a(gi)
            phase1b(gi)

        def phase1a(gi):
            ch0 = gi * 4
            for c in range(4):
                act(uf[:, c, 3:3 + T], A[c][:, 0:T], AF.Copy, ['A%d' % c], ['uf%d' % c])
            for c in range(4):
                ch = ch0 + c
                if use_x_halo:
                    act(uf[:, c, 0:3], X[:, c * 128:c * 128 + 3], AF.Copy, ['X'], ['uf%d' % c])
                elif prepass and pre_first:
                    dve(lambda E, c=c: E.memset(uf[:, c, 0:3], 0.0), [], ['uf%d' % c])
                else:
                    act(uf[:, c, 0:3], ucar[:, ch, :], AF.Copy, ['ucar'], ['uf%d' % c])
                if samp:
                    act(uf[:, c, 3 + T:3 + TS], X[:, c * 128:c * 128 + NS], AF.Copy, ['X'], ['uf%d' % c])

        def phase1b(gi):
            ch0 = gi * 4
            for c in range(4):
                ch = ch0 + c
                dve(lambda E, c=c, ch=ch: E.tensor_scalar(out=xc[:, c, 0:T], in0=uf[:, c, 0:T], scalar1=parT[:, 0, ch:ch + 1],
                                                          scalar2=parT[:, 4, ch:ch + 1], op0=ALU.mult, op1=ALU.add),
                    ['uf%d' % c, 'parT'], ['xc%d' % c])
                for j in range(1, 4):
                    dve(lambda E, c=c, ch=ch, j=j: E.scalar_tensor_tensor(out=xc[:, c, 0:T], in0=uf[:, c, j:j + T], scalar=parT[:, j, ch:ch + 1],
                                                                          in1=xc[:, c, 0:T], op0=ALU.mult, op1=ALU.add),
                        ['uf%d' % c, 'parT', 'xc%d' % c], ['xc%d' % c])
                if samp:
                    dve(lambda E, c=c, ch=ch: E.tensor_scalar(out=xc[:, c, T:TS], in0=uf[:, c, 3 + T:3 + TS], scalar1=parT[:, 3, ch:ch + 1],
                                                              scalar2=parT[:, 4, ch:ch + 1], op0=ALU.mult, op1=ALU.add),
                        ['uf%d' % c, 'parT', 'xc%d' % c], ['xc%d' % c])
                    for j in range(3):
                        dve(lambda E, c=c, ch=ch, j=j: E.scalar_tensor_tensor(out=xc[:, c, T:TS], in0=sT[:, ch, j, :], scalar=parT[:, j, ch:ch + 1],
                                                                              in1=xc[:, c, T:TS], op0=ALU.mult, op1=ALU.add),
                            ['sT', 'parT', 'xc%d' % c], ['xc%d' % c])
            for c in range(4):
                ch = ch0 + c
                act(xcb[:, c, 0:ncol], xc[:, c, 0:ncol], AF.Copy, ['xc%d' % c], ['xcb%d' % c])
                act(ucar[:, ch, :], uf[:, c, T:T + 3], AF.Copy, ['uf%d' % c], ['ucar'])
            if samp:
                deferred.append(lambda gi=gi: tstore([uf[:, c, 3 + T:3 + TS] for c in range(4)], NS, convs[:, 2, gi * 512:(gi + 1) * 512], ['uf%d' % c for c in range(4)], 'x'))
            if last and not prepass:
                deferred.append(lambda gi=gi: tstore([uf[:, c, T:T + 3] for c in range(4)], 3, convp[:, gi * 512:(gi + 1) * 512], ['uf%d' % c for c in range(4)], 'x'))

        def proj_g(gi):
            segs = [(lambda k: xT[:, k, 0:T], lambda c: A[c][:, 0:T])]
            if samp:
                segs.append((lambda k: xT[:, k, T:T + NS], lambda c: X[:, c * 128:c * 128 + NS]))
            stream(w_in, KC, OFF_G + gi * 512, fm_body(segs), ['xT'], ['A0', 'A1', 'A2', 'A3', 'X'], hook=pop_hook)
            while deferred:
                deferred.pop(0)()
            for c in range(4):
                srcs = [(A[c][:, 0:T], 0, T, 'A%d' % c)]
                if samp:
                    srcs.append((X[:, c * 128:c * 128 + NS], T, TS, 'X'))
                for src, a0, a1, key in srcs:
                    act(tmp[3][:, a0:a1], src, AF.Tanh, [key], ['t4'], scale=0.5)
                    dve(lambda E, c=c, src=src, a0=a0, a1=a1: E.scalar_tensor_tensor(out=sgb[:, c, a0:a1], in0=tmp[3][:, a0:a1], scalar=1.0, in1=src,
                                                                                     op0=ALU.add, op1=ALU.mult), [key, 't4'], ['sgb%d' % c])

        def phase2_pair(gi, bi):
            ch0 = gi * 4
            blk = gi * 2 + bi
            g_ = gw[blk % 2]
            gk = 'gw%d' % (blk % 2)
            dma('pool', g_[:, 0, :, :], wga[blk].rearrange("(dc p) e -> p dc e", p=128), 'd_' + gk + 'a', writes=[gk + 'a'])
            dma('pool', g_[:, 1, :, :], wgx[blk].rearrange("(dc p) e -> p dc e", p=128), 'd_' + gk + 'x', writes=[gk + 'x'])
            t1 = tmp[0]
            bufs = {}
            for e in range(2):
                c = bi * 2 + e
                ch = ch0 + c

                def gbody(E, g_=g_, bi=bi, e=e):
                    ins = None
                    for ax, bank in ((0, S0), (1, S1)):
                        for dc in range(2):
                            ins = E.matmul(bank[:, 0:T], lhsT=g_[:, ax, dc, e * 128:(e + 1) * 128], rhs=xcb[:, bi * 2 + dc, 0:T],
                                           start=(dc == 0), stop=(dc == 1))
                    if samp:
                        for ax in range(2):
                            for dc in range(2):
                                ins = E.matmul(OD[:, ax * 128:ax * 128 + NS], lhsT=g_[:, ax, dc, e * 128:(e + 1) * 128],
                                               rhs=xcb[:, bi * 2 + dc, T:TS], start=(dc == 0), stop=(dc == 1))
                    return ins
                S.op('pe', gbody, reads=[gk + 'a', gk + 'x', 'xcb%d' % (bi * 2), 'xcb%d' % (bi * 2 + 1)], writes=['S0', 'S1', 'OD0'])
                t2, t3 = (tmp[1], tmp[2]) if e == 0 else (tmp[4], tmp[5])
                k2, k3 = ('t2', 't3') if e == 0 else ('t2b', 't3b')
                bufs[e] = (t2, t3, k2, k3)
                hk = 'hT%d' % c
                act(t1[:, 0:T], S0[:, 0:T], AF.Tanh, ['S0', 'nbias'], ['t1'], scale=0.5, bias=nbias[:, 0, ch:ch + 1])
                act(t2[:, 0:T], S1[:, 0:T], AF.Tanh, ['S1', 'nbias'], [k2], scale=0.5, bias=nbias[:, 1, ch:ch + 1])
                if samp:
                    act(t1[:, T:TS], OD[:, 0:NS], AF.Tanh, ['OD0', 'nbias'], ['t1'], scale=0.5, bias=nbias[:, 0, ch:ch + 1])
                    act(t2[:, T:TS], OD[:, 128:128 + NS], AF.Tanh, ['OD0', 'nbias'], [k2], scale=0.5, bias=nbias[:, 1, ch:ch + 1])
                act(t3[:, 0:ncol], t1[:, 0:ncol], AF.Exp, ['t1', 'nls'], [k3], scale=hnls[:, 0, ch:ch + 1], bias=hnls[:, 0, ch:ch + 1])
                act(hT[:, c, 0:ncol], t1[:, 0:ncol], AF.Exp, ['t1', 'nls'], [hk], scale=hnls[:, 1, ch:ch + 1], bias=hnls[:, 1, ch:ch + 1])
            for e in range(2):
                c = bi * 2 + e
                ch = ch0 + c
                t2, t3, k2, k3 = bufs[e]
                hk = 'hT%d' % c
                act(hT[:, c, 0:ncol], hT[:, c, 0:ncol], AF.Sqrt, [hk], [hk], scale=-1.0, bias=1.0)
                dve(lambda E, t2=t2, c=c: E.scalar_tensor_tensor(out=t2[:, 0:ncol], in0=t2[:, 0:ncol], scalar=1.0, in1=hT[:, c, 0:ncol], op0=ALU.add, op1=ALU.mult), [k2, hk], [k2])
                dve(lambda E, t2=t2, c=c: E.scalar_tensor_tensor(out=t2[:, 0:ncol], in0=t2[:, 0:ncol], scalar=0.5, in1=xc[:, c, 0:ncol], op0=ALU.mult, op1=ALU.mult),
                    [k2, 'xc%d' % c], [k2])
                hkey = 'hin' if (tile == 0 and not prepass) else 'hcar'
                if prepass and USE_CC:
                    init = 0.0 if tile == 0 else hcar[:, ch:ch + 1]
                elif prepass:
                    init = 0.0 if pre_first else hcar[:, ch:ch + 1]
                    dve(lambda E, t2=t2: E.tensor_tensor(out=t2[:, 0:T], in0=t2[:, 0:T], in1=vm[:, 0:T], op=ALU.mult), [k2, 'vm'], [k2])
                elif tile == 0:
                    init = hin[:, ch:ch + 1]
                else:
                    init = hcar[:, ch:ch + 1]
                dve(lambda E, t2=t2, t3=t3, c=c, init=init: E.tensor_tensor_scan(out=hT[:, c, 0:T], data0=t3[:, 0:T], data1=t2[:, 0:T], initial=init,
                                                                                 op0=ALU.mult, op1=ALU.add), [k3, k2, hkey], [hk])
                dve(lambda E, c=c, ch=ch: E.tensor_copy(out=hcar[:, ch:ch + 1], in_=hT[:, c, T - 1:T]), [hk], ['hcar'])
                if samp:
                    dve(lambda E, t3=t3, c=c, ch=ch: E.tensor_tensor(out=hT[:, c, T:TS], in0=t3[:, T:TS], in1=h0T[:, ch, :], op=ALU.mult), [k3, 'h0T'], [hk])
                    dve(lambda E, t2=t2, c=c: E.tensor_tensor(out=hT[:, c, T:TS], in0=hT[:, c, T:TS], in1=t2[:, T:TS], op=ALU.add), [k2, hk], [hk])

        def phase3(gi):
            ch0 = gi * 4
            if samp:
                deferred.append(lambda gi=gi: tstore([hT[:, c, T:TS] for c in range(4)], NS, lrus[:, gi * 512:(gi + 1) * 512], ['hT%d' % c for c in range(4)], 'x'))
            if last and not prepass:
                deferred.append(lambda gi=gi: tstore([hT[:, c, T - 1:T] for c in range(4)], 1, lrup[:, gi * 512:(gi + 1) * 512], ['hT%d' % c for c in range(4)], 'x'))
            if not prepass:
                for c in range(4):
                    ch = ch0 + c
                    dve(lambda E, c=c, ch=ch: E.scalar_tensor_tensor(out=yoT[:, ch, 0:ncol], in0=hT[:, c, 0:ncol], scalar=0.5, in1=sgb[:, c, 0:ncol], op0=ALU.mult, op1=ALU.mult),
                        ['hT%d' % c, 'sgb%d' % c], ['yoT'])

        proj_u(0)
        phase1(0)
        for gi in range(NG):
            if not prepass:
                proj_g(gi)
            if gi + 1 < NG:
                done = []

                def hook(kq, nkq, gi=gi, done=done):
                    if not done and 2 * (kq + 1) // nkq >= 1:
                        phase2_pair(gi, 0)
                        done.append(1)
                proj_u(gi + 1, hook=hook)
                phase1a(gi + 1)
                phase2_pair(gi, 1)
                phase3(gi)
                phase1b(gi + 1)
            else:
                phase2_pair(gi, 0)
                if tail_fn is not None:
                    tail_fn()
                phase2_pair(gi, 1)
                phase3(gi)
        while deferred:
            deferred.pop(0)()

    def merge(tile, part):
        samp = (tile == NT - 1)
        ncol = TS if samp else T
        for mg in range(D // 512):
            segs = [(lambda k: xT[:, k, 0:T], lambda c: A[c][:, 0:T])]
            if samp:
                segs.append((lambda k: xT[:, k, T:T + NS], lambda c: X[:, c * 128:c * 128 + NS]))
            stream(w_in, KC, (OFF_MR if part == 1 else OFF_MA) + mg * 512, fm_body(segs), ['xT'], ['A0', 'A1', 'A2', 'A3', 'X'])
            for c in range(4):
                act(sm[:, c, 0:T], A[c][:, 0:T], AF.Tanh, ['A%d' % c], ['sm%d' % c], scale=0.5)
                if samp:
                    act(sm[:, c, T:TS], X[:, c * 128:c * 128 + NS], AF.Tanh, ['X'], ['sm%d' % c], scale=0.5)
                dve(lambda E, c=c: E.tensor_scalar(out=sm[:, c, 0:ncol], in0=sm[:, c, 0:ncol], scalar1=0.5, scalar2=0.5, op0=ALU.mult, op1=ALU.add), ['sm%d' % c], ['sm%d' % c])
            segs = [(lambda k: yoT[:, k, 0:T], lambda c: A[c][:, 0:T])]
            if samp:
                segs.append((lambda k: yoT[:, k, T:T + NS], lambda c: X[:, c * 128:c * 128 + NS]))
            stream(w_or if part == 1 else w_oa, KC if part == 1 else KA, mg * 512, fm_body(segs), ['yoT'], ['A0', 'A1', 'A2', 'A3', 'X'])
            for c in range(4):
                ch = mg * 4 + c
                srcs = [(A[c][:, 0:T], 0, T, 'A%d' % c)]
                if samp:
                    srcs.append((X[:, c * 128:c * 128 + NS], T, TS, 'X'))
                for src, a0, a1, key in srcs:
                    if part == 1:
                        dve(lambda E, c=c, ch=ch, src=src, a0=a0, a1=a1: E.tensor_tensor(out=mgT[:, ch, a0:a1], in0=src, in1=sm[:, c, a0:a1], op=ALU.mult),
                            [key, 'sm%d' % c], ['mgT'])
                    else:
                        dve(lambda E, c=c, src=src, a0=a0, a1=a1: E.tensor_tensor(out=sm[:, c, a0:a1], in0=src, in1=sm[:, c, a0:a1], op=ALU.mult),
                            [key, 'sm%d' % c], ['sm%d' % c])
                        dve(lambda E, c=c, ch=ch, a0=a0, a1=a1: E.tensor_tensor(out=mgT[:, ch, a0:a1], in0=mgT[:, ch, a0:a1], in1=sm[:, c, a0:a1], op=ALU.add),
                            ['sm%d' % c, 'mgT'], ['mgT'])

    def rope(src, n, cosap, sinap, outb, out32, skey, okeys, qi):
        q_ = qf[qi]
        qk = 'qf%d' % qi
        act(q_[:, 0:n], src, AF.Copy, [skey], [qk])
        S.op('pe', lambda E: E.matmul(src, lhsT=rmat[:, :], rhs=q_[:, 0:n], start=True, stop=True), reads=['rmat', qk], writes=[skey])
        dve(lambda E: E.tensor_tensor(out=rtmp[:, 0:n], in0=src, in1=sinap, op=ALU.mult), [skey, 'cs'], ['rtmp'])
        dve(lambda E: E.tensor_tensor(out=q_[:, 0:n], in0=q_[:, 0:n], in1=cosap, op=ALU.mult), [qk, 'cs'], [qk])
        if out32 is not None:
            dve(lambda E: E.tensor_tensor(out=out32, in0=q_[:, 0:n], in1=rtmp[:, 0:n], op=ALU.add), [qk, 'rtmp'], okeys)
        if outb is not None:
            dve(lambda E: E.tensor_tensor(out=outb, in0=q_[:, 0:n], in1=rtmp[:, 0:n], op=ALU.add), [qk, 'rtmp'], okeys)

    def attn_stage(tile, cst):
        last = (tile == NT - 1)
        samp = last
        scale = 128.0 ** -0.5
        cosm, sinm = cst[:, 0, 0:T], cst[:, 1, 0:T]
        if tile > 0:
            dve(lambda E: E.tensor_copy(out=kT[:, :, 0:128], in_=kcar[:, :, :]), ['kcar'], ['kT'])
            dve(lambda E: E.tensor_copy(out=Vt[:, 0, :], in_=vcar[:, :]), ['vcar'], ['Vt'])
        for cg in range(DKV // 512):
            segs = [((lambda k, tb=tb: xT[:, k, tb * 128:(tb + 1) * 128]), A[tb][:, :]) for tb in range(NB)]
            wr = ['A%d' % tb for tb in range(NB)]
            if tile == 0:
                segs.append((lambda k: xT[:, k, T:T + 128], OD[:, :]))
                wr += ['OD0', 'OD1']
            if samp:
                segs.append((lambda k: xT[:, k, T:T + NS], X[0:NS, :]))
                wr.append('X')
            stream(w_in, KC, OFF_V + cg * 512, tm_body(segs), ['xT'], wr)
            for tb in range(NB):
                act(Vt[:, tb + 1, cg * 512:(cg + 1) * 512], A[tb][:, :], AF.Copy, ['A%d' % tb], ['Vt'])
            if tile == 0:
                act(Vt[:, 0, cg * 512:(cg + 1) * 512], OD[:, :], AF.Copy, ['OD0', 'OD1'], ['Vt'])
            if last:
                act(win32[:, :], A[NB - 1][:, :], AF.Copy, ['A%d' % (NB - 1)], ['win32'])
                dma('sp', vwp[:, cg * 512:(cg + 1) * 512], win32[:, :], 'd_win32', reads=['win32'])
            if samp:
                act(tst[0:NS, :], X[0:NS, :], AF.Copy, ['X'], ['tst'])
                dma('sp', vws[:, 127, cg * 512:(cg + 1) * 512], tst[0:NS, :], 'd_tst', reads=['tst'], writes=['vws_new'])
        for cg in range(DKV // 512):
            segs = [(lambda k: xT[:, k, 0:T], lambda c: A[c][:, 0:T])]
            if tile == 0:
                segs.append((lambda k: xT[:, k, T:T + 128], lambda c: X[:, c * 128:(c + 1) * 128]))
            if samp:
                segs.append((lambda k: xT[:, k, T:T + NS], lambda c: X[:, c * 128:c * 128 + NS]))
            stream(w_in, KC, OFF_K + cg * 512, fm_body(segs), ['xT'], ['A0', 'A1', 'A2', 'A3', 'X'])
            for c in range(4):
                j = cg * 4 + c
                rope(A[c][:, 0:T], T, cosm, sinm, kT[:, j, 128:128 + T], None, 'A%d' % c, ['kT'], c % 2)
                if last:
                    dve(lambda E, j=j, c=c: E.tensor_tensor(out=kf32[:, j, :], in0=qf[c % 2][:, T - 128:T], in1=rtmp[:, T - 128:T], op=ALU.add),
                        ['qf%d' % (c % 2), 'rtmp'], ['kf32'])
                if tile == 0:
                    rope(X[:, c * 128:(c + 1) * 128], 128, cst[:, 0, T:T + 128], cst[:, 1, T:T + 128], kT[:, j, 0:128], None, 'X', ['kT'], c % 2)
                if samp:
                    rope(X[:, c * 128:c * 128 + NS], NS, cst[:, 0, T + 128:T + 129].broadcast_to([128, NS]), cst[:, 1, T + 128:T + 129].broadcast_to([128, NS]),
                         None, ksf[:, j, :], 'X', ['ksf'], c % 2)
            if last:
                tstore([kf32[:, cg * 4 + c, :] for c in range(4)], 128, kwp[:, cg * 512:(cg + 1) * 512], ['kf32'], 'x')
            if samp:
                tstore([ksf[:, cg * 4 + c, :] for c in range(4)], NS, kws[:, 127, cg * 512:(cg + 1) * 512], ['ksf'], 'x')
        if samp:
            S.readers.setdefault('tstdone', [])
        ui = [0]

        def unit_S(j, tb, hp):
            u = ui[0]
            ui[0] += 1
            Sb = S0 if u % 2 == 0 else S1
            sk = 'S0' if u % 2 == 0 else 'S1'
            p_ = pT[u % 2]
            pk = 'pT%d' % (u % 2)

            def sbody(E, Sb=Sb, j=j, tb=tb, hp=hp):
                rhs = qT[:, 2 * hp:2 * hp + 2, tb * 128:(tb + 1) * 128]
                E.matmul(Sb[:, 0:256].rearrange("p (h q) -> p h q", h=2), lhsT=kT[:, j, tb * 128:(tb + 1) * 128], rhs=rhs, start=True, stop=True)
                return E.matmul(Sb[:, 256:512].rearrange("p (h q) -> p h q", h=2), lhsT=kT[:, j, (tb + 1) * 128:(tb + 2) * 128], rhs=rhs, start=True, stop=True)
            S.op('pe', sbody, reads=['kT', 'qT'], writes=[sk])
            act(p_[:, :], Sb[:, :], AF.Exp, [sk], [pk], scale=scale)
            mv = 1 if (tile == 0 and tb == 0) else 0
            dve(lambda E, p_=p_, mv=mv: E.tensor_tensor(out=p_[:, :], in0=p_[:, :], in1=maskt[:, mv, :], op=ALU.mult), [pk, 'maskt'], [pk])
            return (j, tb, hp, p_, pk)

        def unit_PV(st_):
            j, tb, hp, p_, pk = st_

            def pvbody(E, p_=p_, j=j, tb=tb):
                E.matmul(OD[:, 0:256], lhsT=Vt[:, tb, j * 128:(j + 1) * 128], rhs=p_[:, 0:256], start=True, stop=False)
                E.matmul(OD[:, 0:256], lhsT=Vt[:, tb + 1, j * 128:(j + 1) * 128], rhs=p_[:, 256:512], start=False, stop=True)
                E.matmul(OD[:, 256:512], lhsT=ones[:, :], rhs=p_[:, 0:256], start=True, stop=False)
                return E.matmul(OD[:, 256:512], lhsT=ones[:, :], rhs=p_[:, 256:512], start=False, stop=True)
            S.op('pe', pvbody, reads=['Vt', pk, 'ones'], writes=['OD0', 'OD1'])
            for hh in range(2):
                hd = j * 4 + 2 * hp + hh
                dve(lambda E, hh=hh, hd=hd: E.tensor_scalar(out=rden[:, hh * 128:(hh + 1) * 128], in0=OD[:, 256 + hh * 128:256 + (hh + 1) * 128],
                                                            scalar1=esink[:, hd:hd + 1], scalar2=None, op0=ALU.add), ['OD1', 'esink'], ['rden'])
            dve(lambda E: E.reciprocal(out=rden[:, :], in_=rden[:, :]), ['rden'], ['rden'])
            dve(lambda E: E.tensor_tensor(out=otmp[:, :], in0=OD[:, 0:256], in1=rden[:, :], op=ALU.mult), ['OD0', 'rden'], ['otmp'])
            dve(lambda E, j=j, tb=tb, hp=hp: E.scalar_tensor_tensor(out=yoT[:, j * 4 + 2 * hp:j * 4 + 2 * hp + 2, tb * 128:(tb + 1) * 128],
                                                              in0=otmp[:, :].rearrange("p (h q) -> p h q", h=2), scalar=0.5,
                                                              in1=sgT[:, 2 * hp:2 * hp + 2, tb * 128:(tb + 1) * 128], op0=ALU.mult, op1=ALU.mult),
                ['otmp', 'sgT'], ['yoT'])

        for j in range(KV + 1):
            todo = [(j - 1, tb, hp) for tb in range(NB) for hp in range(2)] if j > 0 else []
            pend = []

            def ahook(kq, nkq, todo=todo, pend=pend):
                while pend:
                    unit_PV(pend.pop(0))
                n_ = (len(todo) + (nkq - kq) - 1) // (nkq - kq)
                for _ in range(min(n_, 2, len(todo))):
                    pend.append(unit_S(*todo.pop(0)))
            if j < KV:
                segs = [(lambda k: xT[:, k, 0:T], lambda c: A[c][:, 0:T])]
                if samp:
                    segs.append((lambda k: xT[:, k, T:T + NS], lambda c: X[:, c * 128:c * 128 + NS]))
                stream(w_in, KC, OFF_Q + j * 512, fm_body(segs), ['xT'], ['A0', 'A1', 'A2', 'A3', 'X'], hook=ahook)
            while pend or todo:
                while pend:
                    unit_PV(pend.pop(0))
                for _ in range(min(2, len(todo))):
                    pend.append(unit_S(*todo.pop(0)))
            if j == KV:
                break
            for c in range(4):
                rope(A[c][:, 0:T], T, cosm, sinm, qT[:, c, :], None, 'A%d' % c, ['qT'], c % 2)
            if samp:
                for c in range(4):
                    rope(X[:, c * 128:c * 128 + NS], NS, cst[:, 0, T + 128:T + 129].broadcast_to([128, NS]), cst[:, 1, T + 128:T + 129].broadcast_to([128, NS]),
                         qTs[:, j * 4 + c, :], None, 'X', ['qTs'], c % 2)
            stream(w_in, KC, OFF_GA + j * 512, fm_body(segs), ['xT'], ['A0', 'A1', 'A2', 'A3', 'X'])
            for c in range(4):
                act(sgT[:, c, :], A[c][:, 0:T], AF.Tanh, ['A%d' % c], ['sgT'], scale=0.5)
                dve(lambda E, c=c: E.scalar_tensor_tensor(out=sgT[:, c, :], in0=sgT[:, c, :], scalar=1.0, in1=A[c][:, 0:T], op0=ALU.add, op1=ALU.mult), ['sgT', 'A%d' % c], ['sgT'])
                if samp:
                    hh_ = j * 4 + c
                    act(sgs[:, hh_, :], X[:, c * 128:c * 128 + NS], AF.Tanh, ['X'], ['sgs'], scale=0.5)
                    dve(lambda E, hh_=hh_, c=c: E.scalar_tensor_tensor(out=sgs[:, hh_, :], in0=sgs[:, hh_, :], scalar=1.0, in1=X[:, c * 128:c * 128 + NS], op0=ALU.add, op1=ALU.mult), ['sgs', 'X'], ['sgs'])
        if not last:
            dve(lambda E: E.tensor_copy(out=kcar[:, :, :], in_=kT[:, :, T:T + 128]), ['kT'], ['kcar'])
            dve(lambda E: E.tensor_copy(out=vcar[:, :], in_=Vt[:, NB, :]), ['Vt'], ['vcar'])
        if samp:
            dma('sp', kws[:, 0:127, :], ck[:, 1:128, :], 'd_shk', writes=['kws_old'])
            dma('sp', vws[:, 0:127, :], cv[:, 1:128, :], 'd_shv', writes=['vws_old'])
            S.barrier()
            for b in range(NS):
                dma('sp', kwin[:, :], kws[b], 'd_kwin', reads=['kws_old'], writes=['kwin'])
                dma('pool', vwinb[:, :], vws[b], 'd_vwin', reads=['vws_old', 'vws_new'], writes=['vwinb'])
                for r_ in range(KV // 4):
                    def trb(E, r_=r_):
                        ins = None
                        for i in range(4):
                            j = 4 * r_ + i
                            ins = E.transpose(S0[:, i * 128:(i + 1) * 128], kwin[:, j * 128:(j + 1) * 128], ident[:, :])
                        return ins
                    S.op('pe', trb, reads=['kwin', 'ident'], writes=['S0'])
                    act(kwT4[:, :, :], S0[:, :].rearrange("p (i s) -> p i s", i=4), AF.Copy, ['S0'], ['kwT'])

                    def smm(E, r_=r_, b=b):
                        ins = None
                        for i in range(4):
                            j = 4 * r_ + i
                            ins = E.matmul(S1[:, j * 4:(j + 1) * 4], lhsT=kwT4[:, i, :], rhs=qTs[:, j * 4:(j + 1) * 4, b], start=True, stop=True)
                        return ins
                    S.op('pe', smm, reads=['kwT', 'qTs'], writes=['S1'])
                act(pTs[:, :], S1[:, 0:H], AF.Exp, ['S1'], ['pTs'], scale=scale)

                def pvs(E):
                    for j in range(KV):
                        E.matmul(OD[:, j * 4:(j + 1) * 4], lhsT=vwinb[:, j * 128:(j + 1) * 128], rhs=pTs[:, j * 4:(j + 1) * 4], start=True, stop=True)
                    return E.matmul(OD[:, 256:256 + H], lhsT=ones[:, :], rhs=pTs[:, :], start=True, stop=True)
                S.op('pe', pvs, reads=['vwinb', 'pTs', 'ones'], writes=['OD0', 'OD1'])
                dve(lambda E: E.tensor_tensor(out=sa1[:, :], in0=OD[:, 256:256 + H], in1=esink[:, :], op=ALU.add), ['OD1', 'esink'], ['sa1'])
                dve(lambda E: E.reciprocal(out=sa1[:, :], in_=sa1[:, :]), ['sa1'], ['sa1'])
                dve(lambda E: E.tensor_tensor(out=sa2[:, :], in0=OD[:, 0:H], in1=sa1[:, :], op=ALU.mult), ['OD0', 'sa1'], ['sa2'])
                dve(lambda E, b=b: E.scalar_tensor_tensor(out=yoT[:, 0:H, T + b], in0=sa2[:, :], scalar=0.5, in1=sgs[:, :, b], op0=ALU.mult, op1=ALU.mult), ['sa2', 'sgs'], ['yoT'])

    def final_stage(tile):
        samp = (tile == NT - 1)
        NCG = D // 512
        nbx = NB + (1 if samp else 0)
        dma('sp', lnG[:, :], lng.partition_broadcast(128), 'd_lnG', writes=['lnG'])
        dma('sp', lnB[:, :], lnb.partition_broadcast(128), 'd_lnB', writes=['lnB'])
        li = 0
        dve(lambda E: E.memset(s1[:, :, :], 0.0), [], ['s1'])
        dve(lambda E: E.memset(s2[:, :, :], 0.0), [], ['s2'])
        for cg in range(NCG):
            segs = [((lambda k, tb=tb: mgT[:, k, tb * 128:(tb + 1) * 128]), A[tb][:, :]) for tb in range(NB)]
            wr = ['A%d' % tb for tb in range(NB)]
            if samp:
                segs.append((lambda k: mgT[:, k, T:T + NS], X[0:NS, :]))
                wr.append('X')
            stream(w_o, KC, cg * 512, tm_body(segs, keys=[[w_] for w_ in wr]), ['mgT'], wr)
            for tb in range(nbx):
                xi = xr[li % NXR]
                xk = 'xr%d' % (li % NXR)
                li += 1
                if tb < NB:
                    n = 128
                    src = A[tb][:, :]
                    skey = 'A%d' % tb
                    dma('sp', xi[:, :], xp[128 + tile * T + tb * 128:128 + tile * T + (tb + 1) * 128, cg * 512:(cg + 1) * 512], 'd_' + xk, writes=[xk])
                else:
                    n = NS
                    src = X[0:NS, :]
                    skey = 'X'
                    dma('sp', xi[0:NS, :], xs[:, cg * 512:(cg + 1) * 512], 'd_' + xk, writes=[xk])
                dve(lambda E, xi=xi, n=n, src=src, tb=tb, cg=cg: E.scalar_tensor_tensor(out=res[0:n, tb, cg * 512:(cg + 1) * 512], in0=xi[0:n, :], scalar=cfg.ALPHA,
                                                                                  in1=src, op0=ALU.mult, op1=ALU.add, accum_out=s1[0:n, tb, cg:cg + 1]),
                    [xk, skey], ['res%d' % tb, 's1'])
                act(junk[0:n, :], res[0:n, tb, cg * 512:(cg + 1) * 512], AF.Square, ['res%d' % tb], ['junk', 's2'], accum_out=s2[0:n, tb, cg:cg + 1])
        dve(lambda E: E.tensor_reduce(out=st[:, 0, 0:nbx], in_=s1[:, 0:nbx, :], axis=AX.X, op=ALU.add), ['s1'], ['st'])
        dve(lambda E: E.tensor_reduce(out=st[:, 1, 0:nbx], in_=s2[:, 0:nbx, :], axis=AX.X, op=ALU.add), ['s2', 'st'], ['st'])
        dve(lambda E: E.tensor_scalar(out=st[:, 0, 0:nbx], in0=st[:, 0, 0:nbx], scalar1=1.0 / D, scalar2=None, op0=ALU.mult), ['st'], ['st'])
        dve(lambda E: E.tensor_scalar(out=st[:, 1, 0:nbx], in0=st[:, 1, 0:nbx], scalar1=1.0 / D, scalar2=None, op0=ALU.mult), ['st'], ['st'])
        dve(lambda E: E.tensor_tensor(out=st[:, 2, 0:nbx], in0=st[:, 0, 0:nbx], in1=st[:, 0, 0:nbx], op=ALU.mult), ['st'], ['st'])
        dve(lambda E: E.tensor_tensor(out=st[:, 1, 0:nbx], in0=st[:, 1, 0:nbx], in1=st[:, 2, 0:nbx], op=ALU.subtract), ['st'], ['st'])
        dve(lambda E: E.tensor_scalar(out=st[:, 1, 0:nbx], in0=st[:, 1, 0:nbx], scalar1=LN_EPS, scalar2=None, op0=ALU.add), ['st'], ['st'])
        act(st[:, 1, 0:nbx], st[:, 1, 0:nbx], AF.Ln, ['st'], ['st'])
        act(st[:, 1, 0:nbx], st[:, 1, 0:nbx], AF.Exp, ['st'], ['st'], scale=-0.5)
        dve(lambda E: E.tensor_tensor(out=st[:, 3, 0:nbx], in0=st[:, 0, 0:nbx], in1=st[:, 1, 0:nbx], op=ALU.mult), ['st'], ['st'])
        dve(lambda E: E.tensor_scalar(out=st[:, 3, 0:nbx], in0=st[:, 3, 0:nbx], scalar1=-1.0, scalar2=None, op0=ALU.mult), ['st'], ['st'])
        for tb in range(nbx):
            n = 128 if tb < NB else NS
            rk = 'res%d' % tb
            act(res[0:n, tb, :], res[0:n, tb, :], AF.Identity, [rk, 'st'], [rk], scale=st[0:n, 1, tb:tb + 1], bias=st[0:n, 3, tb:tb + 1])
            dve(lambda E, n=n, tb=tb: E.tensor_tensor(out=res[0:n, tb, :], in0=res[0:n, tb, :], in1=lnG[0:n, :], op=ALU.mult), [rk, 'lnG'], [rk])
            dve(lambda E, n=n, tb=tb: E.tensor_tensor(out=res[0:n, tb, :], in0=res[0:n, tb, :], in1=lnB[0:n, :], op=ALU.add), [rk, 'lnB'], [rk])
            if tb < NB:
                dma('sp', yp[tile * T + tb * 128:tile * T + (tb + 1) * 128, :], res[:, tb, :], 'd_' + rk, reads=[rk])
            else:
                dma('sp', ys[:, :], res[0:NS, tb, :], 'd_' + rk, reads=[rk])

    print("SBUF R0", R0, "R1", R1, "rnn_end", rnn_end, "att_end", att_end, "fin_end", fin_end)
    assert max(rnn_end, att_end, fin_end) <= LIMIT

    def load_tile(tile, with_cs):
        load_xT(128 + tile * T, NB, 0)
        if tile == 0:
            load_xT(0, 1, T)

    def load_cs(tile):
        if True:
            dma('sp', cst[:, :, 0:T], cs_d[:, :, 128 + tile * T:128 + (tile + 1) * T], 'd_cs', writes=['cs'])
            dma('sp', cst[:, :, T:T + 128], cs_d[:, :, 0:128], 'd_cs', writes=['cs'])
            S.op('sp', lambda E: E.dma_start(out=cst[:, :, T + 128:T + 129], in_=cs_d[:, :, 128 + NT * T:129 + NT * T], allow_slow_non_contiguous=True),
                 writes=['cs'], dma='d_cs')

    if USE_CC:
        dve(lambda E: E.memset(rsum[:, :, :], 0.0), [], ['rsum'])
        for tile in range(NT):
            load_tile(tile, False)
            S.barrier()
            rnn_stage(tile, True)
            S.barrier()
        dve(lambda E: E.tensor_reduce(out=pend[:, 0, :], in_=rsum[:, :, :], axis=AX.X, op=ALU.add), ['rsum'], ['pend'])
        dve(lambda E: E.tensor_tensor(out=pend[:, 0, :], in0=pend[:, 0, :], in1=nls[:, 0, :], op=ALU.mult), ['pend', 'nls'], ['pend'])
        act(pend[:, 0, :], pend[:, 0, :], AF.Exp, ['pend'], ['pend'])
        act(pend[:, 1, :], hcar[:, :], AF.Copy, ['hcar', 'pend'], ['pend'])
        dma('sp', cc_in, pend[:, :, :].rearrange("p a k -> p (a k)"), 'd_cc', reads=['pend'], writes=['cc_in'])
        S.barrier()
        S.op('pool', lambda E: E.collective_compute("AllGather", ALU.bypass, replica_groups=[list(range(NCR))], ins=[cc_in], outs=[cc_out]),
             reads=['cc_in'], writes=['cc_out'], dma='d_ccg', dma_inc=1)
        dma('sp', gath[:, :, :], cc_out.rearrange("(r p) f -> p r f", p=128), 'd_cc2', reads=['cc_out'], writes=['gath'])
        dve(lambda E: E.memset(hin[:, :], 0.0), [], ['hin'])
        for r in range(NCR):
            dve(lambda E, r=r: E.tensor_tensor(out=small[:, 0:KC], in0=gath[:, r, 0:KC], in1=hin[:, :], op=ALU.mult), ['gath', 'hin'], ['small'])
            dve(lambda E, r=r: E.tensor_tensor(out=small[:, 0:KC], in0=small[:, 0:KC], in1=gath[:, r, KC:2 * KC], op=ALU.add), ['gath', 'small'], ['small'])
            dve(lambda E: E.tensor_tensor(out=small[:, 0:KC], in0=small[:, 0:KC], in1=hin[:, :], op=ALU.subtract), ['hin', 'small'], ['small'])
            dve(lambda E, r=r: E.scalar_tensor_tensor(out=hin[:, :], in0=small[:, 0:KC], scalar=cmask[:, r:r + 1], in1=hin[:, :], op0=ALU.mult, op1=ALU.add),
                ['small', 'cmask', 'hin'], ['hin'])
    else:
        dve(lambda E: E.memset(hcar[:, :], 0.0), [], ['hcar'])
        load_xT(0, NB, 0, xp=xpre)
        for pt in range(NPRE):
            dma('sp', vm[:, :], vmask_d[:, pt * T:(pt + 1) * T], 'd_vm', writes=['vm'])
            if pt + 1 < NPRE:
                nxt = (lambda pt=pt: load_xT((pt + 1) * T, NB, 0, xp=xpre))
            else:
                nxt = (lambda: load_tile(0, True))
            rnn_stage(0, True, pre_first=(pt == 0), tail_fn=nxt)
        act(hin[:, :], hcar[:, :], AF.Copy, ['hcar'], ['hin'])
    dma('sp', convs[:, 0:2, :], sconv[:, 1:3, :], 'd_cvs')
    S.barrier()
    for tile in range(NT):
        if tile > 0 or USE_CC or NPRE == 0:
            load_tile(tile, True)
        if tile == NT - 1:
            S.barrier()
            load_sample_states()
        rnn_stage(tile, False)
        S.barrier()
        merge(tile, 1)
        S.barrier()
        load_cs(tile)
        attn_stage(tile, cst)
        S.barrier()
        merge(tile, 2)
        S.barrier()
        final_stage(tile)
        S.barrier()
    S.barrier()
    S.emit()
    return nc


_CACHE = {}


def _consts(cfg, core):
    T, NT = cfg.T, cfg.NT
    seqi, q = divmod(core, cfg.CPS)
    ident = np.eye(128, dtype=np.float32)
    rm = np.zeros((128, 128), np.float32)
    for d in range(64):
        rm[d + 64, d] = -1.0
        rm[d, d + 64] = 1.0
    half = 64
    inv = (ROPE_THETA ** (-np.arange(half, dtype=np.float32) / half)).astype(np.float32)
    start = q * cfg.CHUNK
    pos = np.concatenate([np.arange(start - 128, start + cfg.CHUNK), [cfg.PAST]]).astype(np.float32)
    ang = (pos[None, :] * np.concatenate([inv, inv])[:, None]).astype(np.float32)
    cs = np.stack([np.cos(ang), np.sin(ang)], axis=1).astype(np.float32)
    s_ = np.arange(128)[:, None]
    q_ = np.arange(128)[None, :]
    mprev = (s_ > q_).astype(np.float32)
    mcur = (s_ <= q_).astype(np.float32)
    mh = mprev if q > 0 else np.zeros_like(mprev)
    masks = np.stack([mprev, mcur, mh], axis=1).astype(np.float32)
    cm = np.zeros((128, cfg.NCORES), np.float32)
    for r in range(cfg.NCORES):
        if r // cfg.CPS == seqi and r < core:
            cm[:, r] = 1.0
    return dict(ident=ident, rmat=rm, cs=np.ascontiguousarray(cs), masks=np.ascontiguousarray(masks), cmask=cm)


def run(cfg, x_prompt, x_sample, state_conv, state_lru, cache_k_win, cache_v_win, w_in, conv_w, conv_b,
        w_gate_a, b_gate_a, w_gate_x, b_gate_x, lru_lambda, sinks, w_out_rnn, w_out_attn, w_o, ln_g, ln_b):
    key = (cfg.D, cfg.H, cfg.KV, cfg.T, cfg.NT, cfg.NS)
    if key not in _CACHE:
        _CACHE[key] = build(cfg)
    nc = _CACHE[key]
    f = lambda a: np.ascontiguousarray(np.asarray(a, dtype=np.float32))
    D, KC, NS = cfg.D, cfg.KC, cfg.NS
    fm = lambda v: f(v).reshape(KC, 128).T
    par = np.ascontiguousarray(np.stack([fm(conv_w[0][0]), fm(conv_w[0][1]), fm(conv_w[0][2]), fm(conv_w[0][3]), fm(conv_b[0]),
                                         fm(b_gate_a[0]), fm(b_gate_x[0]), fm(lru_lambda[0])], axis=1))
    shared = dict(w_in=f(w_in[0]), w_or=f(w_out_rnn[0]), w_oa=f(w_out_attn[0]), w_o=f(w_o[0]), wga=f(w_gate_a[0]), wgx=f(w_gate_x[0]),
                  par=par, sinks=f(sinks[0]), lng=f(ln_g[0]), lnb=f(ln_b[0]))
    xpr = f(x_prompt)
    in_maps = []
    for core in range(cfg.NCORES):
        seqi, q = divmod(core, cfg.CPS)
        start = q * cfg.CHUNK
        xpc = np.zeros((128 + cfg.CHUNK, D), np.float32)
        if q > 0:
            xpc[:] = xpr[seqi, start - 128:start + cfg.CHUNK]
        else:
            xpc[128:] = xpr[seqi, 0:cfg.CHUNK]
        sl = slice(core * NS, (core + 1) * NS)
        PRE = max((cfg.CPS - 1) * cfg.NT, 1) * cfg.T
        xpre = np.zeros((PRE, D), np.float32)
        vmask = np.zeros((128, PRE), np.float32)
        if start > 0:
            xpre[PRE - start:] = xpr[seqi, 0:start]
            vmask[:, PRE - start:] = 1.0
        m = dict(shared)
        m.update(xpre=xpre, vmask=vmask)
        m.update(_consts(cfg, core))
        m.update(xp=xpc, xs=f(x_sample[sl, 0]), sconv=f(state_conv[0, sl]), slru=f(state_lru[0, sl]),
                 ck=f(cache_k_win[0, sl]).reshape(NS, 128, cfg.DKV), cv=f(cache_v_win[0, sl]).reshape(NS, 128, cfg.DKV))
        in_maps.append(m)
    resu = run_bass_kernel_spmd(nc, in_maps, core_ids=list(range(cfg.NCORES)))
    R = resu.results
    B, S_, CPS = cfg.B, cfg.S, cfg.CPS
    y_p = np.zeros((B, S_, D), np.float32)
    for core in range(cfg.NCORES):
        seqi, q = divmod(core, CPS)
        y_p[seqi, q * cfg.CHUNK:(q + 1) * cfg.CHUNK] = R[core]["yp"]
    y_s = np.concatenate([R[c]["ys"] for c in range(cfg.NCORES)], axis=0)[:, None, :]
    lastc = [seqi * CPS + CPS - 1 for seqi in range(B)]
    conv_p = np.stack([R[c]["convp"] for c in lastc])[None]
    lru_p = np.stack([R[c]["lrup"][0] for c in lastc])[None]
    kw_p = np.stack([R[c]["kwp"].reshape(128, cfg.KV, 128) for c in lastc])[None]
    vw_p = np.stack([R[c]["vwp"].reshape(128, cfg.KV, 128) for c in lastc])[None]
    conv_s = np.concatenate([R[c]["convs"] for c in range(cfg.NCORES)], axis=0)[None]
    lru_s = np.concatenate([R[c]["lrus"] for c in range(cfg.NCORES)], axis=0)[None]
    kw_s = np.concatenate([R[c]["kws"] for c in range(cfg.NCORES)], axis=0).reshape(cfg.DEC, 128, cfg.KV, 128)[None]
    vw_s = np.concatenate([R[c]["vws"] for c in range(cfg.NCORES)], axis=0).reshape(cfg.DEC, 128, cfg.KV, 128)[None]
    return (y_p, y_s, conv_p, lru_p, kw_p, vw_p, conv_s, lru_s, kw_s, vw_s)


def kernel(**inputs):
    return run(Cfg(), **inputs)
```
